# Optimizing a Trainium2 kernel written in Bass

```python
import math
import jax, jax.numpy as jnp
from jax import lax
import numpy as np

D_MODEL = 2048
BATCH = 1
SEQ = 8192
DEPTH = 1

BLK = 128
WINDOW = 128
HA = 16
KV_A = 2
G_A = HA // KV_A
DH_A = 64
HB = 8
DH_B = 64
N_BUCKETS = 32
MAX_DISTANCE = 128
H_BIAS = HA + HB
MEM_LEN = 256
HC = 4
DH_C = 128
D_FF = ((8 * D_MODEL // 3 + 255) // 256) * 256
QA_W = HA * DH_A
KVA_W = KV_A * DH_A
QB_W = HB * 2 * DH_B
VB_W = HB * 2 * DH_B
SPLITS = [QA_W, KVA_W, KVA_W, QB_W, QB_W, VB_W, D_MODEL, D_MODEL]
IN_WIDTH = sum(SPLITS)
SPLIT_IDX = [int(v) for v in np.cumsum(SPLITS)[:-1]]
LN_EPS = 1e-5

kernel_name = "hybrid_swa_sink_diffattn_deepnorm_layer"


def rel_bucket(dist):
    n = jnp.maximum(dist, 0)
    exact = N_BUCKETS // 2
    logv = jnp.log(jnp.maximum(n, 1).astype(jnp.float32) / exact) / math.log(MAX_DISTANCE / exact)
    large = exact + (logv * (N_BUCKETS - exact)).astype(jnp.int32)
    large = jnp.minimum(large, N_BUCKETS - 1)
    return jnp.where(n < exact, n, large)


def layer_norm(x, g, b):
    xf = x.astype(jnp.float32)
    mu = xf.mean(-1, keepdims=True)
    var = jnp.square(xf - mu).mean(-1, keepdims=True)
    y = (xf - mu) * lax.rsqrt(var + LN_EPS) * g.astype(jnp.float32) + b.astype(jnp.float32)
    return y.astype(x.dtype)


def swa_sink_attention(q, k, v, sinks, table):
    b, s = q.shape[0], q.shape[1]
    nb = s // BLK
    qb = q.reshape(b, nb, BLK, KV_A, G_A, DH_A)
    kb = k.reshape(b, nb, BLK, KV_A, DH_A)
    vb = v.reshape(b, nb, BLK, KV_A, DH_A)
    pad = ((0, 0), (1, 0), (0, 0), (0, 0), (0, 0))
    kw = jnp.concatenate([jnp.pad(kb, pad)[:, :-1], kb], axis=2)
    vw = jnp.concatenate([jnp.pad(vb, pad)[:, :-1], vb], axis=2)
    sc = jnp.einsum("bnqgrd,bnkgd->bngrqk", qb, kw).astype(jnp.float32) * (DH_A ** -0.5)
    i = jnp.arange(BLK)[:, None]
    j = jnp.arange(2 * BLK)[None, :]
    dist = BLK + i - j
    bias = table[:, :HA].T.astype(jnp.float32)[:, rel_bucket(dist)]
    sc = sc + bias.reshape(KV_A, G_A, BLK, 2 * BLK)
    band = (dist >= 0) & (dist < WINDOW)
    blk_ok = (jnp.arange(nb)[:, None] > 0) | (jnp.arange(2 * BLK)[None, :] >= BLK)
    mask = band[None, :, :] & blk_ok[:, None, :]
    sc = jnp.where(mask[None, :, None, None], sc, -jnp.inf)
    sink = sinks.astype(jnp.float32).reshape(KV_A, G_A)[None, None, :, :, None, None]
    m = jnp.maximum(sc.max(-1, keepdims=True), sink)
    p = jnp.exp(sc - m)
    p = p / (p.sum(-1, keepdims=True) + jnp.exp(sink - m))
    o = jnp.einsum("bngrqk,bnkgd->bnqgrd", p.astype(v.dtype), vw)
    return o.reshape(b, s, HA * DH_A)


def diff_attention(q, k, v, lam, lambda_init, subln_w, table):
    b, s = q.shape[0], q.shape[1]
    nb = s // BLK
    qblocks = q.reshape(b, nb, BLK, HB, 2, DH_B).transpose(1, 0, 2, 3, 4, 5)
    kpos = jnp.arange(s)
    tb = table[:, HA:].T.astype(jnp.float32)

    def one_block(args):
        n, qb = args
        sc = jnp.einsum("bqhcd,bkhcd->bchqk", qb, k).astype(jnp.float32) * (DH_B ** -0.5)
        qpos = n * BLK + jnp.arange(BLK)
        dist = qpos[:, None] - kpos[None, :]
        sc = jnp.where(dist >= 0, sc + tb[:, rel_bucket(dist)][None, None], -jnp.inf)
        p = jax.nn.softmax(sc, axis=-1)
        a = p[:, 0] - lam * p[:, 1]
        return jnp.einsum("bhqk,bkhe->bqhe", a.astype(v.dtype), v)

    o = lax.map(one_block, (jnp.arange(nb), qblocks))
    o = o.transpose(1, 0, 2, 3, 4).reshape(b, s, HB, 2 * DH_B).astype(jnp.float32)
    o = o * lax.rsqrt(jnp.square(o).mean(-1, keepdims=True) + LN_EPS) * subln_w.astype(jnp.float32)
    o = o * (1.0 - lambda_init)
    return o.reshape(b, s, HB * 2 * DH_B).astype(v.dtype)


def memory_cross_attention(h, mem, w_cq, w_mem_kv, w_co):
    b, s = h.shape[0], h.shape[1]
    q = (h @ w_cq).reshape(b, s, HC, DH_C)
    kv = (mem @ w_mem_kv).reshape(b, mem.shape[1], 2, HC, DH_C)
    sc = jnp.einsum("bqhd,bkhd->bhqk", q, kv[:, :, 0]).astype(jnp.float32) * (DH_C ** -0.5)
    p = jax.nn.softmax(sc, axis=-1)
    o = jnp.einsum("bhqk,bkhd->bqhd", p.astype(h.dtype), kv[:, :, 1])
    return o.reshape(b, s, HC * DH_C) @ w_co


def swiglu(h, w_gate_up, w_down):
    gu = h @ w_gate_up
    g, u = gu[..., :D_FF], gu[..., D_FF:]
    return (jax.nn.silu(g) * u) @ w_down


def setup_inputs(seed: int = 0) -> dict:
    key = jax.random.key(seed)
    ks = jax.random.split(key, 28)
    beta = (8 * DEPTH) ** -0.25

    def nrm(k, shape, scale):
        return jax.random.normal(k, shape, jnp.float32) * scale

    L = DEPTH
    return {
        "x": nrm(ks[0], (BATCH, SEQ, D_MODEL), 1.0),
        "mem": nrm(ks[1], (BATCH, MEM_LEN, D_MODEL), 1.0),
        "rel_bias_table": nrm(ks[2], (N_BUCKETS, H_BIAS), 0.5),
        "w_in": nrm(ks[3], (L, D_MODEL, IN_WIDTH), D_MODEL ** -0.5),
        "sinks": nrm(ks[4], (L, HA), 0.5),
        "lambda_q1": nrm(ks[5], (L, DH_B), 0.1),
        "lambda_k1": nrm(ks[6], (L, DH_B), 0.1),
        "lambda_q2": nrm(ks[7], (L, DH_B), 0.1),
        "lambda_k2": nrm(ks[8], (L, DH_B), 0.1),
        "subln_w": 1.0 + nrm(ks[9], (L, 2 * DH_B), 0.02),
        "w_branch_a": nrm(ks[10], (L, QA_W, D_MODEL), QA_W ** -0.5),
        "w_branch_b": nrm(ks[11], (L, VB_W, D_MODEL), VB_W ** -0.5),
        "w_o": nrm(ks[12], (L, D_MODEL, D_MODEL), beta * D_MODEL ** -0.5),
        "ln1_g": 1.0 + nrm(ks[13], (L, D_MODEL), 0.02),
        "ln1_b": nrm(ks[14], (L, D_MODEL), 0.02),
        "w_cq": nrm(ks[15], (L, D_MODEL, HC * DH_C), D_MODEL ** -0.5),
        "w_mem_kv": nrm(ks[16], (L, D_MODEL, 2 * HC * DH_C), D_MODEL ** -0.5),
        "w_co": nrm(ks[17], (L, HC * DH_C, D_MODEL), beta * (HC * DH_C) ** -0.5),
        "ln2_g": 1.0 + nrm(ks[18], (L, D_MODEL), 0.02),
        "ln2_b": nrm(ks[19], (L, D_MODEL), 0.02),
        "w_gate_up": nrm(ks[20], (L, D_MODEL, 2 * D_FF), D_MODEL ** -0.5),
        "w_down": nrm(ks[21], (L, D_FF, D_MODEL), beta * D_FF ** -0.5),
        "ln3_g": 1.0 + nrm(ks[22], (L, D_MODEL), 0.02),
        "ln3_b": nrm(ks[23], (L, D_MODEL), 0.02),
    }


def reference(x, mem, rel_bias_table, w_in, sinks, lambda_q1, lambda_k1, lambda_q2, lambda_k2,
              subln_w, w_branch_a, w_branch_b, w_o, ln1_g, ln1_b, w_cq, w_mem_kv, w_co,
              ln2_g, ln2_b, w_gate_up, w_down, ln3_g, ln3_b):
    alpha = (2 * DEPTH) ** 0.25
    b, s = x.shape[0], x.shape[1]
    h = x
    for l in range(DEPTH):
        lambda_init = 0.8 - 0.6 * math.exp(-0.3 * l)
        proj = h @ w_in[l]
        qa, ka, va, qb, kb, vb, ga, gb = jnp.split(proj, SPLIT_IDX, axis=-1)
        o_a = swa_sink_attention(qa.reshape(b, s, HA, DH_A), ka.reshape(b, s, KV_A, DH_A),
                                 va.reshape(b, s, KV_A, DH_A), sinks[l], rel_bias_table)
        f32 = jnp.float32
        lam = (jnp.exp(jnp.sum(lambda_q1[l].astype(f32) * lambda_k1[l].astype(f32)))
               - jnp.exp(jnp.sum(lambda_q2[l].astype(f32) * lambda_k2[l].astype(f32)))
               + lambda_init)
        o_b = diff_attention(qb.reshape(b, s, HB, 2, DH_B), kb.reshape(b, s, HB, 2, DH_B),
                             vb.reshape(b, s, HB, 2 * DH_B), lam, lambda_init, subln_w[l],
                             rel_bias_table)
        mix = jax.nn.sigmoid(ga) * (o_a @ w_branch_a[l]) + jax.nn.sigmoid(gb) * (o_b @ w_branch_b[l])
        h = layer_norm(alpha * h + mix @ w_o[l], ln1_g[l], ln1_b[l])
        c = memory_cross_attention(h, mem, w_cq[l], w_mem_kv[l], w_co[l])
        h = layer_norm(alpha * h + c, ln2_g[l], ln2_b[l])
        f = swiglu(h, w_gate_up[l], w_down[l])
        h = layer_norm(alpha * h + f, ln3_g[l], ln3_b[l])
    return h
```

```python
import math
import contextlib
import numpy as np
import concourse.bass as bass
import concourse.mybir as mybir
from concourse.bass_utils import run_bass_kernel_spmd

F32 = mybir.dt.float32
BF16 = mybir.dt.bfloat16
AF = mybir.ActivationFunctionType
ALU = mybir.AluOpType
AX = mybir.AxisListType

NCORES = 8
S = 8192
D = 2048
TOK = S // NCORES
NB = TOK // 128
DFF = 5632
NEG = -30000.0
ALPHA = 2.0 ** 0.25
LAMBDA_INIT = 0.8 - 0.6 * math.exp(0.0)
EPS = 1e-5
C_QA, C_KA, C_VA, C_QB, C_KB, C_VB, C_GA, C_GB = 0, 1024, 1152, 1280, 2304, 3328, 4352, 6400
NOWN = 1280
NTOKX = S + NOWN
OWN0 = (128, 768)
NFAR = (27, 59)


def LB(j):
    return j + 1 if j < 4 else j + 2


class Region:
    __slots__ = ("name", "last_w", "readers")

    def __init__(self, name):
        self.name = name
        self.last_w = None
        self.readers = []


class Op:
    __slots__ = ("idx", "eng", "fn", "deps", "dma", "token", "signal", "prewait")

    def __init__(self, idx, eng, fn, dma):
        self.idx = idx
        self.eng = eng
        self.fn = fn
        self.dma = dma
        self.deps = set()
        self.token = None
        self.signal = False
        self.prewait = None


class Prog:
    ENGS = ("pe", "act", "dve", "pool", "sp")
    NDMA_SEM = 8

    def __init__(self, nc, same_engine_sync=True):
        self.nc = nc
        self.ops = []
        self.same_engine_sync = same_engine_sync
        self.finals = []
        self.barrier_idx = None
        self.last_on = {}
        self.dma_since = []

    def R(self, name=""):
        return Region(name)

    def Rs(self, n, name=""):
        return [Region(name) for _ in range(n)]

    def op(self, eng, fn, reads=(), writes=(), dma=False):
        o = Op(len(self.ops), eng, fn, dma)
        for r in reads:
            if r.last_w is not None:
                o.deps.add(r.last_w)
        for w in writes:
            if w.last_w is not None:
                o.deps.add(w.last_w)
            for rd in w.readers:
                o.deps.add(rd)
        for r in reads:
            r.readers.append(o.idx)
        for w in writes:
            w.last_w = o.idx
            w.readers = []
        if self.barrier_idx is not None:
            o.deps.add(self.barrier_idx)
        o.deps.discard(o.idx)
        self.ops.append(o)
        if dma:
            self.dma_since.append(o.idx)
        else:
            self.last_on[eng] = o.idx
        return o

    def barrier(self, scratch):
        o = Op(len(self.ops), "dve", lambda e: e.memset(scratch, 0.0), False)
        for e, i in self.last_on.items():
            o.deps.add(i)
        for i in self.dma_since:
            o.deps.add(i)
        self.dma_since = []
        if self.barrier_idx is not None:
            o.deps.add(self.barrier_idx)
        self.ops.append(o)
        self.last_on["dve"] = o.idx
        self.barrier_idx = o.idx
        return o

    def final_wait(self, region):
        self.finals.append(region.last_w)

    def _needs_sync(self, dep, ename):
        if dep.eng == ename and not dep.dma:
            if ename == "pe" or not self.same_engine_sync:
                return False
        return True

    def emit(self):
        nc = self.nc
        ops = self.ops
        for o in ops:
            for d in o.deps:
                dep = ops[d]
                if self._needs_sync(dep, o.eng):
                    dep.signal = True
        for f in self.finals:
            ops[f].signal = True
        for o in ops:
            if o.dma:
                o.signal = True
        with contextlib.ExitStack() as st:
            esem = {e: st.enter_context(nc.semaphore(f"s_{e}")) for e in ("pe", "act", "dve", "pool")}
            dsem = {q: [st.enter_context(nc.semaphore(f"d_{q}{k}")) for k in range(self.NDMA_SEM)]
                    for q in ("sp", "act", "pool")}
            cnt = {e: 0 for e in esem}
            dcnt = {q: 0 for q in dsem}
            for o in ops:
                if o.dma:
                    i = dcnt[o.eng]
                    dcnt[o.eng] += 1
                    k = i % self.NDMA_SEM
                    m = i // self.NDMA_SEM
                    o.token = (dsem[o.eng][k], 16 * (m + 1))
                    if m > 0:
                        o.prewait = (dsem[o.eng][k], 16 * m)
                elif o.signal:
                    cnt[o.eng] += 1
                    o.token = (esem[o.eng], cnt[o.eng])
            self.sem_counts = dict(cnt)
            block = st.enter_context(nc.Block())
            engobj = {"pe": "tensor", "act": "scalar", "dve": "vector", "pool": "gpsimd", "sp": "sync"}

            def run(ename, eng):
                seen = {}

                def wait(tok):
                    sem, v = tok
                    if seen.get(sem.num, 0) >= v:
                        return
                    eng.wait_ge(sem, v)
                    seen[sem.num] = v

                for o in ops:
                    if o.eng != ename:
                        continue
                    for d in sorted(o.deps, reverse=True):
                        dep = ops[d]
                        if not self._needs_sync(dep, ename):
                            continue
                        wait(dep.token)
                    if o.prewait is not None:
                        wait(o.prewait)
                    ins = o.fn(eng)
                    if o.token is not None:
                        sem, v = o.token
                        ins.then_inc(sem, 16 if o.dma else 1)
                if ename == "sp":
                    for f in self.finals:
                        wait(ops[f].token)

            for ename in self.ENGS:
                getattr(block, engobj[ename])(lambda eng, _e=ename: run(_e, eng))


class Arena:
    def __init__(self, t, nbytes):
        self.t = t
        self.cap = nbytes
        self.off = 0

    def alloc(self, shape, dtype):
        esz = 2 if dtype == BF16 else 4
        n = int(np.prod(shape[1:])) * esz
        off = (self.off + 63) // 64 * 64
        assert off + n <= self.cap, f"SBUF arena overflow: {off + n} > {self.cap}"
        self.off = off + n
        ap = self.t[:, off // 2:(off + n) // 2]
        if dtype != BF16:
            ap = ap.bitcast(dtype)
        if len(shape) == 3:
            ap = ap.rearrange("p (a b) -> p a b", a=shape[1])
        elif len(shape) == 4:
            ap = ap.rearrange("p (a b c) -> p a b c", a=shape[1], b=shape[2])
        elif len(shape) == 5:
            ap = ap.rearrange("p (a b c d) -> p a b c d", a=shape[1], b=shape[2], c=shape[3])
        return ap

    def mark(self):
        return self.off

    def reset(self, m):
        self.off = m


def build(stage=99, dbg=None):
    nc = bass.Bass("TRN2", target_bir_lowering=False)

    def din(name, shape, dt=F32):
        return nc.dram_tensor(name, shape, dt, kind="ExternalInput").ap()

    x_all = din("x_all", [S, D])
    x_own = din("x_own", [NOWN, D])
    mem = din("mem", [256, D])
    w_in = din("w_in", [D, 8448])
    w_a = din("w_branch_a", [1024, D])
    w_b = din("w_branch_b", [1024, D])
    w_o = din("w_o", [D, D])
    w_cq = din("w_cq", [D, 512])
    w_mkv = din("w_mem_kv", [D, 1024])
    w_co = din("w_co", [512, D])
    w_gu = din("w_gate_up", [D, 2 * DFF])
    w_dn = din("w_down", [DFF, D])
    lnp = din("lnp", [128, 6, D])
    smallp = din("smallp", [128, 16 + 256 + 128])
    ident_d = din("ident", [128, 128])
    bias_far = din("bias_far", [128, 2 * 64 * 8])
    bias_near = din("bias_near", [2, 8, 128, 5 * 512])
    bias_swa = din("bias_swa", [2, 128, 16 * 2 * 128])
    out = nc.dram_tensor("out", [TOK, D], F32, kind="ExternalOutput").ap()
    kt_all = nc.dram_tensor("kt_all", [8, 128, NTOKX], BF16).ap()
    v_all = nc.dram_tensor("v_all", [74, 128, 1024], BF16).ap()
    dbg_out = None
    if dbg is not None:
        dbg_out = nc.dram_tensor("dbg", [TOK, dbg], F32, kind="ExternalOutput").ap()

    P = Prog(nc)
    with contextlib.ExitStack() as st:
        ARENA_BYTES = 207 * 1024
        arena_t = st.enter_context(nc.sbuf_tensor("arena", [128, ARENA_BYTES // 2], BF16))
        A = Arena(arena_t, ARENA_BYTES)
        pp = [st.enter_context(nc.psum_tensor(f"pp{i}", [128, 1024], F32)) for i in range(4)]

        def bank(b):
            return pp[b // 2][:, (b % 2) * 512:(b % 2 + 1) * 512]

        rbank = P.Rs(8, "bank")
        cpctr = [0]

        def cp(out_ap, in_ap, reads, writes, eng=None):
            if eng is None:
                eng = ("dve", "act")[cpctr[0] % 2]
                cpctr[0] += 1
            if eng == "act":
                return P.op("act", lambda e: e.activation(out_ap, in_ap, AF.Copy), reads=reads, writes=writes)
            if eng == "pool":
                return P.op("pool", lambda e: e.tensor_copy(out_ap, in_ap), reads=reads, writes=writes)
            return P.op("dve", lambda e: e.tensor_copy(out_ap, in_ap), reads=reads, writes=writes)

        def mm(out_ap, lhsT, rhs, start, stop, reads, writes):
            return P.op("pe", lambda e: e.matmul(out_ap, lhsT, rhs, start=start, stop=stop),
                        reads=reads, writes=writes)

        def dma(q, out_ap, in_ap, reads, writes):
            return P.op(q, lambda e: e.dma_start(out=out_ap, in_=in_ap), reads=reads, writes=writes, dma=True)

        idb = A.alloc([128, 128], BF16)
        idf = A.alloc([128, 128], F32)
        small = A.alloc([128, 400], F32)
        scratch = A.alloc([128, 16], F32)
        esink = A.alloc([128, 16], F32)
        neglam = A.alloc([128, 1], F32)
        wsub = A.alloc([128, 128], F32)
        lam_t = A.alloc([128, 8], F32)
        epsc = A.alloc([128, 1], F32)
        r_idb, r_idf, r_small, r_const = P.R(), P.R(), P.R(), P.R()
        dma("pool", idb, ident_d, [], [r_idb])
        dma("sp", idf, ident_d, [], [r_idf])
        dma("sp", small, smallp, [], [r_small])
        P.op("dve", lambda e: e.memset(epsc, EPS), writes=[r_const])
        P.op("act", lambda e: e.activation(esink, small[:, 0:16], AF.Exp), reads=[r_small], writes=[r_const])
        lq = small[:, 16:272].rearrange("p (a b) -> p a b", a=4)
        prod_t = A.alloc([128, 2, 64], F32)
        P.op("dve", lambda e: e.tensor_tensor(prod_t[:, 0, :], lq[:, 0, :], lq[:, 1, :], ALU.mult), reads=[r_small], writes=[r_const])
        P.op("dve", lambda e: e.tensor_tensor(prod_t[:, 1, :], lq[:, 2, :], lq[:, 3, :], ALU.mult), reads=[r_small], writes=[r_const])
        P.op("dve", lambda e: e.reduce_sum(lam_t[:, 0:2], prod_t, axis=AX.X), reads=[r_const], writes=[r_const])
        P.op("act", lambda e: e.activation(lam_t[:, 2:4], lam_t[:, 0:2], AF.Exp), reads=[r_const], writes=[r_const])
        P.op("dve", lambda e: e.tensor_tensor(lam_t[:, 4:5], lam_t[:, 3:4], lam_t[:, 2:3], ALU.subtract), reads=[r_const], writes=[r_const])
        P.op("dve", lambda e: e.tensor_scalar_add(neglam, lam_t[:, 4:5], -LAMBDA_INIT), reads=[r_const], writes=[r_const])
        P.op("dve", lambda e: e.tensor_scalar_mul(wsub, small[:, 272:400], (1.0 - LAMBDA_INIT)), reads=[r_small], writes=[r_const])
        assert A.mark() <= 4 * 1024
        M0 = 4 * 1024
        A.reset(M0)

        xT = A.alloc([128, 16, NOWN], BF16)
        r_xT = [[P.R() for _ in range(2)] for _ in range(10)]
        after_xT = A.mark()
        QbT = A.alloc([128, 8, TOK], BF16)
        oaT = A.alloc([128, 8, TOK], BF16)
        obT = A.alloc([128, 8, TOK], BF16)
        xT_mark = A.mark()
        assert xT_mark <= 93 * 1024
        A.reset(after_xT)

        Wkb = A.alloc([128, 16, 1024], BF16)
        Wvb = A.alloc([128, 16, 1024], BF16)
        r_Wkb, r_Wvb = P.Rs(2), P.Rs(2)
        for hlf in range(2):
            dma("pool", Wkb[:, :, hlf * 512:(hlf + 1) * 512],
                w_in[:, C_KB + hlf * 512:C_KB + (hlf + 1) * 512].rearrange("(c p) n -> p c n", p=128), [], [r_Wkb[hlf]])
        for hlf in range(2):
            dma("pool", Wvb[:, :, hlf * 512:(hlf + 1) * 512],
                w_in[:, C_VB + hlf * 512:C_VB + (hlf + 1) * 512].rearrange("(c p) n -> p c n", p=128), [], [r_Wvb[hlf]])
        xs = [A.alloc([128, 4, D], BF16) for _ in range(2)]
        xTs = [A.alloc([128, 16, 512], BF16) for _ in range(2)]
        KTs = [A.alloc([128, 8, 512], BF16) for _ in range(2)]
        Vs = [A.alloc([128, 4, 1024], BF16) for _ in range(2)]
        r_xs = P.Rs(2)
        r_xTs = [[[P.R() for _ in range(2)] for _ in range(4)] for _ in range(2)]
        r_KTs = [P.Rs(8) for _ in range(2)]
        r_Vs = [[[P.R() for _ in range(2)] for _ in range(4)] for _ in range(2)]
        r_ktall, r_vall = P.R(), P.R()
        tb = [0]
        mb = [0]

        def next_tbank():
            b = 6 + tb[0] % 2
            tb[0] += 1
            return b

        def next_mbank(n=6):
            b = mb[0] % n
            mb[0] += 1
            return b

        def transpose_block_bf16(src_tok_major, dst_fn, rsrc, rdst_fn):
            for hlf in range(2):
                b = next_tbank()
                pb = bank(b).bitcast(BF16)
                for c in range(8):
                    cc = hlf * 8 + c
                    P.op("pe", lambda e, pb=pb, c=c, cc=cc: e.transpose(pb[:, c * 128:(c + 1) * 128],
                                                                       src_tok_major[:, cc * 128:(cc + 1) * 128], idb),
                         reads=[rsrc, r_idb], writes=[rbank[b]])
                cp(dst_fn(hlf), pb.rearrange("p (c n) -> p c n", c=8), [rbank[b]], [rdst_fn(hlf)])

        NG = 19
        for g in range(16 if (dbg is not None and stage < 4) else 0, NG if stage >= 1 else 0):
            par = g % 2
            nb = 4 if g < 18 else 2
            ntok = nb * 128
            if g < 16:
                src = x_all[g * 512:(g + 1) * 512, :]
            else:
                src = x_own[(g - 16) * 512:(g - 16) * 512 + ntok, :]
            dma("pool", xs[par][:, 0:nb, :], src.rearrange("(b p) d -> p b d", p=128), [], [r_xs[par]])
            for b in range(nb):
                if g < 16:
                    dst_fn = lambda hlf, b=b: xTs[par][:, hlf * 8:(hlf + 1) * 8, b * 128:(b + 1) * 128]
                    rdst_fn = lambda hlf, b=b: r_xTs[par][b][hlf]
                else:
                    lb = (g - 16) * 4 + b
                    dst_fn = lambda hlf, lb=lb: xT[:, hlf * 8:(hlf + 1) * 8, lb * 128:(lb + 1) * 128]
                    rdst_fn = lambda hlf, lb=lb: r_xT[lb][hlf]
                transpose_block_bf16(xs[par][:, b, :], dst_fn, r_xs[par], rdst_fn)
            if g < 16:
                xsrc = xTs[par]
                t0 = 0
                rx = lambda b, hlf: r_xTs[par][b][hlf]
            else:
                xsrc = xT
                t0 = (g - 16) * 512
                rx = lambda b, hlf, g=g: r_xT[(g - 16) * 4 + b][hlf]
            for h in range(8):
                bk = next_mbank()
                for c in range(16):
                    mm(bank(bk)[:, 0:ntok], Wkb[:, c, h * 128:(h + 1) * 128], xsrc[:, c, t0:t0 + ntok], c == 0, c == 15,
                       [r_Wkb[h // 4]] + [rx(b, c // 8) for b in range(nb)], [rbank[bk]])
                cp(KTs[par][:, h, 0:ntok], bank(bk)[:, 0:ntok], [rbank[bk]], [r_KTs[par][h]])
            dma("sp", kt_all[:, :, g * 512:g * 512 + ntok].rearrange("h p n -> p h n"), KTs[par][:, :, 0:ntok],
                r_KTs[par], [r_ktall])
            for b in range(nb):
                for hlf in range(2):
                    bk = next_mbank()
                    for c in range(16):
                        mm(bank(bk), xsrc[:, c, t0 + b * 128:t0 + (b + 1) * 128], Wvb[:, c, hlf * 512:(hlf + 1) * 512], c == 0, c == 15,
                           [r_Wvb[hlf], rx(b, c // 8)], [rbank[bk]])
                    cp(Vs[par][:, b, hlf * 512:(hlf + 1) * 512], bank(bk), [rbank[bk]], [r_Vs[par][b][hlf]])
            dma("sp", v_all[g * 4:g * 4 + nb, :, :].rearrange("b p n -> p b n"), Vs[par][:, 0:nb, :],
                [r for bb in range(nb) for r in r_Vs[par][bb]], [r_vall])
        P.barrier(scratch)
        A.reset(xT_mark)

        QaT = A.alloc([128, 8, TOK], BF16)
        KaT = A.alloc([128, 2, NOWN], BF16)
        Va = A.alloc([128, 10, 2, 65], BF16)
        r_QaT = [P.Rs(2) for _ in range(8)]
        r_QbT = [P.Rs(2) for _ in range(8)]
        r_KaT = [P.Rs(3) for _ in range(2)]
        r_Va = P.Rs(10)
        r_Vaones = P.R()
        r_oaT = P.Rs(8)
        r_obT = [P.Rs(8) for _ in range(8)]
        proj_mark = A.mark()
        allx = [r_xT[b][hh] for b in range(10) for hh in range(2)]

        if stage >= 2:
            Wt = [A.alloc([128, 16, 512], BF16) for _ in range(2)]
            r_Wt = P.Rs(2)
            wi = [0]

            def load_w(col0, ncols=512):
                k = wi[0] % 2
                wi[0] += 1
                dma("pool", Wt[k][:, :, 0:ncols], w_in[:, col0:col0 + ncols].rearrange("(c p) n -> p c n", p=128), [], [r_Wt[k]])
                return Wt[k], r_Wt[k]

            for (col0, dst, rdst) in ((C_QA, QaT, r_QaT), (C_QB, QbT, r_QbT)):
                for pn in range(2):
                    W, rW = load_w(col0 + pn * 512)
                    for ii in range(4):
                        i = pn * 4 + ii
                        for hlf in range(2):
                            bk = next_mbank()
                            for c in range(16):
                                mm(bank(bk), W[:, c, ii * 128:(ii + 1) * 128], xT[:, c, OWN0[hlf]:OWN0[hlf] + 512],
                                   c == 0, c == 15, [rW] + allx, [rbank[bk]])
                            cp(dst[:, i, hlf * 512:(hlf + 1) * 512], bank(bk), [rbank[bk]], [rdst[i][hlf]])
            k = wi[0] % 2
            wi[0] += 1
            Wka = Wt[k][:, :, 0:256].rearrange("p c (g u d) -> p c g u d", g=2, u=2)
            for u in range(2):
                for g in range(2):
                    dma("pool", Wka[:, :, g, u, :], w_in[:, C_KA + g * 64:C_KA + (g + 1) * 64].rearrange("(c p) d -> p c d", p=128), [], [r_Wt[k]])
            Wva = Wt[k][:, :, 256:384]
            dma("pool", Wva, w_in[:, C_VA:C_VA + 128].rearrange("(c p) n -> p c n", p=128), [], [r_Wt[k]])
            for g in range(2):
                for pi, (t0, nt) in enumerate(((0, 512), (512, 512), (1024, 256))):
                    bk = next_mbank()
                    for c in range(16):
                        mm(bank(bk)[:, 0:nt], Wt[k][:, c, g * 128:(g + 1) * 128], xT[:, c, t0:t0 + nt], c == 0, c == 15,
                           [r_Wt[k]] + allx, [rbank[bk]])
                    cp(KaT[:, g, t0:t0 + nt], bank(bk)[:, 0:nt], [rbank[bk]], [r_KaT[g][pi]])
            P.op("pool", lambda e: e.memset(Va[:, :, :, 64:65], 1.0), writes=[r_Vaones])
            for b in range(10):
                bk = next_mbank()
                for c in range(16):
                    mm(bank(bk)[:, 0:128], xT[:, c, b * 128:(b + 1) * 128], Wva[:, c, :], c == 0, c == 15,
                       [r_Wt[k], r_xT[b][c // 8]], [rbank[bk]])
                cp(Va[:, b, :, 0:64], bank(bk)[:, 0:128].rearrange("p (g d) -> p g d", g=2), [rbank[bk]], [r_Va[b]])
        P.barrier(scratch)
        A.reset(proj_mark)

        if stage >= 3:
            Bsw = [A.alloc([128, 4, 4, 256], F32) for _ in range(2)]
            r_Bsw = P.Rs(2)
            tmpS = [A.alloc([128, 4, 128], F32) for _ in range(2)]
            r_tmpS = P.Rs(2)
            PTs = [A.alloc([128, 4, 128], BF16) for _ in range(4)]
            r_PTs = P.Rs(4)
            oa = [A.alloc([128, 1024], BF16) for _ in range(2)]
            r_oa = [P.Rs(4) for _ in range(2)]
            den = [A.alloc([128, 8], F32) for _ in range(2)]
            r_den = P.Rs(2)
            allKa = [r for g in range(2) for r in r_KaT[g]]
            it = 0
            for j in range(NB):
                bp = j % 2
                dma("sp", Bsw[bp].rearrange("p a b c -> p (a b c)"), bias_swa[1 if j == 0 else 0], [], [r_Bsw[bp]])
                for g in range(2):
                    for r in range(2):
                        heads = [2 * i + r for i in range(4 * g, 4 * g + 4)]
                        pts = []
                        for kap in range(2):
                            blk = LB(j) - 1 + kap
                            bk = next_mbank(4)
                            mm(bank(bk).rearrange("p (h q) -> p h q", h=4), KaT[r * 64:(r + 1) * 64, g, blk * 128:(blk + 1) * 128],
                               QaT[r * 64:(r + 1) * 64, 4 * g:4 * g + 4, j * 128:(j + 1) * 128], True, True,
                               allKa + [r_QaT[i][j // 4] for i in range(4 * g, 4 * g + 4)], [rbank[bk]])
                            tk = it % 2
                            pk = it % 4
                            it += 1
                            P.op("dve", lambda e, tk=tk, bk=bk, kap=kap, g=g, r=r, bp=bp: e.scalar_tensor_tensor(
                                tmpS[tk], bank(bk).rearrange("p (h q) -> p h q", h=4), 0.125,
                                Bsw[bp][:, g * 2 + r, :, kap * 128:(kap + 1) * 128], ALU.mult, ALU.add),
                                reads=[rbank[bk], r_Bsw[bp]], writes=[r_tmpS[tk]])
                            P.op("act", lambda e, tk=tk, pk=pk: e.activation(PTs[pk], tmpS[tk], AF.Exp),
                                 reads=[r_tmpS[tk]], writes=[r_PTs[pk]])
                            pts.append(pk)
                        ab = 4 + (j * 4 + g * 2 + r) % 2
                        acc = bank(ab).rearrange("p (h c) -> p h c", h=4)
                        for hh in range(4):
                            for kap in range(2):
                                blk = LB(j) - 1 + kap
                                mm(acc[:, hh, 0:65], PTs[pts[kap]][:, hh, :], Va[:, blk, g, :], kap == 0, kap == 1,
                                   [r_PTs[pts[kap]], r_Va[blk], r_Vaones], [rbank[ab]])
                        dp = (g * 2 + r) % 2
                        hsl = slice((g * 2 + r) * 4, (g * 2 + r) * 4 + 4)
                        P.op("dve", lambda e, dp=dp, acc=acc, hsl=hsl: e.tensor_tensor(den[dp][:, 0:4], acc[:, :, 64], esink[:, hsl], ALU.add),
                             reads=[rbank[ab], r_const], writes=[r_den[dp]])
                        P.op("dve", lambda e, dp=dp: e.reciprocal(den[dp][:, 4:8], den[dp][:, 0:4]),
                             reads=[r_den[dp]], writes=[r_den[dp]])
                        for hh in range(4):
                            h = heads[hh]
                            P.op("dve", lambda e, dp=dp, acc=acc, hh=hh, h=h, bp=bp: e.tensor_scalar_mul(
                                oa[bp][:, h * 64:(h + 1) * 64], acc[:, hh, 0:64], den[dp][:, 4 + hh:5 + hh]),
                                reads=[rbank[ab], r_den[dp]], writes=[r_oa[bp][g * 2 + r]])
                b = next_tbank()
                pb = bank(b).bitcast(BF16)
                for c in range(8):
                    P.op("pe", lambda e, pb=pb, c=c, bp=bp: e.transpose(pb[:, c * 128:(c + 1) * 128], oa[bp][:, c * 128:(c + 1) * 128], idb),
                         reads=r_oa[bp] + [r_idb], writes=[rbank[b]])
                cp(oaT[:, :, j * 128:(j + 1) * 128], pb.rearrange("p (c n) -> p c n", c=8), [rbank[b]], [r_oaT[j]])
        P.barrier(scratch)
        A.reset(xT_mark)

        if stage >= 4:
            KTh = [A.alloc([128, NTOKX], BF16) for _ in range(2)]
            Vh = [A.alloc([128, 74, 129], BF16) for _ in range(2)]
            r_KTh, r_Vh, r_Vhones = P.Rs(2), P.Rs(2), P.Rs(2)
            Bn = A.alloc([128, 5, 512], F32)
            r_Bn = P.R()
            Bf = A.alloc([128, 2, 64, 8], F32)
            r_Bf = P.R()
            tmpD = [A.alloc([128, 2, 512], F32) for _ in range(2)]
            r_tmpD = P.Rs(2)
            PT = [A.alloc([128, 2, 512], BF16) for _ in range(2)]
            r_PT = P.Rs(2)
            accS = A.alloc([128, 3, 512], F32)
            r_accS = P.Rs(3)
            o0 = [A.alloc([128, 128], F32) for _ in range(4)]
            o1 = [A.alloc([128, 128], F32) for _ in range(4)]
            obk = A.alloc([128, 4, 128], BF16)
            st4 = [A.alloc([128, 8], F32) for _ in range(4)]
            r_r0, r_r1, r_r1l, r_o0, r_o1, r_ss, r_ln, r_rstd, r_obk = [P.Rs(4) for _ in range(9)]
            dma("sp", Bf.rearrange("p a b c -> p (a b c)"), bias_far, [], [r_Bf])
            zt = A.alloc([128, 512], BF16)
            r_zt = P.R()
            P.op("pool", lambda e: e.memset(zt, 0.0), writes=[r_zt])
            for k in range(2):
                P.op("pool", lambda e, k=k: e.memset(Vh[k][:, :, 128:129], 1.0), writes=[r_Vhones[k]])
            def accap(a):
                return bank(4 + a // 3)[:, (a % 3) * 160:(a % 3) * 160 + 129]

            def accS_ap(a):
                return accS[:, a // 3, (a % 3) * 160:(a % 3) * 160 + 129]

            def make_stages(h, ch):
                def g0():
                    for qb in range(4):
                        s4, a0, a1 = st4[qb], accS_ap(qb), accS_ap(4 + qb)
                        P.op("dve", lambda e, s4=s4, a0=a0: e.reciprocal(s4[:, 0:1], a0[:, 128:129]),
                             reads=[r_accS[qb // 3]], writes=[r_r0[qb]])
                        P.op("dve", lambda e, s4=s4, a1=a1: e.reciprocal(s4[:, 1:2], a1[:, 128:129]),
                             reads=[r_accS[(4 + qb) // 3]], writes=[r_r1[qb]])

                def g1():
                    for qb in range(4):
                        s4, a0 = st4[qb], accS_ap(qb)
                        P.op("dve", lambda e, s4=s4: e.tensor_tensor(s4[:, 2:3], s4[:, 1:2], neglam, ALU.mult),
                             reads=[r_r1[qb], r_const], writes=[r_r1l[qb]])
                        P.op("dve", lambda e, s4=s4, a0=a0, qb=qb: e.tensor_scalar_mul(o0[qb], a0[:, 0:128], s4[:, 0:1]),
                             reads=[r_accS[qb // 3], r_r0[qb]], writes=[r_o0[qb]])

                def g2():
                    for qb in range(4):
                        s4, a1 = st4[qb], accS_ap(4 + qb)
                        P.op("dve", lambda e, s4=s4, a1=a1, qb=qb: e.scalar_tensor_tensor(o1[qb], a1[:, 0:128], s4[:, 2:3], o0[qb], ALU.mult, ALU.add),
                             reads=[r_accS[(4 + qb) // 3], r_r1l[qb], r_o0[qb]], writes=[r_o1[qb]])
                        P.op("dve", lambda e, s4=s4: e.memset(s4[:, 3:4], 0.0), writes=[r_ss[qb]])

                def g3():
                    for qb in range(4):
                        s4 = st4[qb]
                        P.op("act", lambda e, s4=s4, qb=qb: e.activation(o0[qb], o1[qb], AF.Square, accum_out=s4[:, 3:4]),
                             reads=[r_o1[qb]], writes=[r_ss[qb], r_o0[qb]])

                def g4():
                    for qb in range(4):
                        s4 = st4[qb]
                        P.op("act", lambda e, s4=s4: e.activation(s4[:, 4:5], s4[:, 3:4], AF.Ln, bias=epsc, scale=1.0 / 128.0),
                             reads=[r_ss[qb], r_const], writes=[r_ln[qb]])

                def g5():
                    for qb in range(4):
                        s4 = st4[qb]
                        P.op("act", lambda e, s4=s4: e.activation(s4[:, 5:6], s4[:, 4:5], AF.Exp, scale=-0.5),
                             reads=[r_ln[qb]], writes=[r_rstd[qb]])

                def g6():
                    for qb in range(4):
                        s4 = st4[qb]
                        P.op("dve", lambda e, s4=s4, qb=qb: e.scalar_tensor_tensor(obk[:, qb, :], o1[qb], s4[:, 5:6], wsub, ALU.mult, ALU.mult),
                             reads=[r_o1[qb], r_rstd[qb], r_const], writes=[r_obk[qb]])

                def g7():
                    pb = bank(7).bitcast(BF16)
                    for qb in range(4):
                        P.op("pe", lambda e, pb=pb, qb=qb: e.transpose(pb[:, qb * 128:(qb + 1) * 128], obk[:, qb, :], idb),
                             reads=[r_obk[qb], r_idb], writes=[rbank[7]])
                    cp(obT[:, h, ch * 512:(ch + 1) * 512], pb[:, 0:512], [rbank[7]], [r_obT[h][ch * 4 + qb] for qb in range(4)], eng="dve")

                return [g0, g1, g2, g3, g4, g5, g6, g7]

            r_sc = P.Rs(4)
            PTr = [PT[0][:, 0, :], PT[0][:, 1, :], PT[1][:, 0, :], PT[1][:, 1, :]]
            r_PTr = P.Rs(4)
            tmpr = [tmpD[0][:, 0, :], tmpD[0][:, 1, :], tmpD[1][:, 0, :], tmpD[1][:, 1, :]]
            r_tmpr = P.Rs(4)
            r_acc = P.Rs(3)
            pending = []
            for h in range(8):
                hp = h % 2
                dma("sp", KTh[hp], kt_all[h], [r_ktall], [r_KTh[hp]])
                dma("sp", Vh[hp][:, :, 0:128], v_all[:, :, h * 128:(h + 1) * 128].rearrange("b p n -> p b n"), [r_vall], [r_Vh[hp]])
                for ch in range(2):
                    dma("sp", Bn.rearrange("p a b -> p (a b)"), bias_near[ch, h], [], [r_Bn])
                    tiles = [("far", kb) for kb in range(NFAR[ch])] + [("near", kap) for kap in range(5)]
                    nt = len(tiles)
                    ni = 2 * nt
                    qsl = slice(ch * 512, (ch + 1) * 512)

                    def emit_qk(i):
                        t, m = divmod(i, 2)
                        kind, idx = tiles[t]
                        tok0 = idx * 128 if kind == "far" else S + (ch * 5 + idx) * 128
                        mm(bank(i % 4), KTh[hp][m * 64:(m + 1) * 64, tok0:tok0 + 128],
                           QbT[m * 64:(m + 1) * 64, h, qsl], True, True,
                           [r_KTh[hp], r_QbT[h][ch]], [r_sc[i % 4]])

                    for i in range(3):
                        emit_qk(i)
                    for ab in range(3):
                        mm(bank(4 + ab), zt[:, 0:128], zt, True, False, [r_zt], [r_acc[ab]])
                    for i in range(ni):
                        t, m = divmod(i, 2)
                        kind, idx = tiles[t]
                        ri = i % 4
                        if i + 3 < ni:
                            emit_qk(i + 3)
                        if kind == "far":
                            bias_ap = Bf[:, ch, idx, h:h + 1]
                            P.op("act", lambda e, ri=ri, bias_ap=bias_ap: e.activation(
                                PTr[ri], bank(ri), AF.Exp, bias=bias_ap, scale=0.125),
                                reads=[r_sc[ri], r_Bf], writes=[r_PTr[ri]])
                        else:
                            bn_ap = Bn[:, idx, :]
                            P.op("dve", lambda e, ri=ri, bn_ap=bn_ap: e.scalar_tensor_tensor(
                                tmpr[ri], bank(ri), 0.125, bn_ap, ALU.mult, ALU.add),
                                reads=[r_sc[ri], r_Bn], writes=[r_tmpr[ri]])
                            P.op("act", lambda e, ri=ri: e.activation(PTr[ri], tmpr[ri], AF.Exp),
                                 reads=[r_tmpr[ri]], writes=[r_PTr[ri]])
                        vb = idx if kind == "far" else 64 + ch * 5 + idx
                        for qb in range(4):
                            a = m * 4 + qb
                            mm(accap(a), PTr[ri][:, qb * 128:(qb + 1) * 128], Vh[hp][:, vb, :], False, (t == nt - 1) and a in (2, 5, 7),
                               [r_PTr[ri], r_Vh[hp], r_Vhones[hp]], [r_acc[a // 3]])
                        if pending and i >= 4 and i % 2 == 1:
                            pending.pop(0)()
                    while pending:
                        pending.pop(0)()
                    for ab in range(3):
                        cp(accS[:, ab, :], bank(4 + ab), [r_acc[ab]], [r_accS[ab]], eng="dve")
                    pending = make_stages(h, ch)
            while pending:
                pending.pop(0)()
        P.barrier(scratch)
        A.reset(xT_mark)

        y = None

        def pipeline(nitems, stages):
            ns = len(stages)
            for tau in range(nitems + ns - 1):
                for si, stg in enumerate(stages):
                    i = tau - si
                    if 0 <= i < nitems:
                        stg(i)

        def layer_norm_and_T(y, r_y, lnw, r_lnw, hT, r_hT, stt, final_out=None, r_out=None):
            r_st = [P.Rs(4) for _ in range(NB)]
            r_mv, r_rs, r_nm = P.Rs(NB), P.Rs(NB), P.Rs(NB)

            def s0(b):
                yb, s = y[:, b, :], stt[:, b, :]
                stats = s[:, 0:24].rearrange("p (k s) -> p k s", k=4)
                for k in range(4):
                    P.op("dve", lambda e, k=k, yb=yb, stats=stats: e.bn_stats(stats[:, k, :], yb[:, k * 512:(k + 1) * 512]),
                         reads=[r_y[b][k]], writes=[r_st[b][k]])
                P.op("dve", lambda e, s=s, stats=stats: e.bn_aggr(s[:, 24:26], stats), reads=r_st[b], writes=[r_mv[b]])

            def s1(b):
                s = stt[:, b, :]
                P.op("act", lambda e, s=s: e.activation(s[:, 27:28], s[:, 25:26], AF.Ln, bias=epsc), reads=[r_mv[b], r_const], writes=[r_rs[b]])
                P.op("act", lambda e, s=s: e.activation(s[:, 26:27], s[:, 27:28], AF.Exp, scale=-0.5), reads=[r_rs[b]], writes=[r_rs[b]])

            def s2(b):
                s = stt[:, b, :]
                P.op("dve", lambda e, s=s: e.scalar_tensor_tensor(s[:, 28:29], s[:, 24:25], -1.0, s[:, 26:27], ALU.mult, ALU.mult),
                     reads=[r_mv[b], r_rs[b]], writes=[r_nm[b]])

            def s3(b):
                yb, s = y[:, b, :], stt[:, b, :]
                P.op("act", lambda e, s=s, yb=yb: e.activation(yb, yb, AF.Identity, bias=s[:, 28:29], scale=s[:, 26:27]),
                     reads=[r_nm[b], r_rs[b]] + r_y[b], writes=r_y[b])

            def s4(b):
                yb = y[:, b, :]
                P.op("dve", lambda e, yb=yb: e.tensor_tensor(yb, yb, lnw[:, 0, :], ALU.mult), reads=[r_lnw] + r_y[b], writes=r_y[b])

            def s5(b):
                yb = y[:, b, :]
                P.op("pool", lambda e, yb=yb: e.tensor_tensor(yb, yb, lnw[:, 1, :], ALU.add), reads=[r_lnw] + r_y[b], writes=r_y[b])

            def s6(b):
                yb = y[:, b, :]
                if final_out is not None:
                    dma("sp", final_out[b * 128:(b + 1) * 128, :], yb, r_y[b], [r_out])
                    return
                for q4 in range(4):
                    bk = next_tbank()
                    for c in range(4):
                        cc = q4 * 4 + c
                        P.op("pe", lambda e, bk=bk, c=c, cc=cc, yb=yb: e.transpose(bank(bk)[:, c * 128:(c + 1) * 128], yb[:, cc * 128:(cc + 1) * 128], idf),
                             reads=r_y[b] + [r_idf], writes=[rbank[bk]])
                    cp(hT[:, q4 * 4:(q4 + 1) * 4, b * 128:(b + 1) * 128], bank(bk).rearrange("p (c n) -> p c n", c=4),
                       [rbank[bk]], [r_hT[b][q4]])

            pipeline(NB, [s0, s1, s2, s3, s4, s5, s6])

        if stage >= 5:
            A.reset(117 * 1024)
            mixT = A.alloc([128, 16, TOK], BF16)
            r_mixT = [P.Rs(2) for _ in range(16)]
            e_mark = A.mark()
            Wg = [A.alloc([128, 16, 512], BF16) for _ in range(2)]
            Wab = [A.alloc([128, 8, 512], BF16) for _ in range(2)]
            r_Wg, r_Wab = P.Rs(2), P.Rs(2)
            sg = [A.alloc([128, 512], F32) for _ in range(2)]
            r_sg = P.Rs(2)
            m1 = [A.alloc([128, 512], F32) for _ in range(2)]
            r_m1 = P.Rs(2)
            all_oaT = r_oaT
            all_obT = [r for hh in range(8) for r in r_obT[hh]]
            gi = 0
            for jj in range(8):
                k = jj % 2
                dma("pool", Wg[k][:, :, 0:256], w_in[:, C_GA + jj * 256:C_GA + (jj + 1) * 256].rearrange("(c p) n -> p c n", p=128), [], [r_Wg[k]])
                dma("pool", Wg[k][:, :, 256:512], w_in[:, C_GB + jj * 256:C_GB + (jj + 1) * 256].rearrange("(c p) n -> p c n", p=128), [], [r_Wg[k]])
                dma("pool", Wab[k][:, :, 0:256], w_a[:, jj * 256:(jj + 1) * 256].rearrange("(c p) n -> p c n", p=128), [], [r_Wab[k]])
                dma("pool", Wab[k][:, :, 256:512], w_b[:, jj * 256:(jj + 1) * 256].rearrange("(c p) n -> p c n", p=128), [], [r_Wab[k]])
                for sub in range(2):
                    j = jj * 2 + sub
                    for hlf in range(2):
                        tsl = slice(hlf * 512, (hlf + 1) * 512)
                        for br in range(2):
                            gk = gi % 2
                            gi += 1
                            bg = next_mbank()
                            for c in range(16):
                                mm(bank(bg), Wg[k][:, c, br * 256 + sub * 128:br * 256 + (sub + 1) * 128],
                                   xT[:, c, OWN0[hlf]:OWN0[hlf] + 512], c == 0, c == 15, [r_Wg[k]] + allx, [rbank[bg]])
                            P.op("act", lambda e, gk=gk, bg=bg: e.activation(sg[gk], bank(bg), AF.Sigmoid),
                                 reads=[rbank[bg]], writes=[r_sg[gk]])
                            bb = next_mbank()
                            srcT = oaT if br == 0 else obT
                            rsrc = all_oaT if br == 0 else all_obT
                            for c in range(8):
                                mm(bank(bb), Wab[k][:, c, br * 256 + sub * 128:br * 256 + (sub + 1) * 128], srcT[:, c, tsl],
                                   c == 0, c == 7, [r_Wab[k]] + rsrc, [rbank[bb]])
                            if br == 0:
                                mk = (gi // 2) % 2
                                P.op("dve", lambda e, mk=mk, bb=bb, gk=gk: e.tensor_tensor(m1[mk], bank(bb), sg[gk], ALU.mult),
                                     reads=[rbank[bb], r_sg[gk]], writes=[r_m1[mk]])
                                mk_a = mk
                            else:
                                P.op("dve", lambda e, bb=bb, gk=gk: e.tensor_tensor(sg[gk], bank(bb), sg[gk], ALU.mult),
                                     reads=[rbank[bb], r_sg[gk]], writes=[r_sg[gk]])
                                P.op("dve", lambda e, gk=gk, mk_a=mk_a, j=j, tsl=tsl: e.tensor_tensor(mixT[:, j, tsl], m1[mk_a], sg[gk], ALU.add),
                                     reads=[r_m1[mk_a], r_sg[gk]], writes=[r_mixT[j][hlf]])
            P.barrier(scratch)
            A.reset(M0)
            y = A.alloc([128, NB, D], F32)
            r_y = [P.Rs(4) for _ in range(NB)]
            lnw = A.alloc([128, 2, D], F32)
            r_lnw = P.R()
            stt = A.alloc([128, NB, 32], F32)
            hT = A.alloc([128, 16, TOK], BF16)
            r_hT = [P.Rs(4) for _ in range(NB)]
            h_mark = A.mark()
            assert h_mark <= 117 * 1024
            A.reset(e_mark)
            for b in range(NB):
                dma("sp", y[:, b, :], x_own[LB(b) * 128:(LB(b) + 1) * 128, :], [], r_y[b])
            dma("sp", lnw, lnp[:, 0:2, :], [], [r_lnw])
            Wo = [A.alloc([128, 16, 512], BF16) for _ in range(2)]
            r_Wo = P.Rs(2)
            all_mix = [r for j in range(16) for r in r_mixT[j]]
            for pn in range(4):
                k = pn % 2
                dma("pool", Wo[k], w_o[:, pn * 512:(pn + 1) * 512].rearrange("(c p) n -> p c n", p=128), [], [r_Wo[k]])
                for b in range(NB):
                    bk = next_mbank()
                    for c in range(16):
                        mm(bank(bk), mixT[:, c, b * 128:(b + 1) * 128], Wo[k][:, c, :], c == 0, c == 15,
                           [r_Wo[k], r_mixT[c][b // 4]], [rbank[bk]])
                    ysl = y[:, b, pn * 512:(pn + 1) * 512]
                    P.op("dve", lambda e, ysl=ysl, bk=bk: e.scalar_tensor_tensor(ysl, ysl, ALPHA, bank(bk), ALU.mult, ALU.add),
                         reads=[rbank[bk], r_y[b][pn]], writes=[r_y[b][pn]])
            layer_norm_and_T(y, r_y, lnw, r_lnw, hT, r_hT, stt)
            all_hT = [r for b in range(NB) for r in r_hT[b]]

        if stage >= 6:
            P.barrier(scratch)
            A.reset(h_mark)
            memb = A.alloc([128, 2, D], BF16)
            memT = A.alloc([128, 16, 256], BF16)
            r_memb = P.R()
            r_memT = [P.Rs(2) for _ in range(2)]
            Wflat = [A.alloc([128, 8192], BF16) for _ in range(2)]
            Wm = [w.rearrange("p (c n) -> p c n", c=16) for w in Wflat]
            r_Wm = P.Rs(2)
            KcT = A.alloc([128, 4, 256], BF16)
            Vc = A.alloc([128, 2, 4, 129], BF16)
            r_KcT = P.Rs(4)
            r_Vc = P.Rs(2)
            r_Vcones = P.R()
            qcT = A.alloc([128, 4, TOK], BF16)
            r_qcT = [P.Rs(2) for _ in range(4)]
            oc = A.alloc([128, NB, 512], BF16)
            r_oc = [P.Rs(4) for _ in range(NB)]
            ocT = A.alloc([128, 4, TOK], BF16)
            r_ocT = P.Rs(NB)
            Wco = Wflat[1].rearrange("p (c n) -> p c n", c=4)
            r_Wco = r_Wm[1]
            PTc = [A.alloc([128, 512], BF16) for _ in range(4)]
            r_PTc = P.Rs(4)
            rc = [A.alloc([128, 4], F32) for _ in range(2)]
            r_rc = P.Rs(2)
            dma("pool", memb, mem.rearrange("(b p) d -> p b d", p=128), [], [r_memb])
            dma("pool", Wm[0], w_mkv[:, 0:512].rearrange("(c p) n -> p c n", p=128), [], [r_Wm[0]])
            dma("pool", Wm[1], w_mkv[:, 512:1024].rearrange("(c p) n -> p c n", p=128), [], [r_Wm[1]])
            dma("sp", lnw, lnp[:, 2:4, :], [], [r_lnw])
            for mbk in range(2):
                transpose_block_bf16(memb[:, mbk, :], lambda hlf, mbk=mbk: memT[:, hlf * 8:(hlf + 1) * 8, mbk * 128:(mbk + 1) * 128],
                                     r_memb, lambda hlf, mbk=mbk: r_memT[mbk][hlf])
            all_memT = [r for a in r_memT for r in a]
            for h in range(4):
                bk = next_mbank()
                for c in range(16):
                    mm(bank(bk)[:, 0:256], Wm[0][:, c, h * 128:(h + 1) * 128], memT[:, c, :], c == 0, c == 15,
                       [r_Wm[0]] + all_memT, [rbank[bk]])
                cp(KcT[:, h, :], bank(bk)[:, 0:256], [rbank[bk]], [r_KcT[h]])
            P.op("pool", lambda e: e.memset(Vc[:, :, :, 128:129], 1.0), writes=[r_Vcones])
            for mbk in range(2):
                bk = next_mbank()
                for c in range(16):
                    mm(bank(bk), memT[:, c, mbk * 128:(mbk + 1) * 128], Wm[1][:, c, :], c == 0, c == 15,
                       [r_Wm[1], r_memT[mbk][c // 8]], [rbank[bk]])
                cp(Vc[:, mbk, :, 0:128], bank(bk).rearrange("p (h d) -> p h d", h=4), [rbank[bk]], [r_Vc[mbk]])
            Wq = Wm[0]
            r_Wq = r_Wm[0]
            dma("pool", Wq, w_cq.rearrange("(c p) n -> p c n", p=128), [], [r_Wq])
            dma("pool", Wco, w_co.rearrange("(c p) n -> p c n", p=128), [], [r_Wco])
            for h in range(4):
                for hlf in range(2):
                    bk = next_mbank()
                    for c in range(16):
                        mm(bank(bk), Wq[:, c, h * 128:(h + 1) * 128], hT[:, c, hlf * 512:(hlf + 1) * 512], c == 0, c == 15,
                           [r_Wq] + all_hT, [rbank[bk]])
                    cp(qcT[:, h, hlf * 512:(hlf + 1) * 512], bank(bk), [rbank[bk]], [r_qcT[h][hlf]])
            ci = 0
            CSC = 128.0 ** -0.5
            for h in range(4):
                for hlf in range(2):
                    pts = []
                    for mbk in range(2):
                        bk = next_mbank(4)
                        mm(bank(bk), KcT[:, h, mbk * 128:(mbk + 1) * 128], qcT[:, h, hlf * 512:(hlf + 1) * 512], True, True,
                           [r_KcT[h], r_qcT[h][hlf]], [rbank[bk]])
                        pk = ci % 4
                        ci += 1
                        P.op("act", lambda e, pk=pk, bk=bk: e.activation(PTc[pk], bank(bk), AF.Exp, scale=CSC),
                             reads=[rbank[bk]], writes=[r_PTc[pk]])
                        pts.append(pk)
                    ab0 = 4 + 2 * ((h * 2 + hlf) % 2)
                    def cacc(qb, ab0=ab0):
                        return bank(ab0 + qb // 2)[:, (qb % 2) * 160:(qb % 2) * 160 + 129]
                    for qb in range(4):
                        for mbk in range(2):
                            mm(cacc(qb), PTc[pts[mbk]][:, qb * 128:(qb + 1) * 128],
                               Vc[:, mbk, h, :], mbk == 0, mbk == 1, [r_PTc[pts[mbk]], r_Vc[mbk], r_Vcones], [rbank[ab0 + qb // 2]])
                    rk = (h * 2 + hlf) % 2
                    for qb in range(4):
                        a_ap = cacc(qb)
                        blk = hlf * 4 + qb
                        P.op("dve", lambda e, rk=rk, qb=qb, a_ap=a_ap: e.reciprocal(rc[rk][:, qb:qb + 1], a_ap[:, 128:129]),
                             reads=[rbank[ab0 + qb // 2]], writes=[r_rc[rk]])
                        P.op("dve", lambda e, rk=rk, qb=qb, a_ap=a_ap, blk=blk, h=h: e.tensor_scalar_mul(
                            oc[:, blk, h * 128:(h + 1) * 128], a_ap[:, 0:128], rc[rk][:, qb:qb + 1]),
                            reads=[rbank[ab0 + qb // 2], r_rc[rk]], writes=[r_oc[blk][h]])
            for b in range(NB):
                bk = next_tbank()
                pb = bank(bk).bitcast(BF16)
                for c in range(4):
                    P.op("pe", lambda e, pb=pb, c=c, b=b: e.transpose(pb[:, c * 128:(c + 1) * 128], oc[:, b, c * 128:(c + 1) * 128], idb),
                         reads=r_oc[b] + [r_idb], writes=[rbank[bk]])
                cp(ocT[:, :, b * 128:(b + 1) * 128], pb[:, 0:512].rearrange("p (c n) -> p c n", c=4), [rbank[bk]], [r_ocT[b]])
            for b in range(NB):
                for pn in range(4):
                    bk = next_mbank()
                    for c in range(4):
                        mm(bank(bk), ocT[:, c, b * 128:(b + 1) * 128], Wco[:, c, pn * 512:(pn + 1) * 512], c == 0, c == 3,
                           [r_Wco, r_ocT[b]], [rbank[bk]])
                    ysl = y[:, b, pn * 512:(pn + 1) * 512]
                    P.op("dve", lambda e, ysl=ysl, bk=bk: e.scalar_tensor_tensor(ysl, ysl, ALPHA, bank(bk), ALU.mult, ALU.add),
                         reads=[rbank[bk], r_y[b][pn]], writes=[r_y[b][pn]])
            layer_norm_and_T(y, r_y, lnw, r_lnw, hT, r_hT, stt)
            P.barrier(scratch)
            A.reset(h_mark)

        if stage >= 7:
            dma("sp", lnw, lnp[:, 4:6, :], [], [r_lnw])
            Wgu = [A.alloc([128, 16, 512], BF16) for _ in range(2)]
            Wd = [A.alloc([128, 2, D], BF16) for _ in range(2)]
            r_Wgu, r_Wd = P.Rs(2), P.Rs(2)
            Aff = [A.alloc([128, 2, TOK], BF16) for _ in range(2)]
            r_Aff = [[P.Rs(2) for _ in range(2)] for _ in range(2)]
            sgf = [A.alloc([128, 512], F32) for _ in range(2)]
            r_sgf = P.Rs(2)
            fi = 0
            NSC = DFF // 256
            for s in range(NSC):
                k = s % 2
                dma("pool", Wgu[k][:, :, 0:256], w_gu[:, s * 256:(s + 1) * 256].rearrange("(c p) n -> p c n", p=128), [], [r_Wgu[k]])
                dma("pool", Wgu[k][:, :, 256:512], w_gu[:, DFF + s * 256:DFF + (s + 1) * 256].rearrange("(c p) n -> p c n", p=128), [], [r_Wgu[k]])
                dma("pool", Wd[k], w_dn[s * 256:(s + 1) * 256, :].rearrange("(c p) n -> p c n", p=128), [], [r_Wd[k]])
                for sub in range(2):
                    for hlf in range(2):
                        bg = next_mbank()
                        for c in range(16):
                            mm(bank(bg), Wgu[k][:, c, sub * 128:(sub + 1) * 128], hT[:, c, hlf * 512:(hlf + 1) * 512], c == 0, c == 15,
                               [r_Wgu[k]] + all_hT, [rbank[bg]])
                        bu = next_mbank()
                        for c in range(16):
                            mm(bank(bu), Wgu[k][:, c, 256 + sub * 128:256 + (sub + 1) * 128], hT[:, c, hlf * 512:(hlf + 1) * 512], c == 0, c == 15,
                               [r_Wgu[k]] + all_hT, [rbank[bu]])
                        fk = fi % 2
                        fi += 1
                        P.op("act", lambda e, fk=fk, bg=bg: e.activation(sgf[fk], bank(bg), AF.Silu), reads=[rbank[bg]], writes=[r_sgf[fk]])
                        P.op("dve", lambda e, fk=fk, bu=bu, k=k, sub=sub, hlf=hlf: e.tensor_tensor(
                            Aff[k][:, sub, hlf * 512:(hlf + 1) * 512], bank(bu), sgf[fk], ALU.mult),
                            reads=[rbank[bu], r_sgf[fk]], writes=[r_Aff[k][sub][hlf]])
                for b in range(NB):
                    for pn in range(4):
                        bk = next_mbank()
                        for sub in range(2):
                            mm(bank(bk), Aff[k][:, sub, b * 128:(b + 1) * 128], Wd[k][:, sub, pn * 512:(pn + 1) * 512], sub == 0, sub == 1,
                               [r_Wd[k], r_Aff[k][sub][b // 4]], [rbank[bk]])
                        ysl = y[:, b, pn * 512:(pn + 1) * 512]
                        if s == 0:
                            P.op("dve", lambda e, ysl=ysl, bk=bk: e.scalar_tensor_tensor(ysl, ysl, ALPHA, bank(bk), ALU.mult, ALU.add),
                                 reads=[rbank[bk], r_y[b][pn]], writes=[r_y[b][pn]])
                        else:
                            P.op("dve", lambda e, ysl=ysl, bk=bk: e.tensor_tensor(ysl, ysl, bank(bk), ALU.add),
                                 reads=[rbank[bk], r_y[b][pn]], writes=[r_y[b][pn]])
            r_out = P.R()
            layer_norm_and_T(y, r_y, lnw, r_lnw, None, None, stt, final_out=out, r_out=r_out)
            P.final_wait(r_out)

        if dbg is not None:
            P.barrier(scratch)
            r_dbg = P.R()
            if stage == 3:
                stg = A.alloc([128, 8, TOK], F32)
                P.op("dve", lambda e: e.tensor_copy(stg, oaT), reads=r_oaT, writes=[r_dbg])
                dma("sp", dbg_out.rearrange("(c p) t -> p c t", p=128), stg, [r_dbg], [r_dbg])
            elif stage == 4:
                stg = A.alloc([128, 8, TOK], F32)
                P.op("dve", lambda e: e.tensor_copy(stg, obT), reads=[r for hh in range(8) for r in r_obT[hh]], writes=[r_dbg])
                dma("sp", dbg_out.rearrange("(c p) t -> p c t", p=128), stg, [r_dbg], [r_dbg])
            elif stage in (5, 6):
                for b in range(NB):
                    dma("sp", dbg_out[b * 128:(b + 1) * 128, :], y[:, b, :], r_y[b], [r_dbg])
            P.final_wait(r_dbg)
        P.emit()
    return nc


def _rel_bucket(dist):
    n = np.maximum(dist, 0)
    exact = 16
    logv = np.log(np.maximum(n, 1).astype(np.float32) / np.float32(exact)) / np.float32(math.log(128 / 16))
    large = exact + (logv * np.float32(32 - exact)).astype(np.int32)
    large = np.minimum(large, 31)
    return np.where(n < exact, n, large)


def _bias_tables(table, core):
    k = np.arange(128)[:, None]
    q = np.arange(128)[None, :]
    sw = np.full((2, 128, 16, 2, 128), NEG, np.float32)
    for kap in range(2):
        dist = (128 + q - k) if kap == 0 else (q - k)
        ok = (dist >= 0) & (dist < 128)
        bk = _rel_bucket(dist)
        for h in range(16):
            vals = np.where(ok, table[bk, h], np.float32(NEG)).astype(np.float32)
            sw[0, :, h, kap, :] = vals
            if kap == 1:
                sw[1, :, h, kap, :] = vals
    if core != 0:
        sw[1] = sw[0]
    perm = [8 * g + 2 * i4 + r for g in range(2) for r in range(2) for i4 in range(4)]
    sw = sw[:, :, perm]
    q5 = np.arange(512)[None, :]
    near = np.full((2, 8, 128, 5, 512), NEG, np.float32)
    for kap in range(5):
        dist = q5 - 128 * (kap - 1) - k
        ok = dist >= 0
        bk = _rel_bucket(dist)
        for h in range(8):
            vals = np.where(ok, table[bk, 16 + h], np.float32(NEG)).astype(np.float32)
            near[:, h, :, kap, :] = vals[None]
    if core == 0:
        near[0, :, :, 0, :] = NEG
    far = np.full((128, 2, 64, 8), NEG, np.float32)
    for ch in range(2):
        g0 = 4 * (core if ch == 0 else 15 - core)
        for kb in range(64):
            if kb <= g0 - 2:
                far[:, ch, kb, :] = table[31, 16:24][None, :]
    return sw.reshape(2, 128, -1), near.reshape(2, 8, 128, -1), far.reshape(128, -1)


_NC_CACHE = {}


def kernel(x, mem, rel_bias_table, w_in, sinks, lambda_q1, lambda_k1, lambda_q2, lambda_k2,
           subln_w, w_branch_a, w_branch_b, w_o, ln1_g, ln1_b, w_cq, w_mem_kv, w_co,
           ln2_g, ln2_b, w_gate_up, w_down, ln3_g, ln3_b, _stage=99, _dbg=None):
    f = lambda a: np.ascontiguousarray(np.asarray(a, dtype=np.float32))
    x2 = f(x).reshape(S, D)
    table = f(rel_bias_table)
    lnp = np.stack([f(ln1_g)[0], f(ln1_b)[0], f(ln2_g)[0], f(ln2_b)[0], f(ln3_g)[0], f(ln3_b)[0]], 0)
    lnp = np.ascontiguousarray(np.broadcast_to(lnp[None], (128, 6, D)))
    perm = [8 * g + 2 * i4 + r for g in range(2) for r in range(2) for i4 in range(4)]
    small = np.concatenate([f(sinks)[0][perm], f(lambda_q1)[0], f(lambda_k1)[0], f(lambda_q2)[0], f(lambda_k2)[0], f(subln_w)[0]])
    small = np.ascontiguousarray(np.broadcast_to(small[None], (128, small.shape[0])))
    shared = {
        "x_all": x2, "mem": f(mem)[0], "w_in": f(w_in)[0], "w_branch_a": f(w_branch_a)[0], "w_branch_b": f(w_branch_b)[0],
        "w_o": f(w_o)[0], "w_cq": f(w_cq)[0], "w_mem_kv": f(w_mem_kv)[0], "w_co": f(w_co)[0],
        "w_gate_up": f(w_gate_up)[0], "w_down": f(w_down)[0], "lnp": lnp, "smallp": small,
        "ident": np.eye(128, dtype=np.float32),
    }
    in_maps = []
    for c in range(NCORES):
        xo = np.zeros((NOWN, D), np.float32)
        for ch, gc in enumerate((c, 15 - c)):
            r0 = ch * 640
            if gc > 0:
                xo[r0:r0 + 128] = x2[gc * 512 - 128:gc * 512]
            xo[r0 + 128:r0 + 640] = x2[gc * 512:(gc + 1) * 512]
        sw, near, far = _bias_tables(table, c)
        m = dict(shared)
        m.update({"x_own": xo, "bias_swa": sw, "bias_near": near, "bias_far": far})
        in_maps.append(m)
    key = (_stage, _dbg)
    if key not in _NC_CACHE:
        _NC_CACHE[key] = build(_stage, _dbg)
    nc = _NC_CACHE[key]
    res = run_bass_kernel_spmd(nc, in_maps, core_ids=list(range(NCORES)))
    if _dbg is not None:
        return np.concatenate([r["dbg"] for r in res.results], axis=0)
    o = np.empty((S, D), np.float32)
    for c in range(NCORES):
        oc = res.results[c]["out"]
        o[c * 512:(c + 1) * 512] = oc[0:512]
        o[(15 - c) * 512:(16 - c) * 512] = oc[512:1024]
    return o.reshape(1, S, D)
```

```python
import math
import contextlib
import numpy as np
import concourse.bass as bass
import concourse.mybir as mybir
from concourse.bass_utils import run_bass_kernel_spmd

F32 = mybir.dt.float32
BF16 = mybir.dt.bfloat16
AF = mybir.ActivationFunctionType
ALU = mybir.AluOpType
AX = mybir.AxisListType

NCORES = 8
S = 8192
D = 2048
TOK = S // NCORES
NB = TOK // 128
DFF = 5632
NEG = -30000.0
ALPHA = 2.0 ** 0.25
LAMBDA_INIT = 0.8 - 0.6 * math.exp(0.0)
EPS = 1e-5
C_QA, C_KA, C_VA, C_QB, C_KB, C_VB, C_GA, C_GB = 0, 1024, 1152, 1280, 2304, 3328, 4352, 6400
NOWN = 1280
NTOKX = S + NOWN
OWN0 = (128, 768)
NFAR = (27, 59)


def LB(j):
    return j + 1 if j < 4 else j + 2


class Region:
    __slots__ = ("name", "last_w", "readers")

    def __init__(self, name):
        self.name = name
        self.last_w = None
        self.readers = []


class Op:
    __slots__ = ("idx", "eng", "fn", "deps", "dma", "token", "signal", "prewait")

    def __init__(self, idx, eng, fn, dma):
        self.idx = idx
        self.eng = eng
        self.fn = fn
        self.dma = dma
        self.deps = set()
        self.token = None
        self.signal = False
        self.prewait = None


class Prog:
    ENGS = ("pe", "act", "dve", "pool", "sp")
    NDMA_SEM = 8

    def __init__(self, nc, same_engine_sync=True):
        self.nc = nc
        self.ops = []
        self.same_engine_sync = same_engine_sync
        self.finals = []
        self.barrier_idx = None
        self.last_on = {}
        self.dma_since = []

    def R(self, name=""):
        return Region(name)

    def Rs(self, n, name=""):
        return [Region(name) for _ in range(n)]

    def op(self, eng, fn, reads=(), writes=(), dma=False):
        o = Op(len(self.ops), eng, fn, dma)
        for r in reads:
            if r.last_w is not None:
                o.deps.add(r.last_w)
        for w in writes:
            if w.last_w is not None:
                o.deps.add(w.last_w)
            for rd in w.readers:
                o.deps.add(rd)
        for r in reads:
            r.readers.append(o.idx)
        for w in writes:
            w.last_w = o.idx
            w.readers = []
        if self.barrier_idx is not None:
            o.deps.add(self.barrier_idx)
        o.deps.discard(o.idx)
        self.ops.append(o)
        if dma:
            self.dma_since.append(o.idx)
        else:
            self.last_on[eng] = o.idx
        return o

    def barrier(self, scratch):
        o = Op(len(self.ops), "dve", lambda e: e.memset(scratch, 0.0), False)
        for e, i in self.last_on.items():
            o.deps.add(i)
        for i in self.dma_since:
            o.deps.add(i)
        self.dma_since = []
        if self.barrier_idx is not None:
            o.deps.add(self.barrier_idx)
        self.ops.append(o)
        self.last_on["dve"] = o.idx
        self.barrier_idx = o.idx
        return o

    def final_wait(self, region):
        self.finals.append(region.last_w)

    def _needs_sync(self, dep, ename):
        if dep.eng == ename and not dep.dma:
            if ename == "pe" or not self.same_engine_sync:
                return False
        return True

    def emit(self):
        nc = self.nc
        ops = self.ops
        for o in ops:
            for d in o.deps:
                dep = ops[d]
                if self._needs_sync(dep, o.eng):
                    dep.signal = True
        for f in self.finals:
            ops[f].signal = True
        for o in ops:
            if o.dma:
                o.signal = True
        with contextlib.ExitStack() as st:
            esem = {e: st.enter_context(nc.semaphore(f"s_{e}")) for e in ("pe", "act", "dve", "pool")}
            dsem = {q: [st.enter_context(nc.semaphore(f"d_{q}{k}")) for k in range(self.NDMA_SEM)]
                    for q in ("sp", "act", "pool")}
            cnt = {e: 0 for e in esem}
            dcnt = {q: 0 for q in dsem}
            for o in ops:
                if o.dma:
                    i = dcnt[o.eng]
                    dcnt[o.eng] += 1
                    k = i % self.NDMA_SEM
                    m = i // self.NDMA_SEM
                    o.token = (dsem[o.eng][k], 16 * (m + 1))
                    if m > 0:
                        o.prewait = (dsem[o.eng][k], 16 * m)
                elif o.signal:
                    cnt[o.eng] += 1
                    o.token = (esem[o.eng], cnt[o.eng])
            self.sem_counts = dict(cnt)
            block = st.enter_context(nc.Block())
            engobj = {"pe": "tensor", "act": "scalar", "dve": "vector", "pool": "gpsimd", "sp": "sync"}

            def run(ename, eng):
                seen = {}

                def wait(tok):
                    sem, v = tok
                    if seen.get(sem.num, 0) >= v:
                        return
                    eng.wait_ge(sem, v)
                    seen[sem.num] = v

                for o in ops:
                    if o.eng != ename:
                        continue
                    for d in sorted(o.deps, reverse=True):
                        dep = ops[d]
                        if not self._needs_sync(dep, ename):
                            continue
                        wait(dep.token)
                    if o.prewait is not None:
                        wait(o.prewait)
                    ins = o.fn(eng)
                    if o.token is not None:
                        sem, v = o.token
                        ins.then_inc(sem, 16 if o.dma else 1)
                if ename == "sp":
                    for f in self.finals:
                        wait(ops[f].token)

            for ename in self.ENGS:
                getattr(block, engobj[ename])(lambda eng, _e=ename: run(_e, eng))


class Arena:
    def __init__(self, t, nbytes):
        self.t = t
        self.cap = nbytes
        self.off = 0

    def alloc(self, shape, dtype):
        esz = 2 if dtype == BF16 else 4
        n = int(np.prod(shape[1:])) * esz
        off = (self.off + 63) // 64 * 64
        assert off + n <= self.cap, f"SBUF arena overflow: {off + n} > {self.cap}"
        self.off = off + n
        ap = self.t[:, off // 2:(off + n) // 2]
        if dtype != BF16:
            ap = ap.bitcast(dtype)
        if len(shape) == 3:
            ap = ap.rearrange("p (a b) -> p a b", a=shape[1])
        elif len(shape) == 4:
            ap = ap.rearrange("p (a b c) -> p a b c", a=shape[1], b=shape[2])
        elif len(shape) == 5:
            ap = ap.rearrange("p (a b c d) -> p a b c d", a=shape[1], b=shape[2], c=shape[3])
        return ap

    def mark(self):
        return self.off

    def reset(self, m):
        self.off = m


def build(stage=99, dbg=None):
    nc = bass.Bass("TRN2", target_bir_lowering=False)

    def din(name, shape, dt=F32):
        return nc.dram_tensor(name, shape, dt, kind="ExternalInput").ap()

    x_all = din("x_all", [S, D])
    x_own = din("x_own", [NOWN, D])
    mem = din("mem", [256, D])
    w_in = din("w_in", [D, 8448])
    w_a = din("w_branch_a", [1024, D])
    w_b = din("w_branch_b", [1024, D])
    w_o = din("w_o", [D, D])
    w_cq = din("w_cq", [D, 512])
    w_mkv = din("w_mem_kv", [D, 1024])
    w_co = din("w_co", [512, D])
    w_gu = din("w_gate_up", [D, 2 * DFF])
    w_dn = din("w_down", [DFF, D])
    lnp = din("lnp", [128, 6, D])
    smallp = din("smallp", [128, 16 + 256 + 128])
    ident_d = din("ident", [128, 128])
    bias_far = din("bias_far", [128, 2 * 64 * 8])
    bias_near = din("bias_near", [2, 8, 128, 5 * 512])
    bias_swa = din("bias_swa", [2, 128, 16 * 2 * 128])
    out = nc.dram_tensor("out", [TOK, D], F32, kind="ExternalOutput").ap()
    kt_all = nc.dram_tensor("kt_all", [8, 128, NTOKX], BF16).ap()
    v_all = nc.dram_tensor("v_all", [74, 128, 1024], BF16).ap()
    dbg_out = None
    if dbg is not None:
        dbg_out = nc.dram_tensor("dbg", [TOK, dbg], F32, kind="ExternalOutput").ap()

    P = Prog(nc)
    with contextlib.ExitStack() as st:
        ARENA_BYTES = 207 * 1024
        arena_t = st.enter_context(nc.sbuf_tensor("arena", [128, ARENA_BYTES // 2], BF16))
        A = Arena(arena_t, ARENA_BYTES)
        pp = [st.enter_context(nc.psum_tensor(f"pp{i}", [128, 1024], F32)) for i in range(4)]

        def bank(b):
            return pp[b // 2][:, (b % 2) * 512:(b % 2 + 1) * 512]

        rbank = P.Rs(8, "bank")
        cpctr = [0]

        def cp(out_ap, in_ap, reads, writes, eng=None):
            if eng is None:
                eng = ("dve", "act")[cpctr[0] % 2]
                cpctr[0] += 1
            if eng == "act":
                return P.op("act", lambda e: e.activation(out_ap, in_ap, AF.Copy), reads=reads, writes=writes)
            if eng == "pool":
                return P.op("pool", lambda e: e.tensor_copy(out_ap, in_ap), reads=reads, writes=writes)
            return P.op("dve", lambda e: e.tensor_copy(out_ap, in_ap), reads=reads, writes=writes)

        def mm(out_ap, lhsT, rhs, start, stop, reads, writes):
            return P.op("pe", lambda e: e.matmul(out_ap, lhsT, rhs, start=start, stop=stop),
                        reads=reads, writes=writes)

        def dma(q, out_ap, in_ap, reads, writes):
            return P.op(q, lambda e: e.dma_start(out=out_ap, in_=in_ap), reads=reads, writes=writes, dma=True)

        idb = A.alloc([128, 128], BF16)
        idf = A.alloc([128, 128], F32)
        small = A.alloc([128, 400], F32)
        scratch = A.alloc([128, 16], F32)
        esink = A.alloc([128, 16], F32)
        neglam = A.alloc([128, 1], F32)
        wsub = A.alloc([128, 128], F32)
        lam_t = A.alloc([128, 8], F32)
        epsc = A.alloc([128, 1], F32)
        r_idb, r_idf, r_small, r_const = P.R(), P.R(), P.R(), P.R()
        dma("pool", idb, ident_d, [], [r_idb])
        dma("sp", idf, ident_d, [], [r_idf])
        dma("sp", small, smallp, [], [r_small])
        P.op("dve", lambda e: e.memset(epsc, EPS), writes=[r_const])
        P.op("act", lambda e: e.activation(esink, small[:, 0:16], AF.Exp), reads=[r_small], writes=[r_const])
        lq = small[:, 16:272].rearrange("p (a b) -> p a b", a=4)
        prod_t = A.alloc([128, 2, 64], F32)
        P.op("dve", lambda e: e.tensor_tensor(prod_t[:, 0, :], lq[:, 0, :], lq[:, 1, :], ALU.mult), reads=[r_small], writes=[r_const])
        P.op("dve", lambda e: e.tensor_tensor(prod_t[:, 1, :], lq[:, 2, :], lq[:, 3, :], ALU.mult), reads=[r_small], writes=[r_const])
        P.op("dve", lambda e: e.reduce_sum(lam_t[:, 0:2], prod_t, axis=AX.X), reads=[r_const], writes=[r_const])
        P.op("act", lambda e: e.activation(lam_t[:, 2:4], lam_t[:, 0:2], AF.Exp), reads=[r_const], writes=[r_const])
        P.op("dve", lambda e: e.tensor_tensor(lam_t[:, 4:5], lam_t[:, 3:4], lam_t[:, 2:3], ALU.subtract), reads=[r_const], writes=[r_const])
        P.op("dve", lambda e: e.tensor_scalar_add(neglam, lam_t[:, 4:5], -LAMBDA_INIT), reads=[r_const], writes=[r_const])
        P.op("dve", lambda e: e.tensor_scalar_mul(wsub, small[:, 272:400], (1.0 - LAMBDA_INIT)), reads=[r_small], writes=[r_const])
        assert A.mark() <= 4 * 1024
        M0 = 4 * 1024
        A.reset(M0)

        xT = A.alloc([128, 16, NOWN], BF16)
        r_xT = [[P.R() for _ in range(2)] for _ in range(10)]
        after_xT = A.mark()
        QbT = A.alloc([128, 8, TOK], BF16)
        oaT = A.alloc([128, 8, TOK], BF16)
        obT = A.alloc([128, 8, TOK], BF16)
        xT_mark = A.mark()
        assert xT_mark <= 93 * 1024
        A.reset(after_xT)

        Wkb = A.alloc([128, 16, 1024], BF16)
        Wvb = A.alloc([128, 16, 1024], BF16)
        r_Wkb, r_Wvb = P.Rs(2), P.Rs(2)
        for hlf in range(2):
            dma("pool", Wkb[:, :, hlf * 512:(hlf + 1) * 512],
                w_in[:, C_KB + hlf * 512:C_KB + (hlf + 1) * 512].rearrange("(c p) n -> p c n", p=128), [], [r_Wkb[hlf]])
        for hlf in range(2):
            dma("pool", Wvb[:, :, hlf * 512:(hlf + 1) * 512],
                w_in[:, C_VB + hlf * 512:C_VB + (hlf + 1) * 512].rearrange("(c p) n -> p c n", p=128), [], [r_Wvb[hlf]])
        xs = [A.alloc([128, 4, D], BF16) for _ in range(2)]
        xTs = [A.alloc([128, 16, 512], BF16) for _ in range(2)]
        KTs = [A.alloc([128, 8, 512], BF16) for _ in range(2)]
        Vs = [A.alloc([128, 4, 1024], BF16) for _ in range(2)]
        r_xs = P.Rs(2)
        r_xTs = [[[P.R() for _ in range(2)] for _ in range(4)] for _ in range(2)]
        r_KTs = [P.Rs(8) for _ in range(2)]
        r_Vs = [[[P.R() for _ in range(2)] for _ in range(4)] for _ in range(2)]
        r_ktall, r_vall = P.R(), P.R()
        tb = [0]
        mb = [0]

        def next_tbank():
            b = 6 + tb[0] % 2
            tb[0] += 1
            return b

        def next_mbank(n=6):
            b = mb[0] % n
            mb[0] += 1
            return b

        def transpose_block_bf16(src_tok_major, dst_fn, rsrc, rdst_fn):
            for hlf in range(2):
                b = next_tbank()
                pb = bank(b).bitcast(BF16)
                for c in range(8):
                    cc = hlf * 8 + c
                    P.op("pe", lambda e, pb=pb, c=c, cc=cc: e.transpose(pb[:, c * 128:(c + 1) * 128],
                                                                       src_tok_major[:, cc * 128:(cc + 1) * 128], idb),
                         reads=[rsrc, r_idb], writes=[rbank[b]])
                cp(dst_fn(hlf), pb.rearrange("p (c n) -> p c n", c=8), [rbank[b]], [rdst_fn(hlf)])

        NG = 19
        for g in range(16 if (dbg is not None and stage < 4) else 0, NG if stage >= 1 else 0):
            par = g % 2
            nb = 4 if g < 18 else 2
            ntok = nb * 128
            if g < 16:
                src = x_all[g * 512:(g + 1) * 512, :]
            else:
                src = x_own[(g - 16) * 512:(g - 16) * 512 + ntok, :]
            dma("pool", xs[par][:, 0:nb, :], src.rearrange("(b p) d -> p b d", p=128), [], [r_xs[par]])
            for b in range(nb):
                if g < 16:
                    dst_fn = lambda hlf, b=b: xTs[par][:, hlf * 8:(hlf + 1) * 8, b * 128:(b + 1) * 128]
                    rdst_fn = lambda hlf, b=b: r_xTs[par][b][hlf]
                else:
                    lb = (g - 16) * 4 + b
                    dst_fn = lambda hlf, lb=lb: xT[:, hlf * 8:(hlf + 1) * 8, lb * 128:(lb + 1) * 128]
                    rdst_fn = lambda hlf, lb=lb: r_xT[lb][hlf]
                transpose_block_bf16(xs[par][:, b, :], dst_fn, r_xs[par], rdst_fn)
            if g < 16:
                xsrc = xTs[par]
                t0 = 0
                rx = lambda b, hlf: r_xTs[par][b][hlf]
            else:
                xsrc = xT
                t0 = (g - 16) * 512
                rx = lambda b, hlf, g=g: r_xT[(g - 16) * 4 + b][hlf]
            for h in range(8):
                bk = next_mbank()
                for c in range(16):
                    mm(bank(bk)[:, 0:ntok], Wkb[:, c, h * 128:(h + 1) * 128], xsrc[:, c, t0:t0 + ntok], c == 0, c == 15,
                       [r_Wkb[h // 4]] + [rx(b, c // 8) for b in range(nb)], [rbank[bk]])
                cp(KTs[par][:, h, 0:ntok], bank(bk)[:, 0:ntok], [rbank[bk]], [r_KTs[par][h]])
            dma("sp", kt_all[:, :, g * 512:g * 512 + ntok].rearrange("h p n -> p h n"), KTs[par][:, :, 0:ntok],
                r_KTs[par], [r_ktall])
            for b in range(nb):
                for hlf in range(2):
                    bk = next_mbank()
                    for c in range(16):
                        mm(bank(bk), xsrc[:, c, t0 + b * 128:t0 + (b + 1) * 128], Wvb[:, c, hlf * 512:(hlf + 1) * 512], c == 0, c == 15,
                           [r_Wvb[hlf], rx(b, c // 8)], [rbank[bk]])
                    cp(Vs[par][:, b, hlf * 512:(hlf + 1) * 512], bank(bk), [rbank[bk]], [r_Vs[par][b][hlf]])
            dma("sp", v_all[g * 4:g * 4 + nb, :, :].rearrange("b p n -> p b n"), Vs[par][:, 0:nb, :],
                [r for bb in range(nb) for r in r_Vs[par][bb]], [r_vall])
        P.barrier(scratch)
        A.reset(xT_mark)

        QaT = A.alloc([128, 8, TOK], BF16)
        KaT = A.alloc([128, 2, NOWN], BF16)
        Va = A.alloc([128, 10, 2, 65], BF16)
        r_QaT = [P.Rs(2) for _ in range(8)]
        r_QbT = [P.Rs(2) for _ in range(8)]
        r_KaT = [P.Rs(3) for _ in range(2)]
        r_Va = P.Rs(10)
        r_Vaones = P.R()
        r_oaT = P.Rs(8)
        r_obT = [P.Rs(8) for _ in range(8)]
        proj_mark = A.mark()
        allx = [r_xT[b][hh] for b in range(10) for hh in range(2)]

        if stage >= 2:
            Wt = [A.alloc([128, 16, 512], BF16) for _ in range(2)]
            r_Wt = P.Rs(2)
            wi = [0]

            def load_w(col0, ncols=512):
                k = wi[0] % 2
                wi[0] += 1
                dma("pool", Wt[k][:, :, 0:ncols], w_in[:, col0:col0 + ncols].rearrange("(c p) n -> p c n", p=128), [], [r_Wt[k]])
                return Wt[k], r_Wt[k]

            for (col0, dst, rdst) in ((C_QA, QaT, r_QaT), (C_QB, QbT, r_QbT)):
                for pn in range(2):
                    W, rW = load_w(col0 + pn * 512)
                    for ii in range(4):
                        i = pn * 4 + ii
                        for hlf in range(2):
                            bk = next_mbank()
                            for c in range(16):
                                mm(bank(bk), W[:, c, ii * 128:(ii + 1) * 128], xT[:, c, OWN0[hlf]:OWN0[hlf] + 512],
                                   c == 0, c == 15, [rW] + allx, [rbank[bk]])
                            cp(dst[:, i, hlf * 512:(hlf + 1) * 512], bank(bk), [rbank[bk]], [rdst[i][hlf]])
            k = wi[0] % 2
            wi[0] += 1
            Wka = Wt[k][:, :, 0:256].rearrange("p c (g u d) -> p c g u d", g=2, u=2)
            for u in range(2):
                for g in range(2):
                    dma("pool", Wka[:, :, g, u, :], w_in[:, C_KA + g * 64:C_KA + (g + 1) * 64].rearrange("(c p) d -> p c d", p=128), [], [r_Wt[k]])
            Wva = Wt[k][:, :, 256:384]
            dma("pool", Wva, w_in[:, C_VA:C_VA + 128].rearrange("(c p) n -> p c n", p=128), [], [r_Wt[k]])
            for g in range(2):
                for pi, (t0, nt) in enumerate(((0, 512), (512, 512), (1024, 256))):
                    bk = next_mbank()
                    for c in range(16):
                        mm(bank(bk)[:, 0:nt], Wt[k][:, c, g * 128:(g + 1) * 128], xT[:, c, t0:t0 + nt], c == 0, c == 15,
                           [r_Wt[k]] + allx, [rbank[bk]])
                    cp(KaT[:, g, t0:t0 + nt], bank(bk)[:, 0:nt], [rbank[bk]], [r_KaT[g][pi]])
            P.op("pool", lambda e: e.memset(Va[:, :, :, 64:65], 1.0), writes=[r_Vaones])
            for b in range(10):
                bk = next_mbank()
                for c in range(16):
                    mm(bank(bk)[:, 0:128], xT[:, c, b * 128:(b + 1) * 128], Wva[:, c, :], c == 0, c == 15,
                       [r_Wt[k], r_xT[b][c // 8]], [rbank[bk]])
                cp(Va[:, b, :, 0:64], bank(bk)[:, 0:128].rearrange("p (g d) -> p g d", g=2), [rbank[bk]], [r_Va[b]])
        P.barrier(scratch)
        A.reset(proj_mark)

        if stage >= 3:
            Bsw = [A.alloc([128, 4, 4, 256], F32) for _ in range(2)]
            r_Bsw = P.Rs(2)
            tmpS = [A.alloc([128, 4, 128], F32) for _ in range(2)]
            r_tmpS = P.Rs(2)
            PTs = [A.alloc([128, 4, 128], BF16) for _ in range(4)]
            r_PTs = P.Rs(4)
            oa = [A.alloc([128, 1024], BF16) for _ in range(2)]
            r_oa = [P.Rs(4) for _ in range(2)]
            den = [A.alloc([128, 8], F32) for _ in range(2)]
            r_den = P.Rs(2)
            allKa = [r for g in range(2) for r in r_KaT[g]]
            it = 0
            for j in range(NB):
                bp = j % 2
                dma("sp", Bsw[bp].rearrange("p a b c -> p (a b c)"), bias_swa[1 if j == 0 else 0], [], [r_Bsw[bp]])
                for g in range(2):
                    for r in range(2):
                        heads = [2 * i + r for i in range(4 * g, 4 * g + 4)]
                        pts = []
                        for kap in range(2):
                            blk = LB(j) - 1 + kap
                            bk = next_mbank(4)
                            mm(bank(bk).rearrange("p (h q) -> p h q", h=4), KaT[r * 64:(r + 1) * 64, g, blk * 128:(blk + 1) * 128],
                               QaT[r * 64:(r + 1) * 64, 4 * g:4 * g + 4, j * 128:(j + 1) * 128], True, True,
                               allKa + [r_QaT[i][j // 4] for i in range(4 * g, 4 * g + 4)], [rbank[bk]])
                            tk = it % 2
                            pk = it % 4
                            it += 1
                            P.op("dve", lambda e, tk=tk, bk=bk, kap=kap, g=g, r=r, bp=bp: e.scalar_tensor_tensor(
                                tmpS[tk], bank(bk).rearrange("p (h q) -> p h q", h=4), 0.125,
                                Bsw[bp][:, g * 2 + r, :, kap * 128:(kap + 1) * 128], ALU.mult, ALU.add),
                                reads=[rbank[bk], r_Bsw[bp]], writes=[r_tmpS[tk]])
                            P.op("act", lambda e, tk=tk, pk=pk: e.activation(PTs[pk], tmpS[tk], AF.Exp),
                                 reads=[r_tmpS[tk]], writes=[r_PTs[pk]])
                            pts.append(pk)
                        ab = 4 + (j * 4 + g * 2 + r) % 2
                        acc = bank(ab).rearrange("p (h c) -> p h c", h=4)
                        for hh in range(4):
                            for kap in range(2):
                                blk = LB(j) - 1 + kap
                                mm(acc[:, hh, 0:65], PTs[pts[kap]][:, hh, :], Va[:, blk, g, :], kap == 0, kap == 1,
                                   [r_PTs[pts[kap]], r_Va[blk], r_Vaones], [rbank[ab]])
                        dp = (g * 2 + r) % 2
                        hsl = slice((g * 2 + r) * 4, (g * 2 + r) * 4 + 4)
                        P.op("dve", lambda e, dp=dp, acc=acc, hsl=hsl: e.tensor_tensor(den[dp][:, 0:4], acc[:, :, 64], esink[:, hsl], ALU.add),
                             reads=[rbank[ab], r_const], writes=[r_den[dp]])
                        P.op("dve", lambda e, dp=dp: e.reciprocal(den[dp][:, 4:8], den[dp][:, 0:4]),
                             reads=[r_den[dp]], writes=[r_den[dp]])
                        for hh in range(4):
                            h = heads[hh]
                            P.op("dve", lambda e, dp=dp, acc=acc, hh=hh, h=h, bp=bp: e.tensor_scalar_mul(
                                oa[bp][:, h * 64:(h + 1) * 64], acc[:, hh, 0:64], den[dp][:, 4 + hh:5 + hh]),
                                reads=[rbank[ab], r_den[dp]], writes=[r_oa[bp][g * 2 + r]])
                b = next_tbank()
                pb = bank(b).bitcast(BF16)
                for c in range(8):
                    P.op("pe", lambda e, pb=pb, c=c, bp=bp: e.transpose(pb[:, c * 128:(c + 1) * 128], oa[bp][:, c * 128:(c + 1) * 128], idb),
                         reads=r_oa[bp] + [r_idb], writes=[rbank[b]])
                cp(oaT[:, :, j * 128:(j + 1) * 128], pb.rearrange("p (c n) -> p c n", c=8), [rbank[b]], [r_oaT[j]])
        P.barrier(scratch)
        A.reset(xT_mark)

        if stage >= 4:
            KTh = [A.alloc([128, NTOKX], BF16) for _ in range(2)]
            Vh = [A.alloc([128, 74, 129], BF16) for _ in range(2)]
            r_KTh, r_Vh, r_Vhones = P.Rs(2), P.Rs(2), P.Rs(2)
            Bn = A.alloc([128, 5, 512], F32)
            r_Bn = P.R()
            Bf = A.alloc([128, 2, 64, 8], F32)
            r_Bf = P.R()
            tmpD = [A.alloc([128, 2, 512], F32) for _ in range(2)]
            r_tmpD = P.Rs(2)
            PT = [A.alloc([128, 2, 512], BF16) for _ in range(2)]
            r_PT = P.Rs(2)
            accS = A.alloc([128, 3, 512], F32)
            r_accS = P.Rs(3)
            o0 = [A.alloc([128, 128], F32) for _ in range(4)]
            o1 = [A.alloc([128, 128], F32) for _ in range(4)]
            obk = A.alloc([128, 4, 128], BF16)
            st4 = [A.alloc([128, 8], F32) for _ in range(4)]
            r_r0, r_r1, r_r1l, r_o0, r_o1, r_ss, r_ln, r_rstd, r_obk = [P.Rs(4) for _ in range(9)]
            dma("sp", Bf.rearrange("p a b c -> p (a b c)"), bias_far, [], [r_Bf])
            zt = A.alloc([128, 512], BF16)
            r_zt = P.R()
            P.op("pool", lambda e: e.memset(zt, 0.0), writes=[r_zt])
            for k in range(2):
                P.op("pool", lambda e, k=k: e.memset(Vh[k][:, :, 128:129], 1.0), writes=[r_Vhones[k]])
            def accap(a):
                return bank(4 + a // 3)[:, (a % 3) * 160:(a % 3) * 160 + 129]

            def accS_ap(a):
                return accS[:, a // 3, (a % 3) * 160:(a % 3) * 160 + 129]

            def make_stages(h, ch):
                def g0():
                    for qb in range(4):
                        s4, a0, a1 = st4[qb], accS_ap(qb), accS_ap(4 + qb)
                        P.op("dve", lambda e, s4=s4, a0=a0: e.reciprocal(s4[:, 0:1], a0[:, 128:129]),
                             reads=[r_accS[qb // 3]], writes=[r_r0[qb]])
                        P.op("dve", lambda e, s4=s4, a1=a1: e.reciprocal(s4[:, 1:2], a1[:, 128:129]),
                             reads=[r_accS[(4 + qb) // 3]], writes=[r_r1[qb]])

                def g1():
                    for qb in range(4):
                        s4, a0 = st4[qb], accS_ap(qb)
                        P.op("dve", lambda e, s4=s4: e.tensor_tensor(s4[:, 2:3], s4[:, 1:2], neglam, ALU.mult),
                             reads=[r_r1[qb], r_const], writes=[r_r1l[qb]])
                        P.op("dve", lambda e, s4=s4, a0=a0, qb=qb: e.tensor_scalar_mul(o0[qb], a0[:, 0:128], s4[:, 0:1]),
                             reads=[r_accS[qb // 3], r_r0[qb]], writes=[r_o0[qb]])

                def g2():
                    for qb in range(4):
                        s4, a1 = st4[qb], accS_ap(4 + qb)
                        P.op("dve", lambda e, s4=s4, a1=a1, qb=qb: e.scalar_tensor_tensor(o1[qb], a1[:, 0:128], s4[:, 2:3], o0[qb], ALU.mult, ALU.add),
                             reads=[r_accS[(4 + qb) // 3], r_r1l[qb], r_o0[qb]], writes=[r_o1[qb]])
                        P.op("dve", lambda e, s4=s4: e.memset(s4[:, 3:4], 0.0), writes=[r_ss[qb]])

                def g3():
                    for qb in range(4):
                        s4 = st4[qb]
                        P.op("act", lambda e, s4=s4, qb=qb: e.activation(o0[qb], o1[qb], AF.Square, accum_out=s4[:, 3:4]),
                             reads=[r_o1[qb]], writes=[r_ss[qb], r_o0[qb]])

                def g4():
                    for qb in range(4):
                        s4 = st4[qb]
                        P.op("act", lambda e, s4=s4: e.activation(s4[:, 4:5], s4[:, 3:4], AF.Ln, bias=epsc, scale=1.0 / 128.0),
                             reads=[r_ss[qb], r_const], writes=[r_ln[qb]])

                def g5():
                    for qb in range(4):
                        s4 = st4[qb]
                        P.op("act", lambda e, s4=s4: e.activation(s4[:, 5:6], s4[:, 4:5], AF.Exp, scale=-0.5),
                             reads=[r_ln[qb]], writes=[r_rstd[qb]])

                def g6():
                    for qb in range(4):
                        s4 = st4[qb]
                        P.op("dve", lambda e, s4=s4, qb=qb: e.scalar_tensor_tensor(obk[:, qb, :], o1[qb], s4[:, 5:6], wsub, ALU.mult, ALU.mult),
                             reads=[r_o1[qb], r_rstd[qb], r_const], writes=[r_obk[qb]])

                def g7():
                    pb = bank(7).bitcast(BF16)
                    for qb in range(4):
                        P.op("pe", lambda e, pb=pb, qb=qb: e.transpose(pb[:, qb * 128:(qb + 1) * 128], obk[:, qb, :], idb),
                             reads=[r_obk[qb], r_idb], writes=[rbank[7]])
                    cp(obT[:, h, ch * 512:(ch + 1) * 512], pb[:, 0:512], [rbank[7]], [r_obT[h][ch * 4 + qb] for qb in range(4)], eng="dve")

                return [g0, g1, g2, g3, g4, g5, g6, g7]

            sc = [pp[0][:], pp[1][:]]
            r_sc = P.Rs(2)
            r_acc = P.Rs(3)
            pending = []
            for h in range(8):
                hp = h % 2
                dma("sp", KTh[hp], kt_all[h], [r_ktall], [r_KTh[hp]])
                dma("sp", Vh[hp][:, :, 0:128], v_all[:, :, h * 128:(h + 1) * 128].rearrange("b p n -> p b n"), [r_vall], [r_Vh[hp]])
                for ch in range(2):
                    dma("sp", Bn.rearrange("p a b -> p (a b)"), bias_near[ch, h], [], [r_Bn])
                    tiles = [("far", kb) for kb in range(NFAR[ch])] + [("near", kap) for kap in range(5)]
                    nt = len(tiles)
                    qsl = slice(ch * 512, (ch + 1) * 512)

                    def emit_qk(t):
                        kind, idx = tiles[t]
                        sp_ = t % 2
                        tok0 = idx * 128 if kind == "far" else S + (ch * 5 + idx) * 128
                        for m in range(2):
                            mm(sc[sp_][:, m * 512:(m + 1) * 512], KTh[hp][m * 64:(m + 1) * 64, tok0:tok0 + 128],
                               QbT[m * 64:(m + 1) * 64, h, qsl], True, True,
                               [r_KTh[hp], r_QbT[h][ch]], [r_sc[sp_]])

                    emit_qk(0)
                    emit_qk(1)
                    for ab in range(3):
                        mm(bank(4 + ab), zt[:, 0:128], zt, True, False, [r_zt], [r_acc[ab]])
                    for t in range(nt):
                        kind, idx = tiles[t]
                        sp_ = t % 2
                        if kind == "far":
                            bias_ap = Bf[:, ch, idx, h:h + 1]
                            P.op("act", lambda e, sp_=sp_, bias_ap=bias_ap: e.activation(
                                PT[sp_].rearrange("p a b -> p (a b)"), sc[sp_], AF.Exp, bias=bias_ap, scale=0.125),
                                reads=[r_sc[sp_], r_Bf], writes=[r_PT[sp_]])
                        else:
                            for m in range(2):
                                bn_ap = Bn[:, idx, :]
                                P.op("dve", lambda e, sp_=sp_, bn_ap=bn_ap, m=m: e.scalar_tensor_tensor(
                                    tmpD[sp_][:, m, :], sc[sp_][:, m * 512:(m + 1) * 512], 0.125, bn_ap, ALU.mult, ALU.add),
                                    reads=[r_sc[sp_], r_Bn], writes=[r_tmpD[sp_]])
                            P.op("act", lambda e, sp_=sp_: e.activation(PT[sp_], tmpD[sp_], AF.Exp),
                                 reads=[r_tmpD[sp_]], writes=[r_PT[sp_]])
                        if t + 2 < nt:
                            emit_qk(t + 2)
                        vb = idx if kind == "far" else 64 + ch * 5 + idx
                        for m in range(2):
                            for qb in range(4):
                                a = m * 4 + qb
                                mm(accap(a), PT[sp_][:, m, qb * 128:(qb + 1) * 128], Vh[hp][:, vb, :], False, (t == nt - 1) and a in (2, 5, 7),
                                   [r_PT[sp_], r_Vh[hp], r_Vhones[hp]], [r_acc[a // 3]])
                        if pending and t >= 2:
                            pending.pop(0)()
                    while pending:
                        pending.pop(0)()
                    for ab in range(3):
                        cp(accS[:, ab, :], bank(4 + ab), [r_acc[ab]], [r_accS[ab]], eng="dve")
                    pending = make_stages(h, ch)
            while pending:
                pending.pop(0)()
        P.barrier(scratch)
        A.reset(xT_mark)

        y = None

        def pipeline(nitems, stages):
            ns = len(stages)
            for tau in range(nitems + ns - 1):
                for si, stg in enumerate(stages):
                    i = tau - si
                    if 0 <= i < nitems:
                        stg(i)

        def layer_norm_and_T(y, r_y, lnw, r_lnw, hT, r_hT, stt, final_out=None, r_out=None):
            r_st = [P.Rs(4) for _ in range(NB)]
            r_mv, r_rs, r_nm = P.Rs(NB), P.Rs(NB), P.Rs(NB)

            def s0(b):
                yb, s = y[:, b, :], stt[:, b, :]
                stats = s[:, 0:24].rearrange("p (k s) -> p k s", k=4)
                for k in range(4):
                    P.op("dve", lambda e, k=k, yb=yb, stats=stats: e.bn_stats(stats[:, k, :], yb[:, k * 512:(k + 1) * 512]),
                         reads=[r_y[b][k]], writes=[r_st[b][k]])
                P.op("dve", lambda e, s=s, stats=stats: e.bn_aggr(s[:, 24:26], stats), reads=r_st[b], writes=[r_mv[b]])

            def s1(b):
                s = stt[:, b, :]
                P.op("act", lambda e, s=s: e.activation(s[:, 27:28], s[:, 25:26], AF.Ln, bias=epsc), reads=[r_mv[b], r_const], writes=[r_rs[b]])
                P.op("act", lambda e, s=s: e.activation(s[:, 26:27], s[:, 27:28], AF.Exp, scale=-0.5), reads=[r_rs[b]], writes=[r_rs[b]])

            def s2(b):
                s = stt[:, b, :]
                P.op("dve", lambda e, s=s: e.scalar_tensor_tensor(s[:, 28:29], s[:, 24:25], -1.0, s[:, 26:27], ALU.mult, ALU.mult),
                     reads=[r_mv[b], r_rs[b]], writes=[r_nm[b]])

            def s3(b):
                yb, s = y[:, b, :], stt[:, b, :]
                P.op("act", lambda e, s=s, yb=yb: e.activation(yb, yb, AF.Identity, bias=s[:, 28:29], scale=s[:, 26:27]),
                     reads=[r_nm[b], r_rs[b]] + r_y[b], writes=r_y[b])

            def s4(b):
                yb = y[:, b, :]
                P.op("dve", lambda e, yb=yb: e.tensor_tensor(yb, yb, lnw[:, 0, :], ALU.mult), reads=[r_lnw] + r_y[b], writes=r_y[b])

            def s5(b):
                yb = y[:, b, :]
                P.op("pool", lambda e, yb=yb: e.tensor_tensor(yb, yb, lnw[:, 1, :], ALU.add), reads=[r_lnw] + r_y[b], writes=r_y[b])

            def s6(b):
                yb = y[:, b, :]
                if final_out is not None:
                    dma("sp", final_out[b * 128:(b + 1) * 128, :], yb, r_y[b], [r_out])
                    return
                for q4 in range(4):
                    bk = next_tbank()
                    for c in range(4):
                        cc = q4 * 4 + c
                        P.op("pe", lambda e, bk=bk, c=c, cc=cc, yb=yb: e.transpose(bank(bk)[:, c * 128:(c + 1) * 128], yb[:, cc * 128:(cc + 1) * 128], idf),
                             reads=r_y[b] + [r_idf], writes=[rbank[bk]])
                    cp(hT[:, q4 * 4:(q4 + 1) * 4, b * 128:(b + 1) * 128], bank(bk).rearrange("p (c n) -> p c n", c=4),
                       [rbank[bk]], [r_hT[b][q4]])

            pipeline(NB, [s0, s1, s2, s3, s4, s5, s6])

        if stage >= 5:
            A.reset(117 * 1024)
            mixT = A.alloc([128, 16, TOK], BF16)
            r_mixT = [P.Rs(2) for _ in range(16)]
            e_mark = A.mark()
            Wg = [A.alloc([128, 16, 512], BF16) for _ in range(2)]
            Wab = [A.alloc([128, 8, 512], BF16) for _ in range(2)]
            r_Wg, r_Wab = P.Rs(2), P.Rs(2)
            sg = [A.alloc([128, 512], F32) for _ in range(2)]
            r_sg = P.Rs(2)
            m1 = [A.alloc([128, 512], F32) for _ in range(2)]
            r_m1 = P.Rs(2)
            all_oaT = r_oaT
            all_obT = [r for hh in range(8) for r in r_obT[hh]]
            gi = 0
            for jj in range(8):
                k = jj % 2
                dma("pool", Wg[k][:, :, 0:256], w_in[:, C_GA + jj * 256:C_GA + (jj + 1) * 256].rearrange("(c p) n -> p c n", p=128), [], [r_Wg[k]])
                dma("pool", Wg[k][:, :, 256:512], w_in[:, C_GB + jj * 256:C_GB + (jj + 1) * 256].rearrange("(c p) n -> p c n", p=128), [], [r_Wg[k]])
                dma("pool", Wab[k][:, :, 0:256], w_a[:, jj * 256:(jj + 1) * 256].rearrange("(c p) n -> p c n", p=128), [], [r_Wab[k]])
                dma("pool", Wab[k][:, :, 256:512], w_b[:, jj * 256:(jj + 1) * 256].rearrange("(c p) n -> p c n", p=128), [], [r_Wab[k]])
                for sub in range(2):
                    j = jj * 2 + sub
                    for hlf in range(2):
                        tsl = slice(hlf * 512, (hlf + 1) * 512)
                        for br in range(2):
                            gk = gi % 2
                            gi += 1
                            bg = next_mbank()
                            for c in range(16):
                                mm(bank(bg), Wg[k][:, c, br * 256 + sub * 128:br * 256 + (sub + 1) * 128],
                                   xT[:, c, OWN0[hlf]:OWN0[hlf] + 512], c == 0, c == 15, [r_Wg[k]] + allx, [rbank[bg]])
                            P.op("act", lambda e, gk=gk, bg=bg: e.activation(sg[gk], bank(bg), AF.Sigmoid),
                                 reads=[rbank[bg]], writes=[r_sg[gk]])
                            bb = next_mbank()
                            srcT = oaT if br == 0 else obT
                            rsrc = all_oaT if br == 0 else all_obT
                            for c in range(8):
                                mm(bank(bb), Wab[k][:, c, br * 256 + sub * 128:br * 256 + (sub + 1) * 128], srcT[:, c, tsl],
                                   c == 0, c == 7, [r_Wab[k]] + rsrc, [rbank[bb]])
                            if br == 0:
                                mk = (gi // 2) % 2
                                P.op("dve", lambda e, mk=mk, bb=bb, gk=gk: e.tensor_tensor(m1[mk], bank(bb), sg[gk], ALU.mult),
                                     reads=[rbank[bb], r_sg[gk]], writes=[r_m1[mk]])
                                mk_a = mk
                            else:
                                P.op("dve", lambda e, bb=bb, gk=gk: e.tensor_tensor(sg[gk], bank(bb), sg[gk], ALU.mult),
                                     reads=[rbank[bb], r_sg[gk]], writes=[r_sg[gk]])
                                P.op("dve", lambda e, gk=gk, mk_a=mk_a, j=j, tsl=tsl: e.tensor_tensor(mixT[:, j, tsl], m1[mk_a], sg[gk], ALU.add),
                                     reads=[r_m1[mk_a], r_sg[gk]], writes=[r_mixT[j][hlf]])
            P.barrier(scratch)
            A.reset(M0)
            y = A.alloc([128, NB, D], F32)
            r_y = [P.Rs(4) for _ in range(NB)]
            lnw = A.alloc([128, 2, D], F32)
            r_lnw = P.R()
            stt = A.alloc([128, NB, 32], F32)
            hT = A.alloc([128, 16, TOK], BF16)
            r_hT = [P.Rs(4) for _ in range(NB)]
            h_mark = A.mark()
            assert h_mark <= 117 * 1024
            A.reset(e_mark)
            for b in range(NB):
                dma("sp", y[:, b, :], x_own[LB(b) * 128:(LB(b) + 1) * 128, :], [], r_y[b])
            dma("sp", lnw, lnp[:, 0:2, :], [], [r_lnw])
            Wo = [A.alloc([128, 16, 512], BF16) for _ in range(2)]
            r_Wo = P.Rs(2)
            all_mix = [r for j in range(16) for r in r_mixT[j]]
            for pn in range(4):
                k = pn % 2
                dma("pool", Wo[k], w_o[:, pn * 512:(pn + 1) * 512].rearrange("(c p) n -> p c n", p=128), [], [r_Wo[k]])
                for b in range(NB):
                    bk = next_mbank()
                    for c in range(16):
                        mm(bank(bk), mixT[:, c, b * 128:(b + 1) * 128], Wo[k][:, c, :], c == 0, c == 15,
                           [r_Wo[k], r_mixT[c][b // 4]], [rbank[bk]])
                    ysl = y[:, b, pn * 512:(pn + 1) * 512]
                    P.op("dve", lambda e, ysl=ysl, bk=bk: e.scalar_tensor_tensor(ysl, ysl, ALPHA, bank(bk), ALU.mult, ALU.add),
                         reads=[rbank[bk], r_y[b][pn]], writes=[r_y[b][pn]])
            layer_norm_and_T(y, r_y, lnw, r_lnw, hT, r_hT, stt)
            all_hT = [r for b in range(NB) for r in r_hT[b]]

        if stage >= 6:
            P.barrier(scratch)
            A.reset(h_mark)
            memb = A.alloc([128, 2, D], BF16)
            memT = A.alloc([128, 16, 256], BF16)
            r_memb = P.R()
            r_memT = [P.Rs(2) for _ in range(2)]
            Wflat = [A.alloc([128, 8192], BF16) for _ in range(2)]
            Wm = [w.rearrange("p (c n) -> p c n", c=16) for w in Wflat]
            r_Wm = P.Rs(2)
            KcT = A.alloc([128, 4, 256], BF16)
            Vc = A.alloc([128, 2, 4, 129], BF16)
            r_KcT = P.Rs(4)
            r_Vc = P.Rs(2)
            r_Vcones = P.R()
            qcT = A.alloc([128, 4, TOK], BF16)
            r_qcT = [P.Rs(2) for _ in range(4)]
            oc = A.alloc([128, NB, 512], BF16)
            r_oc = [P.Rs(4) for _ in range(NB)]
            ocT = A.alloc([128, 4, TOK], BF16)
            r_ocT = P.Rs(NB)
            Wco = Wflat[1].rearrange("p (c n) -> p c n", c=4)
            r_Wco = r_Wm[1]
            PTc = [A.alloc([128, 512], BF16) for _ in range(4)]
            r_PTc = P.Rs(4)
            rc = [A.alloc([128, 4], F32) for _ in range(2)]
            r_rc = P.Rs(2)
            dma("pool", memb, mem.rearrange("(b p) d -> p b d", p=128), [], [r_memb])
            dma("pool", Wm[0], w_mkv[:, 0:512].rearrange("(c p) n -> p c n", p=128), [], [r_Wm[0]])
            dma("pool", Wm[1], w_mkv[:, 512:1024].rearrange("(c p) n -> p c n", p=128), [], [r_Wm[1]])
            dma("sp", lnw, lnp[:, 2:4, :], [], [r_lnw])
            for mbk in range(2):
                transpose_block_bf16(memb[:, mbk, :], lambda hlf, mbk=mbk: memT[:, hlf * 8:(hlf + 1) * 8, mbk * 128:(mbk + 1) * 128],
                                     r_memb, lambda hlf, mbk=mbk: r_memT[mbk][hlf])
            all_memT = [r for a in r_memT for r in a]
            for h in range(4):
                bk = next_mbank()
                for c in range(16):
                    mm(bank(bk)[:, 0:256], Wm[0][:, c, h * 128:(h + 1) * 128], memT[:, c, :], c == 0, c == 15,
                       [r_Wm[0]] + all_memT, [rbank[bk]])
                cp(KcT[:, h, :], bank(bk)[:, 0:256], [rbank[bk]], [r_KcT[h]])
            P.op("pool", lambda e: e.memset(Vc[:, :, :, 128:129], 1.0), writes=[r_Vcones])
            for mbk in range(2):
                bk = next_mbank()
                for c in range(16):
                    mm(bank(bk), memT[:, c, mbk * 128:(mbk + 1) * 128], Wm[1][:, c, :], c == 0, c == 15,
                       [r_Wm[1], r_memT[mbk][c // 8]], [rbank[bk]])
                cp(Vc[:, mbk, :, 0:128], bank(bk).rearrange("p (h d) -> p h d", h=4), [rbank[bk]], [r_Vc[mbk]])
            Wq = Wm[0]
            r_Wq = r_Wm[0]
            dma("pool", Wq, w_cq.rearrange("(c p) n -> p c n", p=128), [], [r_Wq])
            dma("pool", Wco, w_co.rearrange("(c p) n -> p c n", p=128), [], [r_Wco])
            for h in range(4):
                for hlf in range(2):
                    bk = next_mbank()
                    for c in range(16):
                        mm(bank(bk), Wq[:, c, h * 128:(h + 1) * 128], hT[:, c, hlf * 512:(hlf + 1) * 512], c == 0, c == 15,
                           [r_Wq] + all_hT, [rbank[bk]])
                    cp(qcT[:, h, hlf * 512:(hlf + 1) * 512], bank(bk), [rbank[bk]], [r_qcT[h][hlf]])
            ci = 0
            CSC = 128.0 ** -0.5
            for h in range(4):
                for hlf in range(2):
                    pts = []
                    for mbk in range(2):
                        bk = next_mbank(4)
                        mm(bank(bk), KcT[:, h, mbk * 128:(mbk + 1) * 128], qcT[:, h, hlf * 512:(hlf + 1) * 512], True, True,
                           [r_KcT[h], r_qcT[h][hlf]], [rbank[bk]])
                        pk = ci % 4
                        ci += 1
                        P.op("act", lambda e, pk=pk, bk=bk: e.activation(PTc[pk], bank(bk), AF.Exp, scale=CSC),
                             reads=[rbank[bk]], writes=[r_PTc[pk]])
                        pts.append(pk)
                    ab0 = 4 + 2 * ((h * 2 + hlf) % 2)
                    def cacc(qb, ab0=ab0):
                        return bank(ab0 + qb // 2)[:, (qb % 2) * 160:(qb % 2) * 160 + 129]
                    for qb in range(4):
                        for mbk in range(2):
                            mm(cacc(qb), PTc[pts[mbk]][:, qb * 128:(qb + 1) * 128],
                               Vc[:, mbk, h, :], mbk == 0, mbk == 1, [r_PTc[pts[mbk]], r_Vc[mbk], r_Vcones], [rbank[ab0 + qb // 2]])
                    rk = (h * 2 + hlf) % 2
                    for qb in range(4):
                        a_ap = cacc(qb)
                        blk = hlf * 4 + qb
                        P.op("dve", lambda e, rk=rk, qb=qb, a_ap=a_ap: e.reciprocal(rc[rk][:, qb:qb + 1], a_ap[:, 128:129]),
                             reads=[rbank[ab0 + qb // 2]], writes=[r_rc[rk]])
                        P.op("dve", lambda e, rk=rk, qb=qb, a_ap=a_ap, blk=blk, h=h: e.tensor_scalar_mul(
                            oc[:, blk, h * 128:(h + 1) * 128], a_ap[:, 0:128], rc[rk][:, qb:qb + 1]),
                            reads=[rbank[ab0 + qb // 2], r_rc[rk]], writes=[r_oc[blk][h]])
            for b in range(NB):
                bk = next_tbank()
                pb = bank(bk).bitcast(BF16)
                for c in range(4):
                    P.op("pe", lambda e, pb=pb, c=c, b=b: e.transpose(pb[:, c * 128:(c + 1) * 128], oc[:, b, c * 128:(c + 1) * 128], idb),
                         reads=r_oc[b] + [r_idb], writes=[rbank[bk]])
                cp(ocT[:, :, b * 128:(b + 1) * 128], pb[:, 0:512].rearrange("p (c n) -> p c n", c=4), [rbank[bk]], [r_ocT[b]])
            for b in range(NB):
                for pn in range(4):
                    bk = next_mbank()
                    for c in range(4):
                        mm(bank(bk), ocT[:, c, b * 128:(b + 1) * 128], Wco[:, c, pn * 512:(pn + 1) * 512], c == 0, c == 3,
                           [r_Wco, r_ocT[b]], [rbank[bk]])
                    ysl = y[:, b, pn * 512:(pn + 1) * 512]
                    P.op("dve", lambda e, ysl=ysl, bk=bk: e.scalar_tensor_tensor(ysl, ysl, ALPHA, bank(bk), ALU.mult, ALU.add),
                         reads=[rbank[bk], r_y[b][pn]], writes=[r_y[b][pn]])
            layer_norm_and_T(y, r_y, lnw, r_lnw, hT, r_hT, stt)
            P.barrier(scratch)
            A.reset(h_mark)

        if stage >= 7:
            dma("sp", lnw, lnp[:, 4:6, :], [], [r_lnw])
            Wgu = [A.alloc([128, 16, 512], BF16) for _ in range(2)]
            Wd = [A.alloc([128, 2, D], BF16) for _ in range(2)]
            r_Wgu, r_Wd = P.Rs(2), P.Rs(2)
            Aff = [A.alloc([128, 2, TOK], BF16) for _ in range(2)]
            r_Aff = [[P.Rs(2) for _ in range(2)] for _ in range(2)]
            sgf = [A.alloc([128, 512], F32) for _ in range(2)]
            r_sgf = P.Rs(2)
            fi = 0
            NSC = DFF // 256
            for s in range(NSC):
                k = s % 2
                dma("pool", Wgu[k][:, :, 0:256], w_gu[:, s * 256:(s + 1) * 256].rearrange("(c p) n -> p c n", p=128), [], [r_Wgu[k]])
                dma("pool", Wgu[k][:, :, 256:512], w_gu[:, DFF + s * 256:DFF + (s + 1) * 256].rearrange("(c p) n -> p c n", p=128), [], [r_Wgu[k]])
                dma("pool", Wd[k], w_dn[s * 256:(s + 1) * 256, :].rearrange("(c p) n -> p c n", p=128), [], [r_Wd[k]])
                for sub in range(2):
                    for hlf in range(2):
                        bg = next_mbank()
                        for c in range(16):
                            mm(bank(bg), Wgu[k][:, c, sub * 128:(sub + 1) * 128], hT[:, c, hlf * 512:(hlf + 1) * 512], c == 0, c == 15,
                               [r_Wgu[k]] + all_hT, [rbank[bg]])
                        bu = next_mbank()
                        for c in range(16):
                            mm(bank(bu), Wgu[k][:, c, 256 + sub * 128:256 + (sub + 1) * 128], hT[:, c, hlf * 512:(hlf + 1) * 512], c == 0, c == 15,
                               [r_Wgu[k]] + all_hT, [rbank[bu]])
                        fk = fi % 2
                        fi += 1
                        P.op("act", lambda e, fk=fk, bg=bg: e.activation(sgf[fk], bank(bg), AF.Silu), reads=[rbank[bg]], writes=[r_sgf[fk]])
                        P.op("dve", lambda e, fk=fk, bu=bu, k=k, sub=sub, hlf=hlf: e.tensor_tensor(
                            Aff[k][:, sub, hlf * 512:(hlf + 1) * 512], bank(bu), sgf[fk], ALU.mult),
                            reads=[rbank[bu], r_sgf[fk]], writes=[r_Aff[k][sub][hlf]])
                for b in range(NB):
                    for pn in range(4):
                        bk = next_mbank()
                        for sub in range(2):
                            mm(bank(bk), Aff[k][:, sub, b * 128:(b + 1) * 128], Wd[k][:, sub, pn * 512:(pn + 1) * 512], sub == 0, sub == 1,
                               [r_Wd[k], r_Aff[k][sub][b // 4]], [rbank[bk]])
                        ysl = y[:, b, pn * 512:(pn + 1) * 512]
                        if s == 0:
                            P.op("dve", lambda e, ysl=ysl, bk=bk: e.scalar_tensor_tensor(ysl, ysl, ALPHA, bank(bk), ALU.mult, ALU.add),
                                 reads=[rbank[bk], r_y[b][pn]], writes=[r_y[b][pn]])
                        else:
                            P.op("dve", lambda e, ysl=ysl, bk=bk: e.tensor_tensor(ysl, ysl, bank(bk), ALU.add),
                                 reads=[rbank[bk], r_y[b][pn]], writes=[r_y[b][pn]])
            r_out = P.R()
            layer_norm_and_T(y, r_y, lnw, r_lnw, None, None, stt, final_out=out, r_out=r_out)
            P.final_wait(r_out)

        if dbg is not None:
            P.barrier(scratch)
            r_dbg = P.R()
            if stage == 3:
                stg = A.alloc([128, 8, TOK], F32)
                P.op("dve", lambda e: e.tensor_copy(stg, oaT), reads=r_oaT, writes=[r_dbg])
                dma("sp", dbg_out.rearrange("(c p) t -> p c t", p=128), stg, [r_dbg], [r_dbg])
            elif stage == 4:
                stg = A.alloc([128, 8, TOK], F32)
                P.op("dve", lambda e: e.tensor_copy(stg, obT), reads=[r for hh in range(8) for r in r_obT[hh]], writes=[r_dbg])
                dma("sp", dbg_out.rearrange("(c p) t -> p c t", p=128), stg, [r_dbg], [r_dbg])
            elif stage in (5, 6):
                for b in range(NB):
                    dma("sp", dbg_out[b * 128:(b + 1) * 128, :], y[:, b, :], r_y[b], [r_dbg])
            P.final_wait(r_dbg)
        P.emit()
    return nc


def _rel_bucket(dist):
    n = np.maximum(dist, 0)
    exact = 16
    logv = np.log(np.maximum(n, 1).astype(np.float32) / np.float32(exact)) / np.float32(math.log(128 / 16))
    large = exact + (logv * np.float32(32 - exact)).astype(np.int32)
    large = np.minimum(large, 31)
    return np.where(n < exact, n, large)


def _bias_tables(table, core):
    k = np.arange(128)[:, None]
    q = np.arange(128)[None, :]
    sw = np.full((2, 128, 16, 2, 128), NEG, np.float32)
    for kap in range(2):
        dist = (128 + q - k) if kap == 0 else (q - k)
        ok = (dist >= 0) & (dist < 128)
        bk = _rel_bucket(dist)
        for h in range(16):
            vals = np.where(ok, table[bk, h], np.float32(NEG)).astype(np.float32)
            sw[0, :, h, kap, :] = vals
            if kap == 1:
                sw[1, :, h, kap, :] = vals
    if core != 0:
        sw[1] = sw[0]
    perm = [8 * g + 2 * i4 + r for g in range(2) for r in range(2) for i4 in range(4)]
    sw = sw[:, :, perm]
    q5 = np.arange(512)[None, :]
    near = np.full((2, 8, 128, 5, 512), NEG, np.float32)
    for kap in range(5):
        dist = q5 - 128 * (kap - 1) - k
        ok = dist >= 0
        bk = _rel_bucket(dist)
        for h in range(8):
            vals = np.where(ok, table[bk, 16 + h], np.float32(NEG)).astype(np.float32)
            near[:, h, :, kap, :] = vals[None]
    if core == 0:
        near[0, :, :, 0, :] = NEG
    far = np.full((128, 2, 64, 8), NEG, np.float32)
    for ch in range(2):
        g0 = 4 * (core if ch == 0 else 15 - core)
        for kb in range(64):
            if kb <= g0 - 2:
                far[:, ch, kb, :] = table[31, 16:24][None, :]
    return sw.reshape(2, 128, -1), near.reshape(2, 8, 128, -1), far.reshape(128, -1)


_NC_CACHE = {}


def kernel(x, mem, rel_bias_table, w_in, sinks, lambda_q1, lambda_k1, lambda_q2, lambda_k2,
           subln_w, w_branch_a, w_branch_b, w_o, ln1_g, ln1_b, w_cq, w_mem_kv, w_co,
           ln2_g, ln2_b, w_gate_up, w_down, ln3_g, ln3_b, _stage=99, _dbg=None):
    f = lambda a: np.ascontiguousarray(np.asarray(a, dtype=np.float32))
    x2 = f(x).reshape(S, D)
    table = f(rel_bias_table)
    lnp = np.stack([f(ln1_g)[0], f(ln1_b)[0], f(ln2_g)[0], f(ln2_b)[0], f(ln3_g)[0], f(ln3_b)[0]], 0)
    lnp = np.ascontiguousarray(np.broadcast_to(lnp[None], (128, 6, D)))
    perm = [8 * g + 2 * i4 + r for g in range(2) for r in range(2) for i4 in range(4)]
    small = np.concatenate([f(sinks)[0][perm], f(lambda_q1)[0], f(lambda_k1)[0], f(lambda_q2)[0], f(lambda_k2)[0], f(subln_w)[0]])
    small = np.ascontiguousarray(np.broadcast_to(small[None], (128, small.shape[0])))
    shared = {
        "x_all": x2, "mem": f(mem)[0], "w_in": f(w_in)[0], "w_branch_a": f(w_branch_a)[0], "w_branch_b": f(w_branch_b)[0],
        "w_o": f(w_o)[0], "w_cq": f(w_cq)[0], "w_mem_kv": f(w_mem_kv)[0], "w_co": f(w_co)[0],
        "w_gate_up": f(w_gate_up)[0], "w_down": f(w_down)[0], "lnp": lnp, "smallp": small,
        "ident": np.eye(128, dtype=np.float32),
    }
    in_maps = []
    for c in range(NCORES):
        xo = np.zeros((NOWN, D), np.float32)
        for ch, gc in enumerate((c, 15 - c)):
            r0 = ch * 640
            if gc > 0:
                xo[r0:r0 + 128] = x2[gc * 512 - 128:gc * 512]
            xo[r0 + 128:r0 + 640] = x2[gc * 512:(gc + 1) * 512]
        sw, near, far = _bias_tables(table, c)
        m = dict(shared)
        m.update({"x_own": xo, "bias_swa": sw, "bias_near": near, "bias_far": far})
        in_maps.append(m)
    key = (_stage, _dbg)
    if key not in _NC_CACHE:
        _NC_CACHE[key] = build(_stage, _dbg)
    nc = _NC_CACHE[key]
    res = run_bass_kernel_spmd(nc, in_maps, core_ids=list(range(NCORES)))
    if _dbg is not None:
        return np.concatenate([r["dbg"] for r in res.results], axis=0)
    o = np.empty((S, D), np.float32)
    for c in range(NCORES):
        oc = res.results[c]["out"]
        o[c * 512:(c + 1) * 512] = oc[0:512]
        o[(15 - c) * 512:(16 - c) * 512] = oc[512:1024]
    return o.reshape(1, S, D)
```

```python
import math
import contextlib
import numpy as np
import concourse.bass as bass
import concourse.mybir as mybir
from concourse.bass_utils import run_bass_kernel_spmd

F32 = mybir.dt.float32
BF16 = mybir.dt.bfloat16
AF = mybir.ActivationFunctionType
ALU = mybir.AluOpType
AX = mybir.AxisListType

NCORES = 8
S = 8192
D = 2048
TOK = S // NCORES
NB = TOK // 128
DFF = 5632
NEG = -30000.0
ALPHA = 2.0 ** 0.25
LAMBDA_INIT = 0.8 - 0.6 * math.exp(0.0)
EPS = 1e-5
C_QA, C_KA, C_VA, C_QB, C_KB, C_VB, C_GA, C_GB = 0, 1024, 1152, 1280, 2304, 3328, 4352, 6400
NOWN = 1280
NTOKX = S + NOWN
OWN0 = (128, 768)
NFAR = (27, 59)


def LB(j):
    return j + 1 if j < 4 else j + 2


class Region:
    __slots__ = ("name", "last_w", "readers")

    def __init__(self, name):
        self.name = name
        self.last_w = None
        self.readers = []


class Op:
    __slots__ = ("idx", "eng", "fn", "deps", "dma", "token", "signal", "prewait")

    def __init__(self, idx, eng, fn, dma):
        self.idx = idx
        self.eng = eng
        self.fn = fn
        self.dma = dma
        self.deps = set()
        self.token = None
        self.signal = False
        self.prewait = None


class Prog:
    ENGS = ("pe", "act", "dve", "pool", "sp")
    NDMA_SEM = 8

    def __init__(self, nc, same_engine_sync=True):
        self.nc = nc
        self.ops = []
        self.same_engine_sync = same_engine_sync
        self.finals = []
        self.barrier_idx = None
        self.last_on = {}
        self.dma_since = []

    def R(self, name=""):
        return Region(name)

    def Rs(self, n, name=""):
        return [Region(name) for _ in range(n)]

    def op(self, eng, fn, reads=(), writes=(), dma=False):
        o = Op(len(self.ops), eng, fn, dma)
        for r in reads:
            if r.last_w is not None:
                o.deps.add(r.last_w)
        for w in writes:
            if w.last_w is not None:
                o.deps.add(w.last_w)
            for rd in w.readers:
                o.deps.add(rd)
        for r in reads:
            r.readers.append(o.idx)
        for w in writes:
            w.last_w = o.idx
            w.readers = []
        if self.barrier_idx is not None:
            o.deps.add(self.barrier_idx)
        o.deps.discard(o.idx)
        self.ops.append(o)
        if dma:
            self.dma_since.append(o.idx)
        else:
            self.last_on[eng] = o.idx
        return o

    def barrier(self, scratch):
        o = Op(len(self.ops), "dve", lambda e: e.memset(scratch, 0.0), False)
        for e, i in self.last_on.items():
            o.deps.add(i)
        for i in self.dma_since:
            o.deps.add(i)
        self.dma_since = []
        if self.barrier_idx is not None:
            o.deps.add(self.barrier_idx)
        self.ops.append(o)
        self.last_on["dve"] = o.idx
        self.barrier_idx = o.idx
        return o

    def final_wait(self, region):
        self.finals.append(region.last_w)

    def _needs_sync(self, dep, ename):
        if dep.eng == ename and not dep.dma:
            if ename == "pe" or not self.same_engine_sync:
                return False
        return True

    def emit(self):
        nc = self.nc
        ops = self.ops
        for o in ops:
            for d in o.deps:
                dep = ops[d]
                if self._needs_sync(dep, o.eng):
                    dep.signal = True
        for f in self.finals:
            ops[f].signal = True
        for o in ops:
            if o.dma:
                o.signal = True
        with contextlib.ExitStack() as st:
            esem = {e: st.enter_context(nc.semaphore(f"s_{e}")) for e in ("pe", "act", "dve", "pool")}
            dsem = {q: [st.enter_context(nc.semaphore(f"d_{q}{k}")) for k in range(self.NDMA_SEM)]
                    for q in ("sp", "act", "pool")}
            cnt = {e: 0 for e in esem}
            dcnt = {q: 0 for q in dsem}
            for o in ops:
                if o.dma:
                    i = dcnt[o.eng]
                    dcnt[o.eng] += 1
                    k = i % self.NDMA_SEM
                    m = i // self.NDMA_SEM
                    o.token = (dsem[o.eng][k], 16 * (m + 1))
                    if m > 0:
                        o.prewait = (dsem[o.eng][k], 16 * m)
                elif o.signal:
                    cnt[o.eng] += 1
                    o.token = (esem[o.eng], cnt[o.eng])
            self.sem_counts = dict(cnt)
            block = st.enter_context(nc.Block())
            engobj = {"pe": "tensor", "act": "scalar", "dve": "vector", "pool": "gpsimd", "sp": "sync"}

            def run(ename, eng):
                seen = {}

                def wait(tok):
                    sem, v = tok
                    if seen.get(sem.num, 0) >= v:
                        return
                    eng.wait_ge(sem, v)
                    seen[sem.num] = v

                for o in ops:
                    if o.eng != ename:
                        continue
                    for d in sorted(o.deps, reverse=True):
                        dep = ops[d]
                        if not self._needs_sync(dep, ename):
                            continue
                        wait(dep.token)
                    if o.prewait is not None:
                        wait(o.prewait)
                    ins = o.fn(eng)
                    if o.token is not None:
                        sem, v = o.token
                        ins.then_inc(sem, 16 if o.dma else 1)
                if ename == "sp":
                    for f in self.finals:
                        wait(ops[f].token)

            for ename in self.ENGS:
                getattr(block, engobj[ename])(lambda eng, _e=ename: run(_e, eng))


class Arena:
    def __init__(self, t, nbytes):
        self.t = t
        self.cap = nbytes
        self.off = 0

    def alloc(self, shape, dtype):
        esz = 2 if dtype == BF16 else 4
        n = int(np.prod(shape[1:])) * esz
        off = (self.off + 63) // 64 * 64
        assert off + n <= self.cap, f"SBUF arena overflow: {off + n} > {self.cap}"
        self.off = off + n
        ap = self.t[:, off // 2:(off + n) // 2]
        if dtype != BF16:
            ap = ap.bitcast(dtype)
        if len(shape) == 3:
            ap = ap.rearrange("p (a b) -> p a b", a=shape[1])
        elif len(shape) == 4:
            ap = ap.rearrange("p (a b c) -> p a b c", a=shape[1], b=shape[2])
        elif len(shape) == 5:
            ap = ap.rearrange("p (a b c d) -> p a b c d", a=shape[1], b=shape[2], c=shape[3])
        return ap

    def mark(self):
        return self.off

    def reset(self, m):
        self.off = m


def build(stage=99, dbg=None):
    nc = bass.Bass("TRN2", target_bir_lowering=False)

    def din(name, shape, dt=F32):
        return nc.dram_tensor(name, shape, dt, kind="ExternalInput").ap()

    x_all = din("x_all", [S, D])
    x_own = din("x_own", [NOWN, D])
    mem = din("mem", [256, D])
    w_in = din("w_in", [D, 8448])
    w_a = din("w_branch_a", [1024, D])
    w_b = din("w_branch_b", [1024, D])
    w_o = din("w_o", [D, D])
    w_cq = din("w_cq", [D, 512])
    w_mkv = din("w_mem_kv", [D, 1024])
    w_co = din("w_co", [512, D])
    w_gu = din("w_gate_up", [D, 2 * DFF])
    w_dn = din("w_down", [DFF, D])
    lnp = din("lnp", [128, 6, D])
    smallp = din("smallp", [128, 16 + 256 + 128])
    ident_d = din("ident", [128, 128])
    bias_far = din("bias_far", [128, 2 * 64 * 8])
    bias_near = din("bias_near", [2, 8, 128, 5 * 512])
    bias_swa = din("bias_swa", [2, 128, 16 * 2 * 128])
    out = nc.dram_tensor("out", [TOK, D], F32, kind="ExternalOutput").ap()
    kt_all = nc.dram_tensor("kt_all", [8, 128, NTOKX], BF16).ap()
    v_all = nc.dram_tensor("v_all", [74, 128, 1024], BF16).ap()
    dbg_out = None
    if dbg is not None:
        dbg_out = nc.dram_tensor("dbg", [TOK, dbg], F32, kind="ExternalOutput").ap()

    P = Prog(nc)
    with contextlib.ExitStack() as st:
        ARENA_BYTES = 207 * 1024
        arena_t = st.enter_context(nc.sbuf_tensor("arena", [128, ARENA_BYTES // 2], BF16))
        A = Arena(arena_t, ARENA_BYTES)
        pp = [st.enter_context(nc.psum_tensor(f"pp{i}", [128, 1024], F32)) for i in range(4)]

        def bank(b):
            return pp[b // 2][:, (b % 2) * 512:(b % 2 + 1) * 512]

        rbank = P.Rs(8, "bank")
        cpctr = [0]

        def cp(out_ap, in_ap, reads, writes, eng=None):
            if eng is None:
                eng = ("dve", "act")[cpctr[0] % 2]
                cpctr[0] += 1
            if eng == "act":
                return P.op("act", lambda e: e.activation(out_ap, in_ap, AF.Copy), reads=reads, writes=writes)
            if eng == "pool":
                return P.op("pool", lambda e: e.tensor_copy(out_ap, in_ap), reads=reads, writes=writes)
            return P.op("dve", lambda e: e.tensor_copy(out_ap, in_ap), reads=reads, writes=writes)

        def mm(out_ap, lhsT, rhs, start, stop, reads, writes):
            return P.op("pe", lambda e: e.matmul(out_ap, lhsT, rhs, start=start, stop=stop),
                        reads=reads, writes=writes)

        def dma(q, out_ap, in_ap, reads, writes):
            return P.op(q, lambda e: e.dma_start(out=out_ap, in_=in_ap), reads=reads, writes=writes, dma=True)

        idb = A.alloc([128, 128], BF16)
        idf = A.alloc([128, 128], F32)
        small = A.alloc([128, 400], F32)
        scratch = A.alloc([128, 16], F32)
        esink = A.alloc([128, 16], F32)
        neglam = A.alloc([128, 1], F32)
        wsub = A.alloc([128, 128], F32)
        lam_t = A.alloc([128, 8], F32)
        epsc = A.alloc([128, 1], F32)
        r_idb, r_idf, r_small, r_const = P.R(), P.R(), P.R(), P.R()
        dma("pool", idb, ident_d, [], [r_idb])
        dma("sp", idf, ident_d, [], [r_idf])
        dma("sp", small, smallp, [], [r_small])
        P.op("dve", lambda e: e.memset(epsc, EPS), writes=[r_const])
        P.op("act", lambda e: e.activation(esink, small[:, 0:16], AF.Exp), reads=[r_small], writes=[r_const])
        lq = small[:, 16:272].rearrange("p (a b) -> p a b", a=4)
        prod_t = A.alloc([128, 2, 64], F32)
        P.op("dve", lambda e: e.tensor_tensor(prod_t[:, 0, :], lq[:, 0, :], lq[:, 1, :], ALU.mult), reads=[r_small], writes=[r_const])
        P.op("dve", lambda e: e.tensor_tensor(prod_t[:, 1, :], lq[:, 2, :], lq[:, 3, :], ALU.mult), reads=[r_small], writes=[r_const])
        P.op("dve", lambda e: e.reduce_sum(lam_t[:, 0:2], prod_t, axis=AX.X), reads=[r_const], writes=[r_const])
        P.op("act", lambda e: e.activation(lam_t[:, 2:4], lam_t[:, 0:2], AF.Exp), reads=[r_const], writes=[r_const])
        P.op("dve", lambda e: e.tensor_tensor(lam_t[:, 4:5], lam_t[:, 3:4], lam_t[:, 2:3], ALU.subtract), reads=[r_const], writes=[r_const])
        P.op("dve", lambda e: e.tensor_scalar_add(neglam, lam_t[:, 4:5], -LAMBDA_INIT), reads=[r_const], writes=[r_const])
        P.op("dve", lambda e: e.tensor_scalar_mul(wsub, small[:, 272:400], (1.0 - LAMBDA_INIT)), reads=[r_small], writes=[r_const])
        assert A.mark() <= 4 * 1024
        M0 = 4 * 1024
        A.reset(M0)

        xT = A.alloc([128, 16, NOWN], BF16)
        r_xT = [[P.R() for _ in range(2)] for _ in range(10)]
        after_xT = A.mark()
        QbT = A.alloc([128, 8, TOK], BF16)
        oaT = A.alloc([128, 8, TOK], BF16)
        obT = A.alloc([128, 8, TOK], BF16)
        xT_mark = A.mark()
        assert xT_mark <= 93 * 1024
        A.reset(after_xT)

        Wkb = A.alloc([128, 16, 1024], BF16)
        Wvb = A.alloc([128, 16, 1024], BF16)
        r_Wkb, r_Wvb = P.Rs(2), P.Rs(2)
        for hlf in range(2):
            dma("pool", Wkb[:, :, hlf * 512:(hlf + 1) * 512],
                w_in[:, C_KB + hlf * 512:C_KB + (hlf + 1) * 512].rearrange("(c p) n -> p c n", p=128), [], [r_Wkb[hlf]])
        for hlf in range(2):
            dma("pool", Wvb[:, :, hlf * 512:(hlf + 1) * 512],
                w_in[:, C_VB + hlf * 512:C_VB + (hlf + 1) * 512].rearrange("(c p) n -> p c n", p=128), [], [r_Wvb[hlf]])
        xs = [A.alloc([128, 4, D], BF16) for _ in range(2)]
        xTs = [A.alloc([128, 16, 512], BF16) for _ in range(2)]
        KTs = [A.alloc([128, 8, 512], BF16) for _ in range(2)]
        Vs = [A.alloc([128, 4, 1024], BF16) for _ in range(2)]
        r_xs = P.Rs(2)
        r_xTs = [[[P.R() for _ in range(2)] for _ in range(4)] for _ in range(2)]
        r_KTs = [P.Rs(8) for _ in range(2)]
        r_Vs = [[[P.R() for _ in range(2)] for _ in range(4)] for _ in range(2)]
        r_ktall, r_vall = P.R(), P.R()
        tb = [0]
        mb = [0]

        def next_tbank():
            b = 6 + tb[0] % 2
            tb[0] += 1
            return b

        def next_mbank(n=6):
            b = mb[0] % n
            mb[0] += 1
            return b

        def transpose_block_bf16(src_tok_major, dst_fn, rsrc, rdst_fn):
            for hlf in range(2):
                b = next_tbank()
                pb = bank(b).bitcast(BF16)
                for c in range(8):
                    cc = hlf * 8 + c
                    P.op("pe", lambda e, pb=pb, c=c, cc=cc: e.transpose(pb[:, c * 128:(c + 1) * 128],
                                                                       src_tok_major[:, cc * 128:(cc + 1) * 128], idb),
                         reads=[rsrc, r_idb], writes=[rbank[b]])
                cp(dst_fn(hlf), pb.rearrange("p (c n) -> p c n", c=8), [rbank[b]], [rdst_fn(hlf)])

        NG = 19
        for g in range(16 if (dbg is not None and stage < 4) else 0, NG if stage >= 1 else 0):
            par = g % 2
            nb = 4 if g < 18 else 2
            ntok = nb * 128
            if g < 16:
                src = x_all[g * 512:(g + 1) * 512, :]
            else:
                src = x_own[(g - 16) * 512:(g - 16) * 512 + ntok, :]
            dma("pool", xs[par][:, 0:nb, :], src.rearrange("(b p) d -> p b d", p=128), [], [r_xs[par]])
            for b in range(nb):
                if g < 16:
                    dst_fn = lambda hlf, b=b: xTs[par][:, hlf * 8:(hlf + 1) * 8, b * 128:(b + 1) * 128]
                    rdst_fn = lambda hlf, b=b: r_xTs[par][b][hlf]
                else:
                    lb = (g - 16) * 4 + b
                    dst_fn = lambda hlf, lb=lb: xT[:, hlf * 8:(hlf + 1) * 8, lb * 128:(lb + 1) * 128]
                    rdst_fn = lambda hlf, lb=lb: r_xT[lb][hlf]
                transpose_block_bf16(xs[par][:, b, :], dst_fn, r_xs[par], rdst_fn)
            if g < 16:
                xsrc = xTs[par]
                t0 = 0
                rx = lambda b, hlf: r_xTs[par][b][hlf]
            else:
                xsrc = xT
                t0 = (g - 16) * 512
                rx = lambda b, hlf, g=g: r_xT[(g - 16) * 4 + b][hlf]
            for h in range(8):
                bk = next_mbank()
                for c in range(16):
                    mm(bank(bk)[:, 0:ntok], Wkb[:, c, h * 128:(h + 1) * 128], xsrc[:, c, t0:t0 + ntok], c == 0, c == 15,
                       [r_Wkb[h // 4]] + [rx(b, c // 8) for b in range(nb)], [rbank[bk]])
                cp(KTs[par][:, h, 0:ntok], bank(bk)[:, 0:ntok], [rbank[bk]], [r_KTs[par][h]])
            dma("sp", kt_all[:, :, g * 512:g * 512 + ntok].rearrange("h p n -> p h n"), KTs[par][:, :, 0:ntok],
                r_KTs[par], [r_ktall])
            for b in range(nb):
                for hlf in range(2):
                    bk = next_mbank()
                    for c in range(16):
                        mm(bank(bk), xsrc[:, c, t0 + b * 128:t0 + (b + 1) * 128], Wvb[:, c, hlf * 512:(hlf + 1) * 512], c == 0, c == 15,
                           [r_Wvb[hlf], rx(b, c // 8)], [rbank[bk]])
                    cp(Vs[par][:, b, hlf * 512:(hlf + 1) * 512], bank(bk), [rbank[bk]], [r_Vs[par][b][hlf]])
            dma("sp", v_all[g * 4:g * 4 + nb, :, :].rearrange("b p n -> p b n"), Vs[par][:, 0:nb, :],
                [r for bb in range(nb) for r in r_Vs[par][bb]], [r_vall])
        P.barrier(scratch)
        A.reset(xT_mark)

        QaT = A.alloc([128, 8, TOK], BF16)
        KaT = A.alloc([128, 2, NOWN], BF16)
        Va = A.alloc([128, 10, 2, 65], BF16)
        r_QaT = [P.Rs(2) for _ in range(8)]
        r_QbT = [P.Rs(2) for _ in range(8)]
        r_KaT = [P.Rs(3) for _ in range(2)]
        r_Va = P.Rs(10)
        r_Vaones = P.R()
        r_oaT = P.Rs(8)
        r_obT = [P.Rs(8) for _ in range(8)]
        proj_mark = A.mark()
        allx = [r_xT[b][hh] for b in range(10) for hh in range(2)]

        if stage >= 2:
            Wt = [A.alloc([128, 16, 512], BF16) for _ in range(2)]
            r_Wt = P.Rs(2)
            wi = [0]

            def load_w(col0, ncols=512):
                k = wi[0] % 2
                wi[0] += 1
                dma("pool", Wt[k][:, :, 0:ncols], w_in[:, col0:col0 + ncols].rearrange("(c p) n -> p c n", p=128), [], [r_Wt[k]])
                return Wt[k], r_Wt[k]

            for (col0, dst, rdst) in ((C_QA, QaT, r_QaT), (C_QB, QbT, r_QbT)):
                for pn in range(2):
                    W, rW = load_w(col0 + pn * 512)
                    for ii in range(4):
                        i = pn * 4 + ii
                        for hlf in range(2):
                            bk = next_mbank()
                            for c in range(16):
                                mm(bank(bk), W[:, c, ii * 128:(ii + 1) * 128], xT[:, c, OWN0[hlf]:OWN0[hlf] + 512],
                                   c == 0, c == 15, [rW] + allx, [rbank[bk]])
                            cp(dst[:, i, hlf * 512:(hlf + 1) * 512], bank(bk), [rbank[bk]], [rdst[i][hlf]])
            k = wi[0] % 2
            wi[0] += 1
            Wka = Wt[k][:, :, 0:256].rearrange("p c (g u d) -> p c g u d", g=2, u=2)
            for u in range(2):
                for g in range(2):
                    dma("pool", Wka[:, :, g, u, :], w_in[:, C_KA + g * 64:C_KA + (g + 1) * 64].rearrange("(c p) d -> p c d", p=128), [], [r_Wt[k]])
            Wva = Wt[k][:, :, 256:384]
            dma("pool", Wva, w_in[:, C_VA:C_VA + 128].rearrange("(c p) n -> p c n", p=128), [], [r_Wt[k]])
            for g in range(2):
                for pi, (t0, nt) in enumerate(((0, 512), (512, 512), (1024, 256))):
                    bk = next_mbank()
                    for c in range(16):
                        mm(bank(bk)[:, 0:nt], Wt[k][:, c, g * 128:(g + 1) * 128], xT[:, c, t0:t0 + nt], c == 0, c == 15,
                           [r_Wt[k]] + allx, [rbank[bk]])
                    cp(KaT[:, g, t0:t0 + nt], bank(bk)[:, 0:nt], [rbank[bk]], [r_KaT[g][pi]])
            P.op("pool", lambda e: e.memset(Va[:, :, :, 64:65], 1.0), writes=[r_Vaones])
            for b in range(10):
                bk = next_mbank()
                for c in range(16):
                    mm(bank(bk)[:, 0:128], xT[:, c, b * 128:(b + 1) * 128], Wva[:, c, :], c == 0, c == 15,
                       [r_Wt[k], r_xT[b][c // 8]], [rbank[bk]])
                cp(Va[:, b, :, 0:64], bank(bk)[:, 0:128].rearrange("p (g d) -> p g d", g=2), [rbank[bk]], [r_Va[b]])
        P.barrier(scratch)
        A.reset(proj_mark)

        def pipeline(nitems, stages):
            ns = len(stages)
            for tau in range(nitems + ns - 1):
                for si in range(ns - 1, -1, -1):
                    i = tau - si
                    if 0 <= i < nitems:
                        stages[si](i)

        if stage >= 3:
            Bsw = [A.alloc([128, 4, 4, 256], F32) for _ in range(2)]
            r_Bsw = P.Rs(2)
            tmpS = [A.alloc([128, 4, 128], F32) for _ in range(4)]
            r_tmpS = P.Rs(4)
            PTs = [A.alloc([128, 4, 128], BF16) for _ in range(8)]
            r_PTs = P.Rs(8)
            oa = [A.alloc([128, 1024], BF16) for _ in range(2)]
            r_oa = [P.Rs(4) for _ in range(2)]
            den = [A.alloc([128, 8], F32) for _ in range(4)]
            r_den = P.Rs(4)
            r_rden = P.Rs(4)
            allKa = [r for g in range(2) for r in r_KaT[g]]
            def unpack(i):
                j, gr = divmod(i, 4)
                g, r = divmod(gr, 2)
                return j, g, r

            def sA(i):
                j, g, r = unpack(i)
                bp = j % 2
                if i % 4 == 0:
                    dma("sp", Bsw[bp].rearrange("p a b c -> p (a b c)"), bias_swa[1 if j == 0 else 0], [], [r_Bsw[bp]])
                for kap in range(2):
                    blk = LB(j) - 1 + kap
                    bk = (i % 2) * 2 + kap
                    mm(bank(bk).rearrange("p (h q) -> p h q", h=4), KaT[r * 64:(r + 1) * 64, g, blk * 128:(blk + 1) * 128],
                       QaT[r * 64:(r + 1) * 64, 4 * g:4 * g + 4, j * 128:(j + 1) * 128], True, True,
                       allKa + [r_QaT[ii][j // 4] for ii in range(4 * g, 4 * g + 4)], [rbank[bk]])
                    tk = (i % 2) * 2 + kap
                    pk = (i % 4) * 2 + kap
                    P.op("dve", lambda e, tk=tk, bk=bk, kap=kap, g=g, r=r, bp=bp: e.scalar_tensor_tensor(
                        tmpS[tk], bank(bk).rearrange("p (h q) -> p h q", h=4), 0.125,
                        Bsw[bp][:, g * 2 + r, :, kap * 128:(kap + 1) * 128], ALU.mult, ALU.add),
                        reads=[rbank[bk], r_Bsw[bp]], writes=[r_tmpS[tk]])
                    P.op("act", lambda e, tk=tk, pk=pk: e.activation(PTs[pk], tmpS[tk], AF.Exp),
                         reads=[r_tmpS[tk]], writes=[r_PTs[pk]])

            def sB(i):
                j, g, r = unpack(i)
                ab = 4 + i % 2
                acc = bank(ab).rearrange("p (h c) -> p h c", h=4)
                for hh in range(4):
                    for kap in range(2):
                        blk = LB(j) - 1 + kap
                        pk = (i % 4) * 2 + kap
                        mm(acc[:, hh, 0:65], PTs[pk][:, hh, :], Va[:, blk, g, :], kap == 0, kap == 1,
                           [r_PTs[pk], r_Va[blk], r_Vaones], [rbank[ab]])

            def sC(i):
                j, g, r = unpack(i)
                ab = 4 + i % 2
                acc = bank(ab).rearrange("p (h c) -> p h c", h=4)
                dp = i % 4
                hsl = slice((g * 2 + r) * 4, (g * 2 + r) * 4 + 4)
                P.op("dve", lambda e, dp=dp, acc=acc, hsl=hsl: e.tensor_tensor(den[dp][:, 0:4], acc[:, :, 64], esink[:, hsl], ALU.add),
                     reads=[rbank[ab], r_const], writes=[r_den[dp]])
                P.op("dve", lambda e, dp=dp: e.reciprocal(den[dp][:, 4:8], den[dp][:, 0:4]),
                     reads=[r_den[dp]], writes=[r_rden[dp]])

            def sD(i):
                j, g, r = unpack(i)
                bp = j % 2
                ab = 4 + i % 2
                acc = bank(ab).rearrange("p (h c) -> p h c", h=4)
                dp = i % 4
                heads = [2 * ii + r for ii in range(4 * g, 4 * g + 4)]
                for hh in range(4):
                    h = heads[hh]
                    P.op("dve", lambda e, dp=dp, acc=acc, hh=hh, h=h, bp=bp: e.tensor_scalar_mul(
                        oa[bp][:, h * 64:(h + 1) * 64], acc[:, hh, 0:64], den[dp][:, 4 + hh:5 + hh]),
                        reads=[rbank[ab], r_rden[dp]], writes=[r_oa[bp][g * 2 + r]])
                if i % 4 == 3:
                    b = next_tbank()
                    pb = bank(b).bitcast(BF16)
                    for c in range(8):
                        P.op("pe", lambda e, pb=pb, c=c, bp=bp: e.transpose(pb[:, c * 128:(c + 1) * 128], oa[bp][:, c * 128:(c + 1) * 128], idb),
                             reads=r_oa[bp] + [r_idb], writes=[rbank[b]])
                    cp(oaT[:, :, j * 128:(j + 1) * 128], pb.rearrange("p (c n) -> p c n", c=8), [rbank[b]], [r_oaT[j]])

            pipeline(NB * 4, [sA, lambda i: None, lambda i: None, sB, sC, sD])
        P.barrier(scratch)
        A.reset(xT_mark)

        if stage >= 4:
            KTh = [A.alloc([128, NTOKX], BF16) for _ in range(2)]
            Vh = [A.alloc([128, 74, 129], BF16) for _ in range(2)]
            r_KTh, r_Vh, r_Vhones = P.Rs(2), P.Rs(2), P.Rs(2)
            Bn = A.alloc([128, 5, 512], F32)
            r_Bn = P.R()
            Bf = A.alloc([128, 2, 64, 8], F32)
            r_Bf = P.R()
            tmpD = [A.alloc([128, 2, 512], F32) for _ in range(2)]
            r_tmpD = P.Rs(2)
            PT = [A.alloc([128, 2, 512], BF16) for _ in range(2)]
            r_PT = P.Rs(2)
            accS = A.alloc([128, 3, 512], F32)
            r_accS = P.Rs(3)
            o0 = [A.alloc([128, 128], F32) for _ in range(4)]
            o1 = [A.alloc([128, 128], F32) for _ in range(4)]
            obk = A.alloc([128, 4, 128], BF16)
            st4 = [A.alloc([128, 8], F32) for _ in range(4)]
            r_r0, r_r1, r_r1l, r_o0, r_o1, r_ss, r_ln, r_rstd, r_obk = [P.Rs(4) for _ in range(9)]
            dma("sp", Bf.rearrange("p a b c -> p (a b c)"), bias_far, [], [r_Bf])
            zt = A.alloc([128, 512], BF16)
            r_zt = P.R()
            P.op("pool", lambda e: e.memset(zt, 0.0), writes=[r_zt])
            for k in range(2):
                P.op("pool", lambda e, k=k: e.memset(Vh[k][:, :, 128:129], 1.0), writes=[r_Vhones[k]])
            def accap(a):
                return bank(4 + a // 3)[:, (a % 3) * 160:(a % 3) * 160 + 129]

            def accS_ap(a):
                return accS[:, a // 3, (a % 3) * 160:(a % 3) * 160 + 129]

            def make_stages(h, ch):
                def g0():
                    for qb in range(4):
                        s4, a0, a1 = st4[qb], accS_ap(qb), accS_ap(4 + qb)
                        P.op("dve", lambda e, s4=s4, a0=a0: e.reciprocal(s4[:, 0:1], a0[:, 128:129]),
                             reads=[r_accS[qb // 3]], writes=[r_r0[qb]])
                        P.op("dve", lambda e, s4=s4, a1=a1: e.reciprocal(s4[:, 1:2], a1[:, 128:129]),
                             reads=[r_accS[(4 + qb) // 3]], writes=[r_r1[qb]])

                def g1():
                    for qb in range(4):
                        s4, a0 = st4[qb], accS_ap(qb)
                        P.op("dve", lambda e, s4=s4: e.tensor_tensor(s4[:, 2:3], s4[:, 1:2], neglam, ALU.mult),
                             reads=[r_r1[qb], r_const], writes=[r_r1l[qb]])
                        P.op("dve", lambda e, s4=s4, a0=a0, qb=qb: e.tensor_scalar_mul(o0[qb], a0[:, 0:128], s4[:, 0:1]),
                             reads=[r_accS[qb // 3], r_r0[qb]], writes=[r_o0[qb]])

                def g2():
                    for qb in range(4):
                        s4, a1 = st4[qb], accS_ap(4 + qb)
                        P.op("dve", lambda e, s4=s4, a1=a1, qb=qb: e.scalar_tensor_tensor(o1[qb], a1[:, 0:128], s4[:, 2:3], o0[qb], ALU.mult, ALU.add),
                             reads=[r_accS[(4 + qb) // 3], r_r1l[qb], r_o0[qb]], writes=[r_o1[qb]])
                        P.op("dve", lambda e, s4=s4: e.memset(s4[:, 3:4], 0.0), writes=[r_ss[qb]])

                def g3():
                    for qb in range(4):
                        s4 = st4[qb]
                        P.op("act", lambda e, s4=s4, qb=qb: e.activation(o0[qb], o1[qb], AF.Square, accum_out=s4[:, 3:4]),
                             reads=[r_o1[qb]], writes=[r_ss[qb], r_o0[qb]])

                def g4():
                    for qb in range(4):
                        s4 = st4[qb]
                        P.op("act", lambda e, s4=s4: e.activation(s4[:, 4:5], s4[:, 3:4], AF.Ln, bias=epsc, scale=1.0 / 128.0),
                             reads=[r_ss[qb], r_const], writes=[r_ln[qb]])

                def g5():
                    for qb in range(4):
                        s4 = st4[qb]
                        P.op("act", lambda e, s4=s4: e.activation(s4[:, 5:6], s4[:, 4:5], AF.Exp, scale=-0.5),
                             reads=[r_ln[qb]], writes=[r_rstd[qb]])

                def g6():
                    for qb in range(4):
                        s4 = st4[qb]
                        P.op("dve", lambda e, s4=s4, qb=qb: e.scalar_tensor_tensor(obk[:, qb, :], o1[qb], s4[:, 5:6], wsub, ALU.mult, ALU.mult),
                             reads=[r_o1[qb], r_rstd[qb], r_const], writes=[r_obk[qb]])

                def g7():
                    pb = bank(7).bitcast(BF16)
                    for qb in range(4):
                        P.op("pe", lambda e, pb=pb, qb=qb: e.transpose(pb[:, qb * 128:(qb + 1) * 128], obk[:, qb, :], idb),
                             reads=[r_obk[qb], r_idb], writes=[rbank[7]])
                    cp(obT[:, h, ch * 512:(ch + 1) * 512], pb[:, 0:512], [rbank[7]], [r_obT[h][ch * 4 + qb] for qb in range(4)], eng="dve")

                return [g0, g1, g2, g3, g4, g5, g6, g7]

            sc = [pp[0][:], pp[1][:]]
            r_sc = P.Rs(2)
            r_acc = P.Rs(3)
            pending = []
            for h in range(8):
                hp = h % 2
                dma("sp", KTh[hp], kt_all[h], [r_ktall], [r_KTh[hp]])
                dma("sp", Vh[hp][:, :, 0:128], v_all[:, :, h * 128:(h + 1) * 128].rearrange("b p n -> p b n"), [r_vall], [r_Vh[hp]])
                for ch in range(2):
                    dma("sp", Bn.rearrange("p a b -> p (a b)"), bias_near[ch, h], [], [r_Bn])
                    tiles = [("far", kb) for kb in range(NFAR[ch])] + [("near", kap) for kap in range(5)]
                    nt = len(tiles)
                    qsl = slice(ch * 512, (ch + 1) * 512)

                    def emit_qk(t):
                        kind, idx = tiles[t]
                        sp_ = t % 2
                        tok0 = idx * 128 if kind == "far" else S + (ch * 5 + idx) * 128
                        for m in range(2):
                            mm(sc[sp_][:, m * 512:(m + 1) * 512], KTh[hp][m * 64:(m + 1) * 64, tok0:tok0 + 128],
                               QbT[m * 64:(m + 1) * 64, h, qsl], True, True,
                               [r_KTh[hp], r_QbT[h][ch]], [r_sc[sp_]])

                    emit_qk(0)
                    emit_qk(1)
                    for ab in range(3):
                        mm(bank(4 + ab), zt[:, 0:128], zt, True, False, [r_zt], [r_acc[ab]])
                    for t in range(nt):
                        kind, idx = tiles[t]
                        sp_ = t % 2
                        if kind == "far":
                            bias_ap = Bf[:, ch, idx, h:h + 1]
                            P.op("act", lambda e, sp_=sp_, bias_ap=bias_ap: e.activation(
                                PT[sp_].rearrange("p a b -> p (a b)"), sc[sp_], AF.Exp, bias=bias_ap, scale=0.125),
                                reads=[r_sc[sp_], r_Bf], writes=[r_PT[sp_]])
                        else:
                            for m in range(2):
                                bn_ap = Bn[:, idx, :]
                                P.op("dve", lambda e, sp_=sp_, bn_ap=bn_ap, m=m: e.scalar_tensor_tensor(
                                    tmpD[sp_][:, m, :], sc[sp_][:, m * 512:(m + 1) * 512], 0.125, bn_ap, ALU.mult, ALU.add),
                                    reads=[r_sc[sp_], r_Bn], writes=[r_tmpD[sp_]])
                            P.op("act", lambda e, sp_=sp_: e.activation(PT[sp_], tmpD[sp_], AF.Exp),
                                 reads=[r_tmpD[sp_]], writes=[r_PT[sp_]])
                        if t + 2 < nt:
                            emit_qk(t + 2)
                        vb = idx if kind == "far" else 64 + ch * 5 + idx
                        for m in range(2):
                            for qb in range(4):
                                a = m * 4 + qb
                                mm(accap(a), PT[sp_][:, m, qb * 128:(qb + 1) * 128], Vh[hp][:, vb, :], False, (t == nt - 1) and a in (2, 5, 7),
                                   [r_PT[sp_], r_Vh[hp], r_Vhones[hp]], [r_acc[a // 3]])
                        if pending and t >= 2:
                            pending.pop(0)()
                    while pending:
                        pending.pop(0)()
                    for ab in range(3):
                        cp(accS[:, ab, :], bank(4 + ab), [r_acc[ab]], [r_accS[ab]], eng="dve")
                    pending = make_stages(h, ch)
            while pending:
                pending.pop(0)()
        P.barrier(scratch)
        A.reset(xT_mark)

        y = None

        def layer_norm_and_T(y, r_y, lnw, r_lnw, hT, r_hT, stt, final_out=None, r_out=None):
            r_st = [P.Rs(4) for _ in range(NB)]
            r_mv, r_rs, r_nm = P.Rs(NB), P.Rs(NB), P.Rs(NB)

            def s0(b):
                yb, s = y[:, b, :], stt[:, b, :]
                stats = s[:, 0:24].rearrange("p (k s) -> p k s", k=4)
                for k in range(4):
                    P.op("dve", lambda e, k=k, yb=yb, stats=stats: e.bn_stats(stats[:, k, :], yb[:, k * 512:(k + 1) * 512]),
                         reads=[r_y[b][k]], writes=[r_st[b][k]])
                P.op("dve", lambda e, s=s, stats=stats: e.bn_aggr(s[:, 24:26], stats), reads=r_st[b], writes=[r_mv[b]])

            def s1(b):
                s = stt[:, b, :]
                P.op("act", lambda e, s=s: e.activation(s[:, 27:28], s[:, 25:26], AF.Ln, bias=epsc), reads=[r_mv[b], r_const], writes=[r_rs[b]])
                P.op("act", lambda e, s=s: e.activation(s[:, 26:27], s[:, 27:28], AF.Exp, scale=-0.5), reads=[r_rs[b]], writes=[r_rs[b]])

            def s2(b):
                s = stt[:, b, :]
                P.op("dve", lambda e, s=s: e.scalar_tensor_tensor(s[:, 28:29], s[:, 24:25], -1.0, s[:, 26:27], ALU.mult, ALU.mult),
                     reads=[r_mv[b], r_rs[b]], writes=[r_nm[b]])

            def s3(b):
                yb, s = y[:, b, :], stt[:, b, :]
                P.op("act", lambda e, s=s, yb=yb: e.activation(yb, yb, AF.Identity, bias=s[:, 28:29], scale=s[:, 26:27]),
                     reads=[r_nm[b], r_rs[b]] + r_y[b], writes=r_y[b])

            def s4(b):
                yb = y[:, b, :]
                P.op("dve", lambda e, yb=yb: e.tensor_tensor(yb, yb, lnw[:, 0, :], ALU.mult), reads=[r_lnw] + r_y[b], writes=r_y[b])

            def s5(b):
                yb = y[:, b, :]
                P.op("pool", lambda e, yb=yb: e.tensor_tensor(yb, yb, lnw[:, 1, :], ALU.add), reads=[r_lnw] + r_y[b], writes=r_y[b])

            def s6(b):
                yb = y[:, b, :]
                if final_out is not None:
                    dma("sp", final_out[b * 128:(b + 1) * 128, :], yb, r_y[b], [r_out])
                    return
                for q4 in range(4):
                    bk = next_tbank()
                    for c in range(4):
                        cc = q4 * 4 + c
                        P.op("pe", lambda e, bk=bk, c=c, cc=cc, yb=yb: e.transpose(bank(bk)[:, c * 128:(c + 1) * 128], yb[:, cc * 128:(cc + 1) * 128], idf),
                             reads=r_y[b] + [r_idf], writes=[rbank[bk]])
                    cp(hT[:, q4 * 4:(q4 + 1) * 4, b * 128:(b + 1) * 128], bank(bk).rearrange("p (c n) -> p c n", c=4),
                       [rbank[bk]], [r_hT[b][q4]])

            pipeline(NB, [s0, s1, s2, s3, s4, s5, s6])

        if stage >= 5:
            A.reset(117 * 1024)
            mixT = A.alloc([128, 16, TOK], BF16)
            r_mixT = [P.Rs(2) for _ in range(16)]
            e_mark = A.mark()
            Wg = [A.alloc([128, 16, 512], BF16) for _ in range(2)]
            Wab = [A.alloc([128, 8, 512], BF16) for _ in range(2)]
            r_Wg, r_Wab = P.Rs(2), P.Rs(2)
            sg = [A.alloc([128, 512], F32) for _ in range(2)]
            r_sg = P.Rs(2)
            m1 = [A.alloc([128, 512], F32) for _ in range(2)]
            r_m1 = P.Rs(2)
            all_oaT = r_oaT
            all_obT = [r for hh in range(8) for r in r_obT[hh]]
            gi = 0
            for jj in range(8):
                k = jj % 2
                dma("pool", Wg[k][:, :, 0:256], w_in[:, C_GA + jj * 256:C_GA + (jj + 1) * 256].rearrange("(c p) n -> p c n", p=128), [], [r_Wg[k]])
                dma("pool", Wg[k][:, :, 256:512], w_in[:, C_GB + jj * 256:C_GB + (jj + 1) * 256].rearrange("(c p) n -> p c n", p=128), [], [r_Wg[k]])
                dma("pool", Wab[k][:, :, 0:256], w_a[:, jj * 256:(jj + 1) * 256].rearrange("(c p) n -> p c n", p=128), [], [r_Wab[k]])
                dma("pool", Wab[k][:, :, 256:512], w_b[:, jj * 256:(jj + 1) * 256].rearrange("(c p) n -> p c n", p=128), [], [r_Wab[k]])
                for sub in range(2):
                    j = jj * 2 + sub
                    for hlf in range(2):
                        tsl = slice(hlf * 512, (hlf + 1) * 512)
                        for br in range(2):
                            gk = gi % 2
                            gi += 1
                            bg = next_mbank()
                            for c in range(16):
                                mm(bank(bg), Wg[k][:, c, br * 256 + sub * 128:br * 256 + (sub + 1) * 128],
                                   xT[:, c, OWN0[hlf]:OWN0[hlf] + 512], c == 0, c == 15, [r_Wg[k]] + allx, [rbank[bg]])
                            P.op("act", lambda e, gk=gk, bg=bg: e.activation(sg[gk], bank(bg), AF.Sigmoid),
                                 reads=[rbank[bg]], writes=[r_sg[gk]])
                            bb = next_mbank()
                            srcT = oaT if br == 0 else obT
                            rsrc = all_oaT if br == 0 else all_obT
                            for c in range(8):
                                mm(bank(bb), Wab[k][:, c, br * 256 + sub * 128:br * 256 + (sub + 1) * 128], srcT[:, c, tsl],
                                   c == 0, c == 7, [r_Wab[k]] + rsrc, [rbank[bb]])
                            if br == 0:
                                mk = (gi // 2) % 2
                                P.op("dve", lambda e, mk=mk, bb=bb, gk=gk: e.tensor_tensor(m1[mk], bank(bb), sg[gk], ALU.mult),
                                     reads=[rbank[bb], r_sg[gk]], writes=[r_m1[mk]])
                                mk_a = mk
                            else:
                                P.op("dve", lambda e, bb=bb, gk=gk: e.tensor_tensor(sg[gk], bank(bb), sg[gk], ALU.mult),
                                     reads=[rbank[bb], r_sg[gk]], writes=[r_sg[gk]])
                                P.op("dve", lambda e, gk=gk, mk_a=mk_a, j=j, tsl=tsl: e.tensor_tensor(mixT[:, j, tsl], m1[mk_a], sg[gk], ALU.add),
                                     reads=[r_m1[mk_a], r_sg[gk]], writes=[r_mixT[j][hlf]])
            P.barrier(scratch)
            A.reset(M0)
            y = A.alloc([128, NB, D], F32)
            r_y = [P.Rs(4) for _ in range(NB)]
            lnw = A.alloc([128, 2, D], F32)
            r_lnw = P.R()
            stt = A.alloc([128, NB, 32], F32)
            hT = A.alloc([128, 16, TOK], BF16)
            r_hT = [P.Rs(4) for _ in range(NB)]
            h_mark = A.mark()
            assert h_mark <= 117 * 1024
            A.reset(e_mark)
            for b in range(NB):
                dma("sp", y[:, b, :], x_own[LB(b) * 128:(LB(b) + 1) * 128, :], [], r_y[b])
            dma("sp", lnw, lnp[:, 0:2, :], [], [r_lnw])
            Wo = [A.alloc([128, 16, 512], BF16) for _ in range(2)]
            r_Wo = P.Rs(2)
            all_mix = [r for j in range(16) for r in r_mixT[j]]
            for pn in range(4):
                k = pn % 2
                dma("pool", Wo[k], w_o[:, pn * 512:(pn + 1) * 512].rearrange("(c p) n -> p c n", p=128), [], [r_Wo[k]])
                for b in range(NB):
                    bk = next_mbank()
                    for c in range(16):
                        mm(bank(bk), mixT[:, c, b * 128:(b + 1) * 128], Wo[k][:, c, :], c == 0, c == 15,
                           [r_Wo[k], r_mixT[c][b // 4]], [rbank[bk]])
                    ysl = y[:, b, pn * 512:(pn + 1) * 512]
                    P.op("dve", lambda e, ysl=ysl, bk=bk: e.scalar_tensor_tensor(ysl, ysl, ALPHA, bank(bk), ALU.mult, ALU.add),
                         reads=[rbank[bk], r_y[b][pn]], writes=[r_y[b][pn]])
            layer_norm_and_T(y, r_y, lnw, r_lnw, hT, r_hT, stt)
            all_hT = [r for b in range(NB) for r in r_hT[b]]

        if stage >= 6:
            P.barrier(scratch)
            A.reset(h_mark)
            memb = A.alloc([128, 2, D], BF16)
            memT = A.alloc([128, 16, 256], BF16)
            r_memb = P.R()
            r_memT = [P.Rs(2) for _ in range(2)]
            Wflat = [A.alloc([128, 8192], BF16) for _ in range(2)]
            Wm = [w.rearrange("p (c n) -> p c n", c=16) for w in Wflat]
            r_Wm = P.Rs(2)
            KcT = A.alloc([128, 4, 256], BF16)
            Vc = A.alloc([128, 2, 4, 129], BF16)
            r_KcT = P.Rs(4)
            r_Vc = P.Rs(2)
            r_Vcones = P.R()
            qcT = A.alloc([128, 4, TOK], BF16)
            r_qcT = [P.Rs(2) for _ in range(4)]
            oc = A.alloc([128, NB, 512], BF16)
            r_oc = [P.Rs(4) for _ in range(NB)]
            ocT = A.alloc([128, 4, TOK], BF16)
            r_ocT = P.Rs(NB)
            Wco = Wflat[1].rearrange("p (c n) -> p c n", c=4)
            r_Wco = r_Wm[1]
            PTc = [A.alloc([128, 512], BF16) for _ in range(4)]
            r_PTc = P.Rs(4)
            rc = [A.alloc([128, 4], F32) for _ in range(2)]
            r_rc = P.Rs(2)
            dma("pool", memb, mem.rearrange("(b p) d -> p b d", p=128), [], [r_memb])
            dma("pool", Wm[0], w_mkv[:, 0:512].rearrange("(c p) n -> p c n", p=128), [], [r_Wm[0]])
            dma("pool", Wm[1], w_mkv[:, 512:1024].rearrange("(c p) n -> p c n", p=128), [], [r_Wm[1]])
            dma("sp", lnw, lnp[:, 2:4, :], [], [r_lnw])
            for mbk in range(2):
                transpose_block_bf16(memb[:, mbk, :], lambda hlf, mbk=mbk: memT[:, hlf * 8:(hlf + 1) * 8, mbk * 128:(mbk + 1) * 128],
                                     r_memb, lambda hlf, mbk=mbk: r_memT[mbk][hlf])
            all_memT = [r for a in r_memT for r in a]
            for h in range(4):
                bk = next_mbank()
                for c in range(16):
                    mm(bank(bk)[:, 0:256], Wm[0][:, c, h * 128:(h + 1) * 128], memT[:, c, :], c == 0, c == 15,
                       [r_Wm[0]] + all_memT, [rbank[bk]])
                cp(KcT[:, h, :], bank(bk)[:, 0:256], [rbank[bk]], [r_KcT[h]])
            P.op("pool", lambda e: e.memset(Vc[:, :, :, 128:129], 1.0), writes=[r_Vcones])
            for mbk in range(2):
                bk = next_mbank()
                for c in range(16):
                    mm(bank(bk), memT[:, c, mbk * 128:(mbk + 1) * 128], Wm[1][:, c, :], c == 0, c == 15,
                       [r_Wm[1], r_memT[mbk][c // 8]], [rbank[bk]])
                cp(Vc[:, mbk, :, 0:128], bank(bk).rearrange("p (h d) -> p h d", h=4), [rbank[bk]], [r_Vc[mbk]])
            Wq = Wm[0]
            r_Wq = r_Wm[0]
            dma("pool", Wq, w_cq.rearrange("(c p) n -> p c n", p=128), [], [r_Wq])
            dma("pool", Wco, w_co.rearrange("(c p) n -> p c n", p=128), [], [r_Wco])
            for h in range(4):
                for hlf in range(2):
                    bk = next_mbank()
                    for c in range(16):
                        mm(bank(bk), Wq[:, c, h * 128:(h + 1) * 128], hT[:, c, hlf * 512:(hlf + 1) * 512], c == 0, c == 15,
                           [r_Wq] + all_hT, [rbank[bk]])
                    cp(qcT[:, h, hlf * 512:(hlf + 1) * 512], bank(bk), [rbank[bk]], [r_qcT[h][hlf]])
            ci = 0
            CSC = 128.0 ** -0.5
            for h in range(4):
                for hlf in range(2):
                    pts = []
                    for mbk in range(2):
                        bk = next_mbank(4)
                        mm(bank(bk), KcT[:, h, mbk * 128:(mbk + 1) * 128], qcT[:, h, hlf * 512:(hlf + 1) * 512], True, True,
                           [r_KcT[h], r_qcT[h][hlf]], [rbank[bk]])
                        pk = ci % 4
                        ci += 1
                        P.op("act", lambda e, pk=pk, bk=bk: e.activation(PTc[pk], bank(bk), AF.Exp, scale=CSC),
                             reads=[rbank[bk]], writes=[r_PTc[pk]])
                        pts.append(pk)
                    ab0 = 4 + 2 * ((h * 2 + hlf) % 2)
                    def cacc(qb, ab0=ab0):
                        return bank(ab0 + qb // 2)[:, (qb % 2) * 160:(qb % 2) * 160 + 129]
                    for qb in range(4):
                        for mbk in range(2):
                            mm(cacc(qb), PTc[pts[mbk]][:, qb * 128:(qb + 1) * 128],
                               Vc[:, mbk, h, :], mbk == 0, mbk == 1, [r_PTc[pts[mbk]], r_Vc[mbk], r_Vcones], [rbank[ab0 + qb // 2]])
                    rk = (h * 2 + hlf) % 2
                    for qb in range(4):
                        a_ap = cacc(qb)
                        blk = hlf * 4 + qb
                        P.op("dve", lambda e, rk=rk, qb=qb, a_ap=a_ap: e.reciprocal(rc[rk][:, qb:qb + 1], a_ap[:, 128:129]),
                             reads=[rbank[ab0 + qb // 2]], writes=[r_rc[rk]])
                        P.op("dve", lambda e, rk=rk, qb=qb, a_ap=a_ap, blk=blk, h=h: e.tensor_scalar_mul(
                            oc[:, blk, h * 128:(h + 1) * 128], a_ap[:, 0:128], rc[rk][:, qb:qb + 1]),
                            reads=[rbank[ab0 + qb // 2], r_rc[rk]], writes=[r_oc[blk][h]])
            for b in range(NB):
                bk = next_tbank()
                pb = bank(bk).bitcast(BF16)
                for c in range(4):
                    P.op("pe", lambda e, pb=pb, c=c, b=b: e.transpose(pb[:, c * 128:(c + 1) * 128], oc[:, b, c * 128:(c + 1) * 128], idb),
                         reads=r_oc[b] + [r_idb], writes=[rbank[bk]])
                cp(ocT[:, :, b * 128:(b + 1) * 128], pb[:, 0:512].rearrange("p (c n) -> p c n", c=4), [rbank[bk]], [r_ocT[b]])
            for b in range(NB):
                for pn in range(4):
                    bk = next_mbank()
                    for c in range(4):
                        mm(bank(bk), ocT[:, c, b * 128:(b + 1) * 128], Wco[:, c, pn * 512:(pn + 1) * 512], c == 0, c == 3,
                           [r_Wco, r_ocT[b]], [rbank[bk]])
                    ysl = y[:, b, pn * 512:(pn + 1) * 512]
                    P.op("dve", lambda e, ysl=ysl, bk=bk: e.scalar_tensor_tensor(ysl, ysl, ALPHA, bank(bk), ALU.mult, ALU.add),
                         reads=[rbank[bk], r_y[b][pn]], writes=[r_y[b][pn]])
            layer_norm_and_T(y, r_y, lnw, r_lnw, hT, r_hT, stt)
            P.barrier(scratch)
            A.reset(h_mark)

        if stage >= 7:
            dma("sp", lnw, lnp[:, 4:6, :], [], [r_lnw])
            Wgu = [A.alloc([128, 16, 512], BF16) for _ in range(2)]
            Wd = [A.alloc([128, 2, D], BF16) for _ in range(2)]
            r_Wgu, r_Wd = P.Rs(2), P.Rs(2)
            Aff = [A.alloc([128, 2, TOK], BF16) for _ in range(2)]
            r_Aff = [[P.Rs(2) for _ in range(2)] for _ in range(2)]
            sgf = [A.alloc([128, 512], F32) for _ in range(2)]
            r_sgf = P.Rs(2)
            fi = 0
            NSC = DFF // 256
            for s in range(NSC):
                k = s % 2
                dma("pool", Wgu[k][:, :, 0:256], w_gu[:, s * 256:(s + 1) * 256].rearrange("(c p) n -> p c n", p=128), [], [r_Wgu[k]])
                dma("pool", Wgu[k][:, :, 256:512], w_gu[:, DFF + s * 256:DFF + (s + 1) * 256].rearrange("(c p) n -> p c n", p=128), [], [r_Wgu[k]])
                dma("pool", Wd[k], w_dn[s * 256:(s + 1) * 256, :].rearrange("(c p) n -> p c n", p=128), [], [r_Wd[k]])
                for sub in range(2):
                    for hlf in range(2):
                        bg = next_mbank()
                        for c in range(16):
                            mm(bank(bg), Wgu[k][:, c, sub * 128:(sub + 1) * 128], hT[:, c, hlf * 512:(hlf + 1) * 512], c == 0, c == 15,
                               [r_Wgu[k]] + all_hT, [rbank[bg]])
                        bu = next_mbank()
                        for c in range(16):
                            mm(bank(bu), Wgu[k][:, c, 256 + sub * 128:256 + (sub + 1) * 128], hT[:, c, hlf * 512:(hlf + 1) * 512], c == 0, c == 15,
                               [r_Wgu[k]] + all_hT, [rbank[bu]])
                        fk = fi % 2
                        fi += 1
                        P.op("act", lambda e, fk=fk, bg=bg: e.activation(sgf[fk], bank(bg), AF.Silu), reads=[rbank[bg]], writes=[r_sgf[fk]])
                        P.op("dve", lambda e, fk=fk, bu=bu, k=k, sub=sub, hlf=hlf: e.tensor_tensor(
                            Aff[k][:, sub, hlf * 512:(hlf + 1) * 512], bank(bu), sgf[fk], ALU.mult),
                            reads=[rbank[bu], r_sgf[fk]], writes=[r_Aff[k][sub][hlf]])
                for b in range(NB):
                    for pn in range(4):
                        bk = next_mbank()
                        for sub in range(2):
                            mm(bank(bk), Aff[k][:, sub, b * 128:(b + 1) * 128], Wd[k][:, sub, pn * 512:(pn + 1) * 512], sub == 0, sub == 1,
                               [r_Wd[k], r_Aff[k][sub][b // 4]], [rbank[bk]])
                        ysl = y[:, b, pn * 512:(pn + 1) * 512]
                        if s == 0:
                            P.op("dve", lambda e, ysl=ysl, bk=bk: e.scalar_tensor_tensor(ysl, ysl, ALPHA, bank(bk), ALU.mult, ALU.add),
                                 reads=[rbank[bk], r_y[b][pn]], writes=[r_y[b][pn]])
                        else:
                            P.op("dve", lambda e, ysl=ysl, bk=bk: e.tensor_tensor(ysl, ysl, bank(bk), ALU.add),
                                 reads=[rbank[bk], r_y[b][pn]], writes=[r_y[b][pn]])
            r_out = P.R()
            layer_norm_and_T(y, r_y, lnw, r_lnw, None, None, stt, final_out=out, r_out=r_out)
            P.final_wait(r_out)

        if dbg is not None:
            P.barrier(scratch)
            r_dbg = P.R()
            if stage == 3:
                stg = A.alloc([128, 8, TOK], F32)
                P.op("dve", lambda e: e.tensor_copy(stg, oaT), reads=r_oaT, writes=[r_dbg])
                dma("sp", dbg_out.rearrange("(c p) t -> p c t", p=128), stg, [r_dbg], [r_dbg])
            elif stage == 4:
                stg = A.alloc([128, 8, TOK], F32)
                P.op("dve", lambda e: e.tensor_copy(stg, obT), reads=[r for hh in range(8) for r in r_obT[hh]], writes=[r_dbg])
                dma("sp", dbg_out.rearrange("(c p) t -> p c t", p=128), stg, [r_dbg], [r_dbg])
            elif stage in (5, 6):
                for b in range(NB):
                    dma("sp", dbg_out[b * 128:(b + 1) * 128, :], y[:, b, :], r_y[b], [r_dbg])
            P.final_wait(r_dbg)
        P.emit()
    return nc


def _rel_bucket(dist):
    n = np.maximum(dist, 0)
    exact = 16
    logv = np.log(np.maximum(n, 1).astype(np.float32) / np.float32(exact)) / np.float32(math.log(128 / 16))
    large = exact + (logv * np.float32(32 - exact)).astype(np.int32)
    large = np.minimum(large, 31)
    return np.where(n < exact, n, large)


def _bias_tables(table, core):
    k = np.arange(128)[:, None]
    q = np.arange(128)[None, :]
    sw = np.full((2, 128, 16, 2, 128), NEG, np.float32)
    for kap in range(2):
        dist = (128 + q - k) if kap == 0 else (q - k)
        ok = (dist >= 0) & (dist < 128)
        bk = _rel_bucket(dist)
        for h in range(16):
            vals = np.where(ok, table[bk, h], np.float32(NEG)).astype(np.float32)
            sw[0, :, h, kap, :] = vals
            if kap == 1:
                sw[1, :, h, kap, :] = vals
    if core != 0:
        sw[1] = sw[0]
    perm = [8 * g + 2 * i4 + r for g in range(2) for r in range(2) for i4 in range(4)]
    sw = sw[:, :, perm]
    q5 = np.arange(512)[None, :]
    near = np.full((2, 8, 128, 5, 512), NEG, np.float32)
    for kap in range(5):
        dist = q5 - 128 * (kap - 1) - k
        ok = dist >= 0
        bk = _rel_bucket(dist)
        for h in range(8):
            vals = np.where(ok, table[bk, 16 + h], np.float32(NEG)).astype(np.float32)
            near[:, h, :, kap, :] = vals[None]
    if core == 0:
        near[0, :, :, 0, :] = NEG
    far = np.full((128, 2, 64, 8), NEG, np.float32)
    for ch in range(2):
        g0 = 4 * (core if ch == 0 else 15 - core)
        for kb in range(64):
            if kb <= g0 - 2:
                far[:, ch, kb, :] = table[31, 16:24][None, :]
    return sw.reshape(2, 128, -1), near.reshape(2, 8, 128, -1), far.reshape(128, -1)


_NC_CACHE = {}


def kernel(x, mem, rel_bias_table, w_in, sinks, lambda_q1, lambda_k1, lambda_q2, lambda_k2,
           subln_w, w_branch_a, w_branch_b, w_o, ln1_g, ln1_b, w_cq, w_mem_kv, w_co,
           ln2_g, ln2_b, w_gate_up, w_down, ln3_g, ln3_b, _stage=99, _dbg=None):
    f = lambda a: np.ascontiguousarray(np.asarray(a, dtype=np.float32))
    x2 = f(x).reshape(S, D)
    table = f(rel_bias_table)
    lnp = np.stack([f(ln1_g)[0], f(ln1_b)[0], f(ln2_g)[0], f(ln2_b)[0], f(ln3_g)[0], f(ln3_b)[0]], 0)
    lnp = np.ascontiguousarray(np.broadcast_to(lnp[None], (128, 6, D)))
    perm = [8 * g + 2 * i4 + r for g in range(2) for r in range(2) for i4 in range(4)]
    small = np.concatenate([f(sinks)[0][perm], f(lambda_q1)[0], f(lambda_k1)[0], f(lambda_q2)[0], f(lambda_k2)[0], f(subln_w)[0]])
    small = np.ascontiguousarray(np.broadcast_to(small[None], (128, small.shape[0])))
    shared = {
        "x_all": x2, "mem": f(mem)[0], "w_in": f(w_in)[0], "w_branch_a": f(w_branch_a)[0], "w_branch_b": f(w_branch_b)[0],
        "w_o": f(w_o)[0], "w_cq": f(w_cq)[0], "w_mem_kv": f(w_mem_kv)[0], "w_co": f(w_co)[0],
        "w_gate_up": f(w_gate_up)[0], "w_down": f(w_down)[0], "lnp": lnp, "smallp": small,
        "ident": np.eye(128, dtype=np.float32),
    }
    in_maps = []
    for c in range(NCORES):
        xo = np.zeros((NOWN, D), np.float32)
        for ch, gc in enumerate((c, 15 - c)):
            r0 = ch * 640
            if gc > 0:
                xo[r0:r0 + 128] = x2[gc * 512 - 128:gc * 512]
            xo[r0 + 128:r0 + 640] = x2[gc * 512:(gc + 1) * 512]
        sw, near, far = _bias_tables(table, c)
        m = dict(shared)
        m.update({"x_own": xo, "bias_swa": sw, "bias_near": near, "bias_far": far})
        in_maps.append(m)
    key = (_stage, _dbg)
    if key not in _NC_CACHE:
        _NC_CACHE[key] = build(_stage, _dbg)
    nc = _NC_CACHE[key]
    res = run_bass_kernel_spmd(nc, in_maps, core_ids=list(range(NCORES)))
    if _dbg is not None:
        return np.concatenate([r["dbg"] for r in res.results], axis=0)
    o = np.empty((S, D), np.float32)
    for c in range(NCORES):
        oc = res.results[c]["out"]
        o[c * 512:(c + 1) * 512] = oc[0:512]
        o[(15 - c) * 512:(16 - c) * 512] = oc[512:1024]
    return o.reshape(1, S, D)
```

```python
import math
import contextlib
import numpy as np
import concourse.bass as bass
import concourse.mybir as mybir
from concourse.bass_utils import run_bass_kernel_spmd

F32 = mybir.dt.float32
BF16 = mybir.dt.bfloat16
AF = mybir.ActivationFunctionType
ALU = mybir.AluOpType
AX = mybir.AxisListType

NCORES = 8
S = 8192
D = 2048
TOK = S // NCORES
NB = TOK // 128
DFF = 5632
NEG = -30000.0
ALPHA = 2.0 ** 0.25
LAMBDA_INIT = 0.8 - 0.6 * math.exp(0.0)
EPS = 1e-5
C_QA, C_KA, C_VA, C_QB, C_KB, C_VB, C_GA, C_GB = 0, 1024, 1152, 1280, 2304, 3328, 4352, 6400
NOWN = 1280
NTOKX = S + NOWN
OWN0 = (128, 768)
NFAR = (27, 59)


def LB(j):
    return j + 1 if j < 4 else j + 2


class Region:
    __slots__ = ("name", "last_w", "readers")

    def __init__(self, name):
        self.name = name
        self.last_w = None
        self.readers = []


class Op:
    __slots__ = ("idx", "eng", "fn", "deps", "dma", "token", "signal", "prewait")

    def __init__(self, idx, eng, fn, dma):
        self.idx = idx
        self.eng = eng
        self.fn = fn
        self.dma = dma
        self.deps = set()
        self.token = None
        self.signal = False
        self.prewait = None


class Prog:
    ENGS = ("pe", "act", "dve", "pool", "sp")
    NDMA_SEM = 8

    def __init__(self, nc, same_engine_sync=True):
        self.nc = nc
        self.ops = []
        self.same_engine_sync = same_engine_sync
        self.finals = []
        self.barrier_idx = None
        self.last_on = {}
        self.dma_since = []

    def R(self, name=""):
        return Region(name)

    def Rs(self, n, name=""):
        return [Region(name) for _ in range(n)]

    def op(self, eng, fn, reads=(), writes=(), dma=False):
        o = Op(len(self.ops), eng, fn, dma)
        for r in reads:
            if r.last_w is not None:
                o.deps.add(r.last_w)
        for w in writes:
            if w.last_w is not None:
                o.deps.add(w.last_w)
            for rd in w.readers:
                o.deps.add(rd)
        for r in reads:
            r.readers.append(o.idx)
        for w in writes:
            w.last_w = o.idx
            w.readers = []
        if self.barrier_idx is not None:
            o.deps.add(self.barrier_idx)
        o.deps.discard(o.idx)
        self.ops.append(o)
        if dma:
            self.dma_since.append(o.idx)
        else:
            self.last_on[eng] = o.idx
        return o

    def barrier(self, scratch):
        o = Op(len(self.ops), "dve", lambda e: e.memset(scratch, 0.0), False)
        for e, i in self.last_on.items():
            o.deps.add(i)
        for i in self.dma_since:
            o.deps.add(i)
        self.dma_since = []
        if self.barrier_idx is not None:
            o.deps.add(self.barrier_idx)
        self.ops.append(o)
        self.last_on["dve"] = o.idx
        self.barrier_idx = o.idx
        return o

    def final_wait(self, region):
        self.finals.append(region.last_w)

    def _needs_sync(self, dep, ename):
        if dep.eng == ename and not dep.dma:
            if ename == "pe" or not self.same_engine_sync:
                return False
        return True

    def emit(self):
        nc = self.nc
        ops = self.ops
        for o in ops:
            for d in o.deps:
                dep = ops[d]
                if self._needs_sync(dep, o.eng):
                    dep.signal = True
        for f in self.finals:
            ops[f].signal = True
        for o in ops:
            if o.dma:
                o.signal = True
        with contextlib.ExitStack() as st:
            esem = {e: st.enter_context(nc.semaphore(f"s_{e}")) for e in ("pe", "act", "dve", "pool")}
            dsem = {q: [st.enter_context(nc.semaphore(f"d_{q}{k}")) for k in range(self.NDMA_SEM)]
                    for q in ("sp", "act", "pool")}
            cnt = {e: 0 for e in esem}
            dcnt = {q: 0 for q in dsem}
            for o in ops:
                if o.dma:
                    i = dcnt[o.eng]
                    dcnt[o.eng] += 1
                    k = i % self.NDMA_SEM
                    m = i // self.NDMA_SEM
                    o.token = (dsem[o.eng][k], 16 * (m + 1))
                    if m > 0:
                        o.prewait = (dsem[o.eng][k], 16 * m)
                elif o.signal:
                    cnt[o.eng] += 1
                    o.token = (esem[o.eng], cnt[o.eng])
            self.sem_counts = dict(cnt)
            block = st.enter_context(nc.Block())
            engobj = {"pe": "tensor", "act": "scalar", "dve": "vector", "pool": "gpsimd", "sp": "sync"}

            def run(ename, eng):
                seen = {}

                def wait(tok):
                    sem, v = tok
                    if seen.get(sem.num, 0) >= v:
                        return
                    eng.wait_ge(sem, v)
                    seen[sem.num] = v

                for o in ops:
                    if o.eng != ename:
                        continue
                    for d in sorted(o.deps, reverse=True):
                        dep = ops[d]
                        if not self._needs_sync(dep, ename):
                            continue
                        wait(dep.token)
                    if o.prewait is not None:
                        wait(o.prewait)
                    ins = o.fn(eng)
                    if o.token is not None:
                        sem, v = o.token
                        ins.then_inc(sem, 16 if o.dma else 1)
                if ename == "sp":
                    for f in self.finals:
                        wait(ops[f].token)

            for ename in self.ENGS:
                getattr(block, engobj[ename])(lambda eng, _e=ename: run(_e, eng))


class Arena:
    def __init__(self, t, nbytes):
        self.t = t
        self.cap = nbytes
        self.off = 0

    def alloc(self, shape, dtype):
        esz = 2 if dtype == BF16 else 4
        n = int(np.prod(shape[1:])) * esz
        off = (self.off + 63) // 64 * 64
        assert off + n <= self.cap, f"SBUF arena overflow: {off + n} > {self.cap}"
        self.off = off + n
        ap = self.t[:, off // 2:(off + n) // 2]
        if dtype != BF16:
            ap = ap.bitcast(dtype)
        if len(shape) == 3:
            ap = ap.rearrange("p (a b) -> p a b", a=shape[1])
        elif len(shape) == 4:
            ap = ap.rearrange("p (a b c) -> p a b c", a=shape[1], b=shape[2])
        elif len(shape) == 5:
            ap = ap.rearrange("p (a b c d) -> p a b c d", a=shape[1], b=shape[2], c=shape[3])
        return ap

    def mark(self):
        return self.off

    def reset(self, m):
        self.off = m


def build(stage=99, dbg=None):
    nc = bass.Bass("TRN2", target_bir_lowering=False)

    def din(name, shape, dt=F32):
        return nc.dram_tensor(name, shape, dt, kind="ExternalInput").ap()

    x_all = din("x_all", [S, D])
    x_own = din("x_own", [NOWN, D])
    mem = din("mem", [256, D])
    w_in = din("w_in", [D, 8448])
    w_a = din("w_branch_a", [1024, D])
    w_b = din("w_branch_b", [1024, D])
    w_o = din("w_o", [D, D])
    w_cq = din("w_cq", [D, 512])
    w_mkv = din("w_mem_kv", [D, 1024])
    w_co = din("w_co", [512, D])
    w_gu = din("w_gate_up", [D, 2 * DFF])
    w_dn = din("w_down", [DFF, D])
    lnp = din("lnp", [128, 6, D])
    smallp = din("smallp", [128, 16 + 256 + 128])
    ident_d = din("ident", [128, 128])
    bias_far = din("bias_far", [128, 2 * 64 * 8])
    bias_near = din("bias_near", [2, 8, 128, 5 * 512])
    bias_swa = din("bias_swa", [2, 128, 16 * 2 * 128])
    out = nc.dram_tensor("out", [TOK, D], F32, kind="ExternalOutput").ap()
    kt_all = nc.dram_tensor("kt_all", [8, 128, NTOKX], BF16).ap()
    v_all = nc.dram_tensor("v_all", [74, 128, 1024], BF16).ap()
    dbg_out = None
    if dbg is not None:
        dbg_out = nc.dram_tensor("dbg", [TOK, dbg], F32, kind="ExternalOutput").ap()

    P = Prog(nc)
    with contextlib.ExitStack() as st:
        ARENA_BYTES = 207 * 1024
        arena_t = st.enter_context(nc.sbuf_tensor("arena", [128, ARENA_BYTES // 2], BF16))
        A = Arena(arena_t, ARENA_BYTES)
        pp = [st.enter_context(nc.psum_tensor(f"pp{i}", [128, 1024], F32)) for i in range(4)]

        def bank(b):
            return pp[b // 2][:, (b % 2) * 512:(b % 2 + 1) * 512]

        rbank = P.Rs(8, "bank")
        cpctr = [0]

        def cp(out_ap, in_ap, reads, writes, eng=None):
            if eng is None:
                eng = ("dve", "act")[cpctr[0] % 2]
                cpctr[0] += 1
            if eng == "act":
                return P.op("act", lambda e: e.activation(out_ap, in_ap, AF.Copy), reads=reads, writes=writes)
            if eng == "pool":
                return P.op("pool", lambda e: e.tensor_copy(out_ap, in_ap), reads=reads, writes=writes)
            return P.op("dve", lambda e: e.tensor_copy(out_ap, in_ap), reads=reads, writes=writes)

        def mm(out_ap, lhsT, rhs, start, stop, reads, writes):
            return P.op("pe", lambda e: e.matmul(out_ap, lhsT, rhs, start=start, stop=stop),
                        reads=reads, writes=writes)

        def dma(q, out_ap, in_ap, reads, writes):
            return P.op(q, lambda e: e.dma_start(out=out_ap, in_=in_ap), reads=reads, writes=writes, dma=True)

        idb = A.alloc([128, 128], BF16)
        idf = A.alloc([128, 128], F32)
        small = A.alloc([128, 400], F32)
        scratch = A.alloc([128, 16], F32)
        esink = A.alloc([128, 16], F32)
        neglam = A.alloc([128, 1], F32)
        wsub = A.alloc([128, 128], F32)
        lam_t = A.alloc([128, 8], F32)
        epsc = A.alloc([128, 1], F32)
        r_idb, r_idf, r_small, r_const = P.R(), P.R(), P.R(), P.R()
        dma("pool", idb, ident_d, [], [r_idb])
        dma("sp", idf, ident_d, [], [r_idf])
        dma("sp", small, smallp, [], [r_small])
        P.op("dve", lambda e: e.memset(epsc, EPS), writes=[r_const])
        P.op("act", lambda e: e.activation(esink, small[:, 0:16], AF.Exp), reads=[r_small], writes=[r_const])
        lq = small[:, 16:272].rearrange("p (a b) -> p a b", a=4)
        prod_t = A.alloc([128, 2, 64], F32)
        P.op("dve", lambda e: e.tensor_tensor(prod_t[:, 0, :], lq[:, 0, :], lq[:, 1, :], ALU.mult), reads=[r_small], writes=[r_const])
        P.op("dve", lambda e: e.tensor_tensor(prod_t[:, 1, :], lq[:, 2, :], lq[:, 3, :], ALU.mult), reads=[r_small], writes=[r_const])
        P.op("dve", lambda e: e.reduce_sum(lam_t[:, 0:2], prod_t, axis=AX.X), reads=[r_const], writes=[r_const])
        P.op("act", lambda e: e.activation(lam_t[:, 2:4], lam_t[:, 0:2], AF.Exp), reads=[r_const], writes=[r_const])
        P.op("dve", lambda e: e.tensor_tensor(lam_t[:, 4:5], lam_t[:, 3:4], lam_t[:, 2:3], ALU.subtract), reads=[r_const], writes=[r_const])
        P.op("dve", lambda e: e.tensor_scalar_add(neglam, lam_t[:, 4:5], -LAMBDA_INIT), reads=[r_const], writes=[r_const])
        P.op("dve", lambda e: e.tensor_scalar_mul(wsub, small[:, 272:400], (1.0 - LAMBDA_INIT)), reads=[r_small], writes=[r_const])
        assert A.mark() <= 4 * 1024
        M0 = 4 * 1024
        A.reset(M0)

        xT = A.alloc([128, 16, NOWN], BF16)
        r_xT = [[P.R() for _ in range(2)] for _ in range(10)]
        after_xT = A.mark()
        QbT = A.alloc([128, 8, TOK], BF16)
        oaT = A.alloc([128, 8, TOK], BF16)
        obT = A.alloc([128, 8, TOK], BF16)
        xT_mark = A.mark()
        assert xT_mark <= 93 * 1024
        A.reset(after_xT)

        Wkb = A.alloc([128, 16, 1024], BF16)
        Wvb = A.alloc([128, 16, 1024], BF16)
        r_Wkb, r_Wvb = P.Rs(2), P.Rs(2)
        for hlf in range(2):
            dma("pool", Wkb[:, :, hlf * 512:(hlf + 1) * 512],
                w_in[:, C_KB + hlf * 512:C_KB + (hlf + 1) * 512].rearrange("(c p) n -> p c n", p=128), [], [r_Wkb[hlf]])
        for hlf in range(2):
            dma("pool", Wvb[:, :, hlf * 512:(hlf + 1) * 512],
                w_in[:, C_VB + hlf * 512:C_VB + (hlf + 1) * 512].rearrange("(c p) n -> p c n", p=128), [], [r_Wvb[hlf]])
        xs = [A.alloc([128, 4, D], BF16) for _ in range(2)]
        xTs = [A.alloc([128, 16, 512], BF16) for _ in range(2)]
        KTs = [A.alloc([128, 8, 512], BF16) for _ in range(2)]
        Vs = [A.alloc([128, 4, 1024], BF16) for _ in range(2)]
        r_xs = P.Rs(2)
        r_xTs = [[[P.R() for _ in range(2)] for _ in range(4)] for _ in range(2)]
        r_KTs = [P.Rs(8) for _ in range(2)]
        r_Vs = [[[P.R() for _ in range(2)] for _ in range(4)] for _ in range(2)]
        r_ktall, r_vall = P.R(), P.R()
        tb = [0]
        mb = [0]

        def next_tbank():
            b = 6 + tb[0] % 2
            tb[0] += 1
            return b

        def next_mbank(n=6):
            b = mb[0] % n
            mb[0] += 1
            return b

        def transpose_block_bf16(src_tok_major, dst_fn, rsrc, rdst_fn):
            for hlf in range(2):
                b = next_tbank()
                pb = bank(b).bitcast(BF16)
                for c in range(8):
                    cc = hlf * 8 + c
                    P.op("pe", lambda e, pb=pb, c=c, cc=cc: e.transpose(pb[:, c * 128:(c + 1) * 128],
                                                                       src_tok_major[:, cc * 128:(cc + 1) * 128], idb),
                         reads=[rsrc, r_idb], writes=[rbank[b]])
                cp(dst_fn(hlf), pb.rearrange("p (c n) -> p c n", c=8), [rbank[b]], [rdst_fn(hlf)])

        NG = 19
        for g in range(16 if (dbg is not None and stage < 4) else 0, NG if stage >= 1 else 0):
            if g == 15:
                continue
            par = g % 2
            nb = 4 if g < 18 else 2
            ntok = nb * 128
            if g < 16:
                src = x_all[g * 512:(g + 1) * 512, :]
            else:
                src = x_own[(g - 16) * 512:(g - 16) * 512 + ntok, :]
            dma("pool", xs[par][:, 0:nb, :], src.rearrange("(b p) d -> p b d", p=128), [], [r_xs[par]])
            for b in range(nb):
                if g < 16:
                    dst_fn = lambda hlf, b=b: xTs[par][:, hlf * 8:(hlf + 1) * 8, b * 128:(b + 1) * 128]
                    rdst_fn = lambda hlf, b=b: r_xTs[par][b][hlf]
                else:
                    lb = (g - 16) * 4 + b
                    dst_fn = lambda hlf, lb=lb: xT[:, hlf * 8:(hlf + 1) * 8, lb * 128:(lb + 1) * 128]
                    rdst_fn = lambda hlf, lb=lb: r_xT[lb][hlf]
                transpose_block_bf16(xs[par][:, b, :], dst_fn, r_xs[par], rdst_fn)
            if g < 16:
                xsrc = xTs[par]
                t0 = 0
                rx = lambda b, hlf: r_xTs[par][b][hlf]
            else:
                xsrc = xT
                t0 = (g - 16) * 512
                rx = lambda b, hlf, g=g: r_xT[(g - 16) * 4 + b][hlf]
            for h in range(8):
                bk = next_mbank()
                for c in range(16):
                    mm(bank(bk)[:, 0:ntok], Wkb[:, c, h * 128:(h + 1) * 128], xsrc[:, c, t0:t0 + ntok], c == 0, c == 15,
                       [r_Wkb[h // 4]] + [rx(b, c // 8) for b in range(nb)], [rbank[bk]])
                cp(KTs[par][:, h, 0:ntok], bank(bk)[:, 0:ntok], [rbank[bk]], [r_KTs[par][h]])
            dma("sp", kt_all[:, :, g * 512:g * 512 + ntok].rearrange("h p n -> p h n"), KTs[par][:, :, 0:ntok],
                r_KTs[par], [r_ktall])
            for b in range(nb):
                for hlf in range(2):
                    bk = next_mbank()
                    for c in range(16):
                        mm(bank(bk), xsrc[:, c, t0 + b * 128:t0 + (b + 1) * 128], Wvb[:, c, hlf * 512:(hlf + 1) * 512], c == 0, c == 15,
                           [r_Wvb[hlf], rx(b, c // 8)], [rbank[bk]])
                    cp(Vs[par][:, b, hlf * 512:(hlf + 1) * 512], bank(bk), [rbank[bk]], [r_Vs[par][b][hlf]])
            dma("sp", v_all[g * 4:g * 4 + nb, :, :].rearrange("b p n -> p b n"), Vs[par][:, 0:nb, :],
                [r for bb in range(nb) for r in r_Vs[par][bb]], [r_vall])
        P.barrier(scratch)
        A.reset(xT_mark)

        QaT = A.alloc([128, 8, TOK], BF16)
        KaT = A.alloc([128, 2, NOWN], BF16)
        Va = A.alloc([128, 10, 2, 65], BF16)
        r_QaT = [P.Rs(2) for _ in range(8)]
        r_QbT = [P.Rs(2) for _ in range(8)]
        r_KaT = [P.Rs(3) for _ in range(2)]
        r_Va = P.Rs(10)
        r_Vaones = P.R()
        r_oaT = P.Rs(8)
        r_obT = [P.Rs(8) for _ in range(8)]
        proj_mark = A.mark()
        allx = [r_xT[b][hh] for b in range(10) for hh in range(2)]

        if stage >= 2:
            Wt = [A.alloc([128, 16, 512], BF16) for _ in range(2)]
            r_Wt = P.Rs(2)
            wi = [0]

            def load_w(col0, ncols=512):
                k = wi[0] % 2
                wi[0] += 1
                dma("pool", Wt[k][:, :, 0:ncols], w_in[:, col0:col0 + ncols].rearrange("(c p) n -> p c n", p=128), [], [r_Wt[k]])
                return Wt[k], r_Wt[k]

            for (col0, dst, rdst) in ((C_QA, QaT, r_QaT), (C_QB, QbT, r_QbT)):
                for pn in range(2):
                    W, rW = load_w(col0 + pn * 512)
                    for ii in range(4):
                        i = pn * 4 + ii
                        for hlf in range(2):
                            bk = next_mbank()
                            for c in range(16):
                                mm(bank(bk), W[:, c, ii * 128:(ii + 1) * 128], xT[:, c, OWN0[hlf]:OWN0[hlf] + 512],
                                   c == 0, c == 15, [rW] + allx, [rbank[bk]])
                            cp(dst[:, i, hlf * 512:(hlf + 1) * 512], bank(bk), [rbank[bk]], [rdst[i][hlf]])
            k = wi[0] % 2
            wi[0] += 1
            Wka = Wt[k][:, :, 0:256].rearrange("p c (g u d) -> p c g u d", g=2, u=2)
            for u in range(2):
                for g in range(2):
                    dma("pool", Wka[:, :, g, u, :], w_in[:, C_KA + g * 64:C_KA + (g + 1) * 64].rearrange("(c p) d -> p c d", p=128), [], [r_Wt[k]])
            Wva = Wt[k][:, :, 256:384]
            dma("pool", Wva, w_in[:, C_VA:C_VA + 128].rearrange("(c p) n -> p c n", p=128), [], [r_Wt[k]])
            for g in range(2):
                for pi, (t0, nt) in enumerate(((0, 512), (512, 512), (1024, 256))):
                    bk = next_mbank()
                    for c in range(16):
                        mm(bank(bk)[:, 0:nt], Wt[k][:, c, g * 128:(g + 1) * 128], xT[:, c, t0:t0 + nt], c == 0, c == 15,
                           [r_Wt[k]] + allx, [rbank[bk]])
                    cp(KaT[:, g, t0:t0 + nt], bank(bk)[:, 0:nt], [rbank[bk]], [r_KaT[g][pi]])
            P.op("pool", lambda e: e.memset(Va[:, :, :, 64:65], 1.0), writes=[r_Vaones])
            for b in range(10):
                bk = next_mbank()
                for c in range(16):
                    mm(bank(bk)[:, 0:128], xT[:, c, b * 128:(b + 1) * 128], Wva[:, c, :], c == 0, c == 15,
                       [r_Wt[k], r_xT[b][c // 8]], [rbank[bk]])
                cp(Va[:, b, :, 0:64], bank(bk)[:, 0:128].rearrange("p (g d) -> p g d", g=2), [rbank[bk]], [r_Va[b]])
        P.barrier(scratch)
        A.reset(proj_mark)

        def pipeline(nitems, stages):
            ns = len(stages)
            for tau in range(nitems + ns - 1):
                for si in range(ns - 1, -1, -1):
                    i = tau - si
                    if 0 <= i < nitems:
                        stages[si](i)

        if stage >= 3:
            Bsw = [A.alloc([128, 4, 4, 256], F32) for _ in range(2)]
            r_Bsw = P.Rs(2)
            tmpS = [A.alloc([128, 4, 128], F32) for _ in range(4)]
            r_tmpS = P.Rs(4)
            PTs = [A.alloc([128, 4, 128], BF16) for _ in range(8)]
            r_PTs = P.Rs(8)
            oa = [A.alloc([128, 1024], BF16) for _ in range(2)]
            r_oa = [P.Rs(16) for _ in range(2)]
            den = [A.alloc([128, 8], F32) for _ in range(4)]
            r_den = P.Rs(4)
            r_rden = P.Rs(4)
            allKa = [r for g in range(2) for r in r_KaT[g]]
            def unpack(i):
                j, gr = divmod(i, 4)
                g, r = divmod(gr, 2)
                return j, g, r

            def sQ(i):
                j, g, r = unpack(i)
                bp = j % 2
                if i % 4 == 0:
                    dma("sp", Bsw[bp].rearrange("p a b c -> p (a b c)"), bias_swa[1 if j == 0 else 0], [], [r_Bsw[bp]])
                for kap in range(2):
                    blk = LB(j) - 1 + kap
                    bk = (i % 2) * 2 + kap
                    mm(bank(bk).rearrange("p (h q) -> p h q", h=4), KaT[r * 64:(r + 1) * 64, g, blk * 128:(blk + 1) * 128],
                       QaT[r * 64:(r + 1) * 64, 4 * g:4 * g + 4, j * 128:(j + 1) * 128], True, True,
                       allKa + [r_QaT[ii][j // 4] for ii in range(4 * g, 4 * g + 4)], [rbank[bk]])

            def sA(i):
                j, g, r = unpack(i)
                bp = j % 2
                for kap in range(2):
                    bk = (i % 2) * 2 + kap
                    tk = (i % 2) * 2 + kap
                    pk = (i % 4) * 2 + kap
                    P.op("dve", lambda e, tk=tk, bk=bk, kap=kap, g=g, r=r, bp=bp: e.scalar_tensor_tensor(
                        tmpS[tk], bank(bk).rearrange("p (h q) -> p h q", h=4), 0.125,
                        Bsw[bp][:, g * 2 + r, :, kap * 128:(kap + 1) * 128], ALU.mult, ALU.add),
                        reads=[rbank[bk], r_Bsw[bp]], writes=[r_tmpS[tk]])
                    P.op("act", lambda e, tk=tk, pk=pk: e.activation(PTs[pk], tmpS[tk], AF.Exp),
                         reads=[r_tmpS[tk]], writes=[r_PTs[pk]])

            def sB(i):
                j, g, r = unpack(i)
                ab = 4 + i % 2
                acc = bank(ab).rearrange("p (h c) -> p h c", h=4)
                for hh in range(4):
                    for kap in range(2):
                        blk = LB(j) - 1 + kap
                        pk = (i % 4) * 2 + kap
                        mm(acc[:, hh, 0:65], PTs[pk][:, hh, :], Va[:, blk, g, :], kap == 0, kap == 1,
                           [r_PTs[pk], r_Va[blk], r_Vaones], [rbank[ab]])

            def sC(i):
                j, g, r = unpack(i)
                ab = 4 + i % 2
                acc = bank(ab).rearrange("p (h c) -> p h c", h=4)
                dp = i % 4
                hsl = slice((g * 2 + r) * 4, (g * 2 + r) * 4 + 4)
                P.op("dve", lambda e, dp=dp, acc=acc, hsl=hsl: e.tensor_tensor(den[dp][:, 0:4], acc[:, :, 64], esink[:, hsl], ALU.add),
                     reads=[rbank[ab], r_const], writes=[r_den[dp]])
                P.op("dve", lambda e, dp=dp: e.reciprocal(den[dp][:, 4:8], den[dp][:, 0:4]),
                     reads=[r_den[dp]], writes=[r_rden[dp]])

            def sD(i):
                j, g, r = unpack(i)
                bp = j % 2
                ab = 4 + i % 2
                acc = bank(ab).rearrange("p (h c) -> p h c", h=4)
                dp = i % 4
                heads = [2 * ii + r for ii in range(4 * g, 4 * g + 4)]
                for hh in range(4):
                    h = heads[hh]
                    P.op("dve", lambda e, dp=dp, acc=acc, hh=hh, h=h, bp=bp: e.tensor_scalar_mul(
                        oa[bp][:, h * 64:(h + 1) * 64], acc[:, hh, 0:64], den[dp][:, 4 + hh:5 + hh]),
                        reads=[rbank[ab], r_rden[dp]], writes=[r_oa[bp][(g * 2 + r) * 4 + hh]])
                if i % 4 == 3:
                    b = next_tbank()
                    pb = bank(b).bitcast(BF16)
                    for c in range(8):
                        P.op("pe", lambda e, pb=pb, c=c, bp=bp: e.transpose(pb[:, c * 128:(c + 1) * 128], oa[bp][:, c * 128:(c + 1) * 128], idb),
                             reads=r_oa[bp] + [r_idb], writes=[rbank[b]])
                    cp(oaT[:, :, j * 128:(j + 1) * 128], pb.rearrange("p (c n) -> p c n", c=8), [rbank[b]], [r_oaT[j]])

            pipeline(NB * 4, [sQ, sA, lambda i: None, sB, sC, sD])
        P.barrier(scratch)
        A.reset(xT_mark)

        if stage >= 4:
            KTh = [A.alloc([128, NTOKX], BF16) for _ in range(2)]
            Vh = [A.alloc([128, 74, 129], BF16) for _ in range(2)]
            r_KTh, r_Vh, r_Vhones = P.Rs(2), P.Rs(2), P.Rs(2)
            Bn = A.alloc([128, 5, 512], F32)
            r_Bn = P.R()
            Bf = A.alloc([128, 2, 64, 8], F32)
            r_Bf = P.R()
            tmpD = [A.alloc([128, 2, 512], F32) for _ in range(2)]
            r_tmpD = P.Rs(2)
            PT = [A.alloc([128, 2, 512], BF16) for _ in range(2)]
            r_PT = P.Rs(2)
            accS = A.alloc([128, 3, 512], F32)
            r_accS = P.Rs(3)
            o0 = [A.alloc([128, 128], F32) for _ in range(4)]
            o1 = [A.alloc([128, 128], F32) for _ in range(4)]
            obk = A.alloc([128, 4, 128], BF16)
            st4 = [A.alloc([128, 8], F32) for _ in range(4)]
            r_r0, r_r1, r_r1l, r_o0, r_o1, r_ss, r_ln, r_rstd, r_obk = [P.Rs(4) for _ in range(9)]
            dma("sp", Bf.rearrange("p a b c -> p (a b c)"), bias_far, [], [r_Bf])
            zt = A.alloc([128, 512], BF16)
            r_zt = P.R()
            P.op("pool", lambda e: e.memset(zt, 0.0), writes=[r_zt])
            for k in range(2):
                P.op("pool", lambda e, k=k: e.memset(Vh[k][:, :, 128:129], 1.0), writes=[r_Vhones[k]])
            def accap(a):
                return bank(4 + a // 3)[:, (a % 3) * 160:(a % 3) * 160 + 129]

            def accS_ap(a):
                return accS[:, a // 3, (a % 3) * 160:(a % 3) * 160 + 129]

            def make_stages(h, ch):
                def g0():
                    for qb in range(4):
                        s4, a0, a1 = st4[qb], accS_ap(qb), accS_ap(4 + qb)
                        P.op("dve", lambda e, s4=s4, a0=a0: e.reciprocal(s4[:, 0:1], a0[:, 128:129]),
                             reads=[r_accS[qb // 3]], writes=[r_r0[qb]])
                        P.op("dve", lambda e, s4=s4, a1=a1: e.reciprocal(s4[:, 1:2], a1[:, 128:129]),
                             reads=[r_accS[(4 + qb) // 3]], writes=[r_r1[qb]])

                def g1():
                    for qb in range(4):
                        s4, a0 = st4[qb], accS_ap(qb)
                        P.op("dve", lambda e, s4=s4: e.tensor_tensor(s4[:, 2:3], s4[:, 1:2], neglam, ALU.mult),
                             reads=[r_r1[qb], r_const], writes=[r_r1l[qb]])
                        P.op("dve", lambda e, s4=s4, a0=a0, qb=qb: e.tensor_scalar_mul(o0[qb], a0[:, 0:128], s4[:, 0:1]),
                             reads=[r_accS[qb // 3], r_r0[qb]], writes=[r_o0[qb]])

                def g2():
                    for qb in range(4):
                        s4, a1 = st4[qb], accS_ap(4 + qb)
                        P.op("dve", lambda e, s4=s4, a1=a1, qb=qb: e.scalar_tensor_tensor(o1[qb], a1[:, 0:128], s4[:, 2:3], o0[qb], ALU.mult, ALU.add),
                             reads=[r_accS[(4 + qb) // 3], r_r1l[qb], r_o0[qb]], writes=[r_o1[qb]])
                        P.op("dve", lambda e, s4=s4: e.memset(s4[:, 3:4], 0.0), writes=[r_ss[qb]])

                def g3():
                    for qb in range(4):
                        s4 = st4[qb]
                        P.op("act", lambda e, s4=s4, qb=qb: e.activation(o0[qb], o1[qb], AF.Square, accum_out=s4[:, 3:4]),
                             reads=[r_o1[qb]], writes=[r_ss[qb], r_o0[qb]])

                def g4():
                    for qb in range(4):
                        s4 = st4[qb]
                        P.op("act", lambda e, s4=s4: e.activation(s4[:, 4:5], s4[:, 3:4], AF.Ln, bias=epsc, scale=1.0 / 128.0),
                             reads=[r_ss[qb], r_const], writes=[r_ln[qb]])

                def g5():
                    for qb in range(4):
                        s4 = st4[qb]
                        P.op("act", lambda e, s4=s4: e.activation(s4[:, 5:6], s4[:, 4:5], AF.Exp, scale=-0.5),
                             reads=[r_ln[qb]], writes=[r_rstd[qb]])

                def g6():
                    for qb in range(4):
                        s4 = st4[qb]
                        P.op("dve", lambda e, s4=s4, qb=qb: e.scalar_tensor_tensor(obk[:, qb, :], o1[qb], s4[:, 5:6], wsub, ALU.mult, ALU.mult),
                             reads=[r_o1[qb], r_rstd[qb], r_const], writes=[r_obk[qb]])

                def g7():
                    pb = bank(7).bitcast(BF16)
                    for qb in range(4):
                        P.op("pe", lambda e, pb=pb, qb=qb: e.transpose(pb[:, qb * 128:(qb + 1) * 128], obk[:, qb, :], idb),
                             reads=[r_obk[qb], r_idb], writes=[rbank[7]])
                    cp(obT[:, h, ch * 512:(ch + 1) * 512], pb[:, 0:512], [rbank[7]], [r_obT[h][ch * 4 + qb] for qb in range(4)], eng="dve")

                return [g0, g1, g2, g3, g4, g5, g6, g7]

            sc = [pp[0][:], pp[1][:]]
            r_sc = P.Rs(2)
            r_acc = P.Rs(3)
            pending = []
            for h in range(8):
                hp = h % 2
                dma("sp", KTh[hp][:, 0:7680], kt_all[h][:, 0:7680], [r_ktall], [r_KTh[hp]])
                dma("sp", KTh[hp][:, S:NTOKX], kt_all[h][:, S:NTOKX], [r_ktall], [r_KTh[hp]])
                dma("sp", Vh[hp][:, 0:60, 0:128], v_all[0:60, :, h * 128:(h + 1) * 128].rearrange("b p n -> p b n"), [r_vall], [r_Vh[hp]])
                dma("sp", Vh[hp][:, 64:74, 0:128], v_all[64:74, :, h * 128:(h + 1) * 128].rearrange("b p n -> p b n"), [r_vall], [r_Vh[hp]])
                for ch in range(2):
                    dma("sp", Bn.rearrange("p a b -> p (a b)"), bias_near[ch, h], [], [r_Bn])
                    tiles = [("far", kb) for kb in range(NFAR[ch])] + [("near", kap) for kap in range(5)]
                    nt = len(tiles)
                    qsl = slice(ch * 512, (ch + 1) * 512)

                    def emit_qk(t):
                        kind, idx = tiles[t]
                        sp_ = t % 2
                        tok0 = idx * 128 if kind == "far" else S + (ch * 5 + idx) * 128
                        for m in range(2):
                            mm(sc[sp_][:, m * 512:(m + 1) * 512], KTh[hp][m * 64:(m + 1) * 64, tok0:tok0 + 128],
                               QbT[m * 64:(m + 1) * 64, h, qsl], True, True,
                               [r_KTh[hp], r_QbT[h][ch]], [r_sc[sp_]])

                    emit_qk(0)
                    emit_qk(1)
                    for ab in range(3):
                        mm(bank(4 + ab), zt[:, 0:128], zt, True, False, [r_zt], [r_acc[ab]])
                    for t in range(nt):
                        kind, idx = tiles[t]
                        sp_ = t % 2
                        if kind == "far":
                            bias_ap = Bf[:, ch, idx, h:h + 1]
                            P.op("act", lambda e, sp_=sp_, bias_ap=bias_ap: e.activation(
                                PT[sp_].rearrange("p a b -> p (a b)"), sc[sp_], AF.Exp, bias=bias_ap, scale=0.125),
                                reads=[r_sc[sp_], r_Bf], writes=[r_PT[sp_]])
                        else:
                            for m in range(2):
                                bn_ap = Bn[:, idx, :]
                                P.op("dve", lambda e, sp_=sp_, bn_ap=bn_ap, m=m: e.scalar_tensor_tensor(
                                    tmpD[sp_][:, m, :], sc[sp_][:, m * 512:(m + 1) * 512], 0.125, bn_ap, ALU.mult, ALU.add),
                                    reads=[r_sc[sp_], r_Bn], writes=[r_tmpD[sp_]])
                            P.op("act", lambda e, sp_=sp_: e.activation(PT[sp_], tmpD[sp_], AF.Exp),
                                 reads=[r_tmpD[sp_]], writes=[r_PT[sp_]])
                        if t + 2 < nt:
                            emit_qk(t + 2)
                        vb = idx if kind == "far" else 64 + ch * 5 + idx
                        for m in range(2):
                            for qb in range(4):
                                a = m * 4 + qb
                                mm(accap(a), PT[sp_][:, m, qb * 128:(qb + 1) * 128], Vh[hp][:, vb, :], False, (t == nt - 1) and a in (2, 5, 7),
                                   [r_PT[sp_], r_Vh[hp], r_Vhones[hp]], [r_acc[a // 3]])
                        if pending and t >= 2:
                            pending.pop(0)()
                    while pending:
                        pending.pop(0)()
                    for ab in range(3):
                        cp(accS[:, ab, :], bank(4 + ab), [r_acc[ab]], [r_accS[ab]], eng="dve")
                    pending = make_stages(h, ch)
            while pending:
                pending.pop(0)()
        P.barrier(scratch)
        A.reset(xT_mark)

        y = None

        def layer_norm_and_T(y, r_y, lnw, r_lnw, hT, r_hT, stt, final_out=None, r_out=None):
            r_st = [P.Rs(4) for _ in range(NB)]
            r_mv, r_rs, r_nm = P.Rs(NB), P.Rs(NB), P.Rs(NB)

            def s0(b):
                yb, s = y[:, b, :], stt[:, b, :]
                stats = s[:, 0:24].rearrange("p (k s) -> p k s", k=4)
                for k in range(4):
                    P.op("dve", lambda e, k=k, yb=yb, stats=stats: e.bn_stats(stats[:, k, :], yb[:, k * 512:(k + 1) * 512]),
                         reads=[r_y[b][k]], writes=[r_st[b][k]])
                P.op("dve", lambda e, s=s, stats=stats: e.bn_aggr(s[:, 24:26], stats), reads=r_st[b], writes=[r_mv[b]])

            def s1(b):
                s = stt[:, b, :]
                P.op("act", lambda e, s=s: e.activation(s[:, 27:28], s[:, 25:26], AF.Ln, bias=epsc), reads=[r_mv[b], r_const], writes=[r_rs[b]])
                P.op("act", lambda e, s=s: e.activation(s[:, 26:27], s[:, 27:28], AF.Exp, scale=-0.5), reads=[r_rs[b]], writes=[r_rs[b]])

            def s2(b):
                s = stt[:, b, :]
                P.op("dve", lambda e, s=s: e.scalar_tensor_tensor(s[:, 28:29], s[:, 24:25], -1.0, s[:, 26:27], ALU.mult, ALU.mult),
                     reads=[r_mv[b], r_rs[b]], writes=[r_nm[b]])

            def s3(b):
                yb, s = y[:, b, :], stt[:, b, :]
                P.op("act", lambda e, s=s, yb=yb: e.activation(yb, yb, AF.Identity, bias=s[:, 28:29], scale=s[:, 26:27]),
                     reads=[r_nm[b], r_rs[b]] + r_y[b], writes=r_y[b])

            def s4(b):
                yb = y[:, b, :]
                P.op("dve", lambda e, yb=yb: e.tensor_tensor(yb, yb, lnw[:, 0, :], ALU.mult), reads=[r_lnw] + r_y[b], writes=r_y[b])

            def s5(b):
                yb = y[:, b, :]
                P.op("pool", lambda e, yb=yb: e.tensor_tensor(yb, yb, lnw[:, 1, :], ALU.add), reads=[r_lnw] + r_y[b], writes=r_y[b])

            def s6(b):
                yb = y[:, b, :]
                if final_out is not None:
                    dma("sp", final_out[b * 128:(b + 1) * 128, :], yb, r_y[b], [r_out])
                    return
                for q4 in range(4):
                    bk = next_tbank()
                    for c in range(4):
                        cc = q4 * 4 + c
                        P.op("pe", lambda e, bk=bk, c=c, cc=cc, yb=yb: e.transpose(bank(bk)[:, c * 128:(c + 1) * 128], yb[:, cc * 128:(cc + 1) * 128], idf),
                             reads=r_y[b] + [r_idf], writes=[rbank[bk]])
                    cp(hT[:, q4 * 4:(q4 + 1) * 4, b * 128:(b + 1) * 128], bank(bk).rearrange("p (c n) -> p c n", c=4),
                       [rbank[bk]], [r_hT[b][q4]])

            pipeline(NB, [s0, s1, s2, s3, s4, s5, s6])

        if stage >= 5:
            A.reset(117 * 1024)
            mixT = A.alloc([128, 16, TOK], BF16)
            r_mixT = [P.Rs(2) for _ in range(16)]
            e_mark = A.mark()
            Wg = [A.alloc([128, 16, 512], BF16) for _ in range(2)]
            Wab = [A.alloc([128, 8, 512], BF16) for _ in range(2)]
            r_Wg, r_Wab = P.Rs(2), P.Rs(2)
            sg = [A.alloc([128, 512], F32) for _ in range(2)]
            r_sg = P.Rs(2)
            m1 = [A.alloc([128, 512], F32) for _ in range(2)]
            r_m1 = P.Rs(2)
            all_oaT = r_oaT
            all_obT = [r for hh in range(8) for r in r_obT[hh]]
            gi = 0
            for jj in range(8):
                k = jj % 2
                dma("pool", Wg[k][:, :, 0:256], w_in[:, C_GA + jj * 256:C_GA + (jj + 1) * 256].rearrange("(c p) n -> p c n", p=128), [], [r_Wg[k]])
                dma("pool", Wg[k][:, :, 256:512], w_in[:, C_GB + jj * 256:C_GB + (jj + 1) * 256].rearrange("(c p) n -> p c n", p=128), [], [r_Wg[k]])
                dma("pool", Wab[k][:, :, 0:256], w_a[:, jj * 256:(jj + 1) * 256].rearrange("(c p) n -> p c n", p=128), [], [r_Wab[k]])
                dma("pool", Wab[k][:, :, 256:512], w_b[:, jj * 256:(jj + 1) * 256].rearrange("(c p) n -> p c n", p=128), [], [r_Wab[k]])
                for sub in range(2):
                    j = jj * 2 + sub
                    for hlf in range(2):
                        tsl = slice(hlf * 512, (hlf + 1) * 512)
                        for br in range(2):
                            gk = gi % 2
                            gi += 1
                            bg = next_mbank()
                            for c in range(16):
                                mm(bank(bg), Wg[k][:, c, br * 256 + sub * 128:br * 256 + (sub + 1) * 128],
                                   xT[:, c, OWN0[hlf]:OWN0[hlf] + 512], c == 0, c == 15, [r_Wg[k]] + allx, [rbank[bg]])
                            P.op("act", lambda e, gk=gk, bg=bg: e.activation(sg[gk], bank(bg), AF.Sigmoid),
                                 reads=[rbank[bg]], writes=[r_sg[gk]])
                            bb = next_mbank()
                            srcT = oaT if br == 0 else obT
                            rsrc = all_oaT if br == 0 else all_obT
                            for c in range(8):
                                mm(bank(bb), Wab[k][:, c, br * 256 + sub * 128:br * 256 + (sub + 1) * 128], srcT[:, c, tsl],
                                   c == 0, c == 7, [r_Wab[k]] + rsrc, [rbank[bb]])
                            if br == 0:
                                mk = (gi // 2) % 2
                                P.op("dve", lambda e, mk=mk, bb=bb, gk=gk: e.tensor_tensor(m1[mk], bank(bb), sg[gk], ALU.mult),
                                     reads=[rbank[bb], r_sg[gk]], writes=[r_m1[mk]])
                                mk_a = mk
                            else:
                                P.op("dve", lambda e, bb=bb, gk=gk: e.tensor_tensor(sg[gk], bank(bb), sg[gk], ALU.mult),
                                     reads=[rbank[bb], r_sg[gk]], writes=[r_sg[gk]])
                                P.op("dve", lambda e, gk=gk, mk_a=mk_a, j=j, tsl=tsl: e.tensor_tensor(mixT[:, j, tsl], m1[mk_a], sg[gk], ALU.add),
                                     reads=[r_m1[mk_a], r_sg[gk]], writes=[r_mixT[j][hlf]])
            P.barrier(scratch)
            A.reset(M0)
            y = A.alloc([128, NB, D], F32)
            r_y = [P.Rs(4) for _ in range(NB)]
            lnw = A.alloc([128, 2, D], F32)
            r_lnw = P.R()
            stt = A.alloc([128, NB, 32], F32)
            hT = A.alloc([128, 16, TOK], BF16)
            r_hT = [P.Rs(4) for _ in range(NB)]
            h_mark = A.mark()
            assert h_mark <= 117 * 1024
            A.reset(e_mark)
            Wo = [A.alloc([128, 16, 512], BF16) for _ in range(2)]
            r_Wo = P.Rs(2)
            dma("pool", Wo[0], w_o[:, 0:512].rearrange("(c p) n -> p c n", p=128), [], [r_Wo[0]])
            for b in range(NB):
                dma("sp", y[:, b, :], x_own[LB(b) * 128:(LB(b) + 1) * 128, :], [r_Wo[0]], r_y[b])
            dma("sp", lnw, lnp[:, 0:2, :], [], [r_lnw])
            all_mix = [r for j in range(16) for r in r_mixT[j]]
            for pn in range(4):
                k = pn % 2
                if pn > 0:
                    dma("pool", Wo[k], w_o[:, pn * 512:(pn + 1) * 512].rearrange("(c p) n -> p c n", p=128), [], [r_Wo[k]])
                for b in range(NB):
                    bk = next_mbank()
                    for c in range(16):
                        mm(bank(bk), mixT[:, c, b * 128:(b + 1) * 128], Wo[k][:, c, :], c == 0, c == 15,
                           [r_Wo[k], r_mixT[c][b // 4]], [rbank[bk]])
                    ysl = y[:, b, pn * 512:(pn + 1) * 512]
                    P.op("dve", lambda e, ysl=ysl, bk=bk: e.scalar_tensor_tensor(ysl, ysl, ALPHA, bank(bk), ALU.mult, ALU.add),
                         reads=[rbank[bk], r_y[b][pn]], writes=[r_y[b][pn]])
            layer_norm_and_T(y, r_y, lnw, r_lnw, hT, r_hT, stt)
            all_hT = [r for b in range(NB) for r in r_hT[b]]

        if stage >= 6:
            P.barrier(scratch)
            A.reset(h_mark)
            memb = A.alloc([128, 2, D], BF16)
            memT = A.alloc([128, 16, 256], BF16)
            r_memb = P.R()
            r_memT = [P.Rs(2) for _ in range(2)]
            Wflat = [A.alloc([128, 8192], BF16) for _ in range(2)]
            Wm = [w.rearrange("p (c n) -> p c n", c=16) for w in Wflat]
            r_Wm = P.Rs(2)
            KcT = A.alloc([128, 4, 256], BF16)
            Vc = A.alloc([128, 2, 4, 129], BF16)
            r_KcT = P.Rs(4)
            r_Vc = P.Rs(2)
            r_Vcones = P.R()
            qcT = A.alloc([128, 4, TOK], BF16)
            r_qcT = [P.Rs(2) for _ in range(4)]
            oc = A.alloc([128, NB, 512], BF16)
            r_oc = [P.Rs(4) for _ in range(NB)]
            ocT = A.alloc([128, 4, TOK], BF16)
            r_ocT = P.Rs(NB)
            Wco = Wflat[1].rearrange("p (c n) -> p c n", c=4)
            r_Wco = r_Wm[1]
            PTc = [A.alloc([128, 512], BF16) for _ in range(4)]
            r_PTc = P.Rs(4)
            rc = [A.alloc([128, 4], F32) for _ in range(2)]
            r_rc = P.Rs(2)
            dma("pool", memb, mem.rearrange("(b p) d -> p b d", p=128), [], [r_memb])
            dma("pool", Wm[0], w_mkv[:, 0:512].rearrange("(c p) n -> p c n", p=128), [], [r_Wm[0]])
            dma("pool", Wm[1], w_mkv[:, 512:1024].rearrange("(c p) n -> p c n", p=128), [], [r_Wm[1]])
            dma("sp", lnw, lnp[:, 2:4, :], [], [r_lnw])
            for mbk in range(2):
                transpose_block_bf16(memb[:, mbk, :], lambda hlf, mbk=mbk: memT[:, hlf * 8:(hlf + 1) * 8, mbk * 128:(mbk + 1) * 128],
                                     r_memb, lambda hlf, mbk=mbk: r_memT[mbk][hlf])
            all_memT = [r for a in r_memT for r in a]
            for h in range(4):
                bk = next_mbank()
                for c in range(16):
                    mm(bank(bk)[:, 0:256], Wm[0][:, c, h * 128:(h + 1) * 128], memT[:, c, :], c == 0, c == 15,
                       [r_Wm[0]] + all_memT, [rbank[bk]])
                cp(KcT[:, h, :], bank(bk)[:, 0:256], [rbank[bk]], [r_KcT[h]])
            P.op("pool", lambda e: e.memset(Vc[:, :, :, 128:129], 1.0), writes=[r_Vcones])
            for mbk in range(2):
                bk = next_mbank()
                for c in range(16):
                    mm(bank(bk), memT[:, c, mbk * 128:(mbk + 1) * 128], Wm[1][:, c, :], c == 0, c == 15,
                       [r_Wm[1], r_memT[mbk][c // 8]], [rbank[bk]])
                cp(Vc[:, mbk, :, 0:128], bank(bk).rearrange("p (h d) -> p h d", h=4), [rbank[bk]], [r_Vc[mbk]])
            Wq = Wm[0]
            r_Wq = r_Wm[0]
            dma("pool", Wq, w_cq.rearrange("(c p) n -> p c n", p=128), [], [r_Wq])
            dma("pool", Wco, w_co.rearrange("(c p) n -> p c n", p=128), [], [r_Wco])
            for h in range(4):
                for hlf in range(2):
                    bk = next_mbank()
                    for c in range(16):
                        mm(bank(bk), Wq[:, c, h * 128:(h + 1) * 128], hT[:, c, hlf * 512:(hlf + 1) * 512], c == 0, c == 15,
                           [r_Wq] + all_hT, [rbank[bk]])
                    cp(qcT[:, h, hlf * 512:(hlf + 1) * 512], bank(bk), [rbank[bk]], [r_qcT[h][hlf]])
            ci = 0
            CSC = 128.0 ** -0.5
            for h in range(4):
                for hlf in range(2):
                    pts = []
                    for mbk in range(2):
                        bk = next_mbank(4)
                        mm(bank(bk), KcT[:, h, mbk * 128:(mbk + 1) * 128], qcT[:, h, hlf * 512:(hlf + 1) * 512], True, True,
                           [r_KcT[h], r_qcT[h][hlf]], [rbank[bk]])
                        pk = ci % 4
                        ci += 1
                        P.op("act", lambda e, pk=pk, bk=bk: e.activation(PTc[pk], bank(bk), AF.Exp, scale=CSC),
                             reads=[rbank[bk]], writes=[r_PTc[pk]])
                        pts.append(pk)
                    ab0 = 4 + 2 * ((h * 2 + hlf) % 2)
                    def cacc(qb, ab0=ab0):
                        return bank(ab0 + qb // 2)[:, (qb % 2) * 160:(qb % 2) * 160 + 129]
                    for qb in range(4):
                        for mbk in range(2):
                            mm(cacc(qb), PTc[pts[mbk]][:, qb * 128:(qb + 1) * 128],
                               Vc[:, mbk, h, :], mbk == 0, mbk == 1, [r_PTc[pts[mbk]], r_Vc[mbk], r_Vcones], [rbank[ab0 + qb // 2]])
                    rk = (h * 2 + hlf) % 2
                    for qb in range(4):
                        a_ap = cacc(qb)
                        blk = hlf * 4 + qb
                        P.op("dve", lambda e, rk=rk, qb=qb, a_ap=a_ap: e.reciprocal(rc[rk][:, qb:qb + 1], a_ap[:, 128:129]),
                             reads=[rbank[ab0 + qb // 2]], writes=[r_rc[rk]])
                        P.op("dve", lambda e, rk=rk, qb=qb, a_ap=a_ap, blk=blk, h=h: e.tensor_scalar_mul(
                            oc[:, blk, h * 128:(h + 1) * 128], a_ap[:, 0:128], rc[rk][:, qb:qb + 1]),
                            reads=[rbank[ab0 + qb // 2], r_rc[rk]], writes=[r_oc[blk][h]])
            for b in range(NB):
                bk = next_tbank()
                pb = bank(bk).bitcast(BF16)
                for c in range(4):
                    P.op("pe", lambda e, pb=pb, c=c, b=b: e.transpose(pb[:, c * 128:(c + 1) * 128], oc[:, b, c * 128:(c + 1) * 128], idb),
                         reads=r_oc[b] + [r_idb], writes=[rbank[bk]])
                cp(ocT[:, :, b * 128:(b + 1) * 128], pb[:, 0:512].rearrange("p (c n) -> p c n", c=4), [rbank[bk]], [r_ocT[b]])
            for b in range(NB):
                for pn in range(4):
                    bk = next_mbank()
                    for c in range(4):
                        mm(bank(bk), ocT[:, c, b * 128:(b + 1) * 128], Wco[:, c, pn * 512:(pn + 1) * 512], c == 0, c == 3,
                           [r_Wco, r_ocT[b]], [rbank[bk]])
                    ysl = y[:, b, pn * 512:(pn + 1) * 512]
                    P.op("dve", lambda e, ysl=ysl, bk=bk: e.scalar_tensor_tensor(ysl, ysl, ALPHA, bank(bk), ALU.mult, ALU.add),
                         reads=[rbank[bk], r_y[b][pn]], writes=[r_y[b][pn]])
            layer_norm_and_T(y, r_y, lnw, r_lnw, hT, r_hT, stt)
            P.barrier(scratch)
            A.reset(h_mark)

        if stage >= 7:
            dma("sp", lnw, lnp[:, 4:6, :], [], [r_lnw])
            Wgu = [A.alloc([128, 16, 512], BF16) for _ in range(2)]
            Wd = [A.alloc([128, 2, D], BF16) for _ in range(2)]
            r_Wgu, r_Wd = P.Rs(2), P.Rs(2)
            Aff = [A.alloc([128, 2, TOK], BF16) for _ in range(2)]
            r_Aff = [[P.Rs(2) for _ in range(2)] for _ in range(2)]
            sgf = [A.alloc([128, 512], F32) for _ in range(2)]
            r_sgf = P.Rs(2)
            fi = 0
            NSC = DFF // 256
            for s in range(NSC):
                k = s % 2
                dma("pool", Wgu[k][:, :, 0:256], w_gu[:, s * 256:(s + 1) * 256].rearrange("(c p) n -> p c n", p=128), [], [r_Wgu[k]])
                dma("pool", Wgu[k][:, :, 256:512], w_gu[:, DFF + s * 256:DFF + (s + 1) * 256].rearrange("(c p) n -> p c n", p=128), [], [r_Wgu[k]])
                dma("pool", Wd[k], w_dn[s * 256:(s + 1) * 256, :].rearrange("(c p) n -> p c n", p=128), [], [r_Wd[k]])
                for sub in range(2):
                    for hlf in range(2):
                        bg = next_mbank()
                        for c in range(16):
                            mm(bank(bg), Wgu[k][:, c, sub * 128:(sub + 1) * 128], hT[:, c, hlf * 512:(hlf + 1) * 512], c == 0, c == 15,
                               [r_Wgu[k]] + all_hT, [rbank[bg]])
                        bu = next_mbank()
                        for c in range(16):
                            mm(bank(bu), Wgu[k][:, c, 256 + sub * 128:256 + (sub + 1) * 128], hT[:, c, hlf * 512:(hlf + 1) * 512], c == 0, c == 15,
                               [r_Wgu[k]] + all_hT, [rbank[bu]])
                        fk = fi % 2
                        fi += 1
                        P.op("act", lambda e, fk=fk, bg=bg: e.activation(sgf[fk], bank(bg), AF.Silu), reads=[rbank[bg]], writes=[r_sgf[fk]])
                        P.op("dve", lambda e, fk=fk, bu=bu, k=k, sub=sub, hlf=hlf: e.tensor_tensor(
                            Aff[k][:, sub, hlf * 512:(hlf + 1) * 512], bank(bu), sgf[fk], ALU.mult),
                            reads=[rbank[bu], r_sgf[fk]], writes=[r_Aff[k][sub][hlf]])
                for b in range(NB):
                    for pn in range(4):
                        bk = next_mbank()
                        for sub in range(2):
                            mm(bank(bk), Aff[k][:, sub, b * 128:(b + 1) * 128], Wd[k][:, sub, pn * 512:(pn + 1) * 512], sub == 0, sub == 1,
                               [r_Wd[k], r_Aff[k][sub][b // 4]], [rbank[bk]])
                        ysl = y[:, b, pn * 512:(pn + 1) * 512]
                        if s == 0:
                            P.op("dve", lambda e, ysl=ysl, bk=bk: e.scalar_tensor_tensor(ysl, ysl, ALPHA, bank(bk), ALU.mult, ALU.add),
                                 reads=[rbank[bk], r_y[b][pn]], writes=[r_y[b][pn]])
                        else:
                            P.op("dve", lambda e, ysl=ysl, bk=bk: e.tensor_tensor(ysl, ysl, bank(bk), ALU.add),
                                 reads=[rbank[bk], r_y[b][pn]], writes=[r_y[b][pn]])
            r_out = P.R()
            layer_norm_and_T(y, r_y, lnw, r_lnw, None, None, stt, final_out=out, r_out=r_out)
            P.final_wait(r_out)

        if dbg is not None:
            P.barrier(scratch)
            r_dbg = P.R()
            if stage == 3:
                stg = A.alloc([128, 8, TOK], F32)
                P.op("dve", lambda e: e.tensor_copy(stg, oaT), reads=r_oaT, writes=[r_dbg])
                dma("sp", dbg_out.rearrange("(c p) t -> p c t", p=128), stg, [r_dbg], [r_dbg])
            elif stage == 4:
                stg = A.alloc([128, 8, TOK], F32)
                P.op("dve", lambda e: e.tensor_copy(stg, obT), reads=[r for hh in range(8) for r in r_obT[hh]], writes=[r_dbg])
                dma("sp", dbg_out.rearrange("(c p) t -> p c t", p=128), stg, [r_dbg], [r_dbg])
            elif stage in (5, 6):
                for b in range(NB):
                    dma("sp", dbg_out[b * 128:(b + 1) * 128, :], y[:, b, :], r_y[b], [r_dbg])
            P.final_wait(r_dbg)
        P.emit()
    return nc


def _rel_bucket(dist):
    n = np.maximum(dist, 0)
    exact = 16
    logv = np.log(np.maximum(n, 1).astype(np.float32) / np.float32(exact)) / np.float32(math.log(128 / 16))
    large = exact + (logv * np.float32(32 - exact)).astype(np.int32)
    large = np.minimum(large, 31)
    return np.where(n < exact, n, large)


def _bias_tables(table, core):
    k = np.arange(128)[:, None]
    q = np.arange(128)[None, :]
    sw = np.full((2, 128, 16, 2, 128), NEG, np.float32)
    for kap in range(2):
        dist = (128 + q - k) if kap == 0 else (q - k)
        ok = (dist >= 0) & (dist < 128)
        bk = _rel_bucket(dist)
        for h in range(16):
            vals = np.where(ok, table[bk, h], np.float32(NEG)).astype(np.float32)
            sw[0, :, h, kap, :] = vals
            if kap == 1:
                sw[1, :, h, kap, :] = vals
    if core != 0:
        sw[1] = sw[0]
    perm = [8 * g + 2 * i4 + r for g in range(2) for r in range(2) for i4 in range(4)]
    sw = sw[:, :, perm]
    q5 = np.arange(512)[None, :]
    near = np.full((2, 8, 128, 5, 512), NEG, np.float32)
    for kap in range(5):
        dist = q5 - 128 * (kap - 1) - k
        ok = dist >= 0
        bk = _rel_bucket(dist)
        for h in range(8):
            vals = np.where(ok, table[bk, 16 + h], np.float32(NEG)).astype(np.float32)
            near[:, h, :, kap, :] = vals[None]
    if core == 0:
        near[0, :, :, 0, :] = NEG
    far = np.full((128, 2, 64, 8), NEG, np.float32)
    for ch in range(2):
        g0 = 4 * (core if ch == 0 else 15 - core)
        for kb in range(64):
            if kb <= g0 - 2:
                far[:, ch, kb, :] = table[31, 16:24][None, :]
    return sw.reshape(2, 128, -1), near.reshape(2, 8, 128, -1), far.reshape(128, -1)


_NC_CACHE = {}


def kernel(x, mem, rel_bias_table, w_in, sinks, lambda_q1, lambda_k1, lambda_q2, lambda_k2,
           subln_w, w_branch_a, w_branch_b, w_o, ln1_g, ln1_b, w_cq, w_mem_kv, w_co,
           ln2_g, ln2_b, w_gate_up, w_down, ln3_g, ln3_b, _stage=99, _dbg=None):
    f = lambda a: np.ascontiguousarray(np.asarray(a, dtype=np.float32))
    x2 = f(x).reshape(S, D)
    table = f(rel_bias_table)
    lnp = np.stack([f(ln1_g)[0], f(ln1_b)[0], f(ln2_g)[0], f(ln2_b)[0], f(ln3_g)[0], f(ln3_b)[0]], 0)
    lnp = np.ascontiguousarray(np.broadcast_to(lnp[None], (128, 6, D)))
    perm = [8 * g + 2 * i4 + r for g in range(2) for r in range(2) for i4 in range(4)]
    small = np.concatenate([f(sinks)[0][perm], f(lambda_q1)[0], f(lambda_k1)[0], f(lambda_q2)[0], f(lambda_k2)[0], f(subln_w)[0]])
    small = np.ascontiguousarray(np.broadcast_to(small[None], (128, small.shape[0])))
    shared = {
        "x_all": x2, "mem": f(mem)[0], "w_in": f(w_in)[0], "w_branch_a": f(w_branch_a)[0], "w_branch_b": f(w_branch_b)[0],
        "w_o": f(w_o)[0], "w_cq": f(w_cq)[0], "w_mem_kv": f(w_mem_kv)[0], "w_co": f(w_co)[0],
        "w_gate_up": f(w_gate_up)[0], "w_down": f(w_down)[0], "lnp": lnp, "smallp": small,
        "ident": np.eye(128, dtype=np.float32),
    }
    in_maps = []
    for c in range(NCORES):
        xo = np.zeros((NOWN, D), np.float32)
        for ch, gc in enumerate((c, 15 - c)):
            r0 = ch * 640
            if gc > 0:
                xo[r0:r0 + 128] = x2[gc * 512 - 128:gc * 512]
            xo[r0 + 128:r0 + 640] = x2[gc * 512:(gc + 1) * 512]
        sw, near, far = _bias_tables(table, c)
        m = dict(shared)
        m.update({"x_own": xo, "bias_swa": sw, "bias_near": near, "bias_far": far})
        in_maps.append(m)
    key = (_stage, _dbg)
    if key not in _NC_CACHE:
        _NC_CACHE[key] = build(_stage, _dbg)
    nc = _NC_CACHE[key]
    res = run_bass_kernel_spmd(nc, in_maps, core_ids=list(range(NCORES)))
    if _dbg is not None:
        return np.concatenate([r["dbg"] for r in res.results], axis=0)
    o = np.empty((S, D), np.float32)
    for c in range(NCORES):
        oc = res.results[c]["out"]
        o[c * 512:(c + 1) * 512] = oc[0:512]
        o[(15 - c) * 512:(16 - c) * 512] = oc[512:1024]
    return o.reshape(1, S, D)
```

```python
import math
import contextlib
import numpy as np
import concourse.bass as bass
import concourse.mybir as mybir
from concourse.bass_utils import run_bass_kernel_spmd

F32 = mybir.dt.float32
BF16 = mybir.dt.bfloat16
AF = mybir.ActivationFunctionType
ALU = mybir.AluOpType
AX = mybir.AxisListType

NCORES = 8
S = 8192
D = 2048
TOK = S // NCORES
NB = TOK // 128
DFF = 5632
NEG = -30000.0
ALPHA = 2.0 ** 0.25
LAMBDA_INIT = 0.8 - 0.6 * math.exp(0.0)
EPS = 1e-5
C_QA, C_KA, C_VA, C_QB, C_KB, C_VB, C_GA, C_GB = 0, 1024, 1152, 1280, 2304, 3328, 4352, 6400
NOWN = 1280
NTOKX = S + NOWN
OWN0 = (128, 768)
NFAR = (27, 59)


def LB(j):
    return j + 1 if j < 4 else j + 2


class Region:
    __slots__ = ("name", "last_w", "readers")

    def __init__(self, name):
        self.name = name
        self.last_w = None
        self.readers = []


class Op:
    __slots__ = ("idx", "eng", "fn", "deps", "dma", "token", "signal", "prewait")

    def __init__(self, idx, eng, fn, dma):
        self.idx = idx
        self.eng = eng
        self.fn = fn
        self.dma = dma
        self.deps = set()
        self.token = None
        self.signal = False
        self.prewait = None


class Prog:
    ENGS = ("pe", "act", "dve", "pool", "sp")
    NDMA_SEM = 8

    def __init__(self, nc, same_engine_sync=True):
        self.nc = nc
        self.ops = []
        self.same_engine_sync = same_engine_sync
        self.finals = []
        self.barrier_idx = None
        self.last_on = {}
        self.dma_since = []

    def R(self, name=""):
        return Region(name)

    def Rs(self, n, name=""):
        return [Region(name) for _ in range(n)]

    def op(self, eng, fn, reads=(), writes=(), dma=False):
        o = Op(len(self.ops), eng, fn, dma)
        for r in reads:
            if r.last_w is not None:
                o.deps.add(r.last_w)
        for w in writes:
            if w.last_w is not None:
                o.deps.add(w.last_w)
            for rd in w.readers:
                o.deps.add(rd)
        for r in reads:
            r.readers.append(o.idx)
        for w in writes:
            w.last_w = o.idx
            w.readers = []
        if self.barrier_idx is not None:
            o.deps.add(self.barrier_idx)
        o.deps.discard(o.idx)
        self.ops.append(o)
        if dma:
            self.dma_since.append(o.idx)
        else:
            self.last_on[eng] = o.idx
        return o

    def barrier(self, scratch):
        o = Op(len(self.ops), "dve", lambda e: e.memset(scratch, 0.0), False)
        for e, i in self.last_on.items():
            o.deps.add(i)
        for i in self.dma_since:
            o.deps.add(i)
        self.dma_since = []
        if self.barrier_idx is not None:
            o.deps.add(self.barrier_idx)
        self.ops.append(o)
        self.last_on["dve"] = o.idx
        self.barrier_idx = o.idx
        return o

    def final_wait(self, region):
        self.finals.append(region.last_w)

    def _needs_sync(self, dep, ename):
        if dep.eng == ename and not dep.dma:
            if ename == "pe" or not self.same_engine_sync:
                return False
        return True

    def emit(self):
        nc = self.nc
        ops = self.ops
        for o in ops:
            for d in o.deps:
                dep = ops[d]
                if self._needs_sync(dep, o.eng):
                    dep.signal = True
        for f in self.finals:
            ops[f].signal = True
        for o in ops:
            if o.dma:
                o.signal = True
        with contextlib.ExitStack() as st:
            esem = {e: st.enter_context(nc.semaphore(f"s_{e}")) for e in ("pe", "act", "dve", "pool")}
            dsem = {q: [st.enter_context(nc.semaphore(f"d_{q}{k}")) for k in range(self.NDMA_SEM)]
                    for q in ("sp", "act", "pool")}
            cnt = {e: 0 for e in esem}
            dcnt = {q: 0 for q in dsem}
            for o in ops:
                if o.dma:
                    i = dcnt[o.eng]
                    dcnt[o.eng] += 1
                    k = i % self.NDMA_SEM
                    m = i // self.NDMA_SEM
                    o.token = (dsem[o.eng][k], 16 * (m + 1))
                    if m > 0:
                        o.prewait = (dsem[o.eng][k], 16 * m)
                elif o.signal:
                    cnt[o.eng] += 1
                    o.token = (esem[o.eng], cnt[o.eng])
            self.sem_counts = dict(cnt)
            block = st.enter_context(nc.Block())
            engobj = {"pe": "tensor", "act": "scalar", "dve": "vector", "pool": "gpsimd", "sp": "sync"}

            def run(ename, eng):
                seen = {}

                def wait(tok):
                    sem, v = tok
                    if seen.get(sem.num, 0) >= v:
                        return
                    eng.wait_ge(sem, v)
                    seen[sem.num] = v

                for o in ops:
                    if o.eng != ename:
                        continue
                    for d in sorted(o.deps, reverse=True):
                        dep = ops[d]
                        if not self._needs_sync(dep, ename):
                            continue
                        wait(dep.token)
                    if o.prewait is not None:
                        wait(o.prewait)
                    ins = o.fn(eng)
                    if o.token is not None:
                        sem, v = o.token
                        ins.then_inc(sem, 16 if o.dma else 1)
                if ename == "sp":
                    for f in self.finals:
                        wait(ops[f].token)

            for ename in self.ENGS:
                getattr(block, engobj[ename])(lambda eng, _e=ename: run(_e, eng))


class Arena:
    def __init__(self, t, nbytes):
        self.t = t
        self.cap = nbytes
        self.off = 0

    def alloc(self, shape, dtype):
        esz = 2 if dtype == BF16 else 4
        n = int(np.prod(shape[1:])) * esz
        off = (self.off + 63) // 64 * 64
        assert off + n <= self.cap, f"SBUF arena overflow: {off + n} > {self.cap}"
        self.off = off + n
        ap = self.t[:, off // 2:(off + n) // 2]
        if dtype != BF16:
            ap = ap.bitcast(dtype)
        if len(shape) == 3:
            ap = ap.rearrange("p (a b) -> p a b", a=shape[1])
        elif len(shape) == 4:
            ap = ap.rearrange("p (a b c) -> p a b c", a=shape[1], b=shape[2])
        elif len(shape) == 5:
            ap = ap.rearrange("p (a b c d) -> p a b c d", a=shape[1], b=shape[2], c=shape[3])
        return ap

    def mark(self):
        return self.off

    def reset(self, m):
        self.off = m


def build(stage=99, dbg=None):
    nc = bass.Bass("TRN2", target_bir_lowering=False)

    def din(name, shape, dt=F32):
        return nc.dram_tensor(name, shape, dt, kind="ExternalInput").ap()

    x_all = din("x_all", [S, D])
    x_own = din("x_own", [NOWN, D])
    mem = din("mem", [256, D])
    w_in = din("w_in", [D, 8448])
    w_a = din("w_branch_a", [1024, D])
    w_b = din("w_branch_b", [1024, D])
    w_o = din("w_o", [D, D])
    w_cq = din("w_cq", [D, 512])
    w_mkv = din("w_mem_kv", [D, 1024])
    w_co = din("w_co", [512, D])
    w_gu = din("w_gate_up", [D, 2 * DFF])
    w_dn = din("w_down", [DFF, D])
    lnp = din("lnp", [128, 6, D])
    smallp = din("smallp", [128, 16 + 256 + 128])
    ident_d = din("ident", [128, 128])
    bias_far = din("bias_far", [128, 2 * 64 * 8])
    bias_near = din("bias_near", [2, 8, 128, 5 * 512])
    bias_swa = din("bias_swa", [2, 128, 16 * 2 * 128])
    out = nc.dram_tensor("out", [TOK, D], F32, kind="ExternalOutput").ap()
    kt_all = nc.dram_tensor("kt_all", [8, 128, NTOKX], BF16).ap()
    v_all = nc.dram_tensor("v_all", [74, 128, 1024], BF16).ap()
    kc_s = nc.dram_tensor("kc_s", [128, 1024], BF16).ap()
    vc_s = nc.dram_tensor("vc_s", [128, 1024], BF16).ap()
    dbg_out = None
    if dbg is not None:
        dbg_out = nc.dram_tensor("dbg", [TOK, dbg], F32, kind="ExternalOutput").ap()

    P = Prog(nc)
    with contextlib.ExitStack() as st:
        ARENA_BYTES = 207 * 1024
        arena_t = st.enter_context(nc.sbuf_tensor("arena", [128, ARENA_BYTES // 2], BF16))
        A = Arena(arena_t, ARENA_BYTES)
        pp = [st.enter_context(nc.psum_tensor(f"pp{i}", [128, 1024], F32)) for i in range(4)]

        def bank(b):
            return pp[b // 2][:, (b % 2) * 512:(b % 2 + 1) * 512]

        rbank = P.Rs(8, "bank")
        cpctr = [0]

        def cp(out_ap, in_ap, reads, writes, eng=None):
            if eng is None:
                eng = ("dve", "act")[cpctr[0] % 2]
                cpctr[0] += 1
            if eng == "act":
                return P.op("act", lambda e: e.activation(out_ap, in_ap, AF.Copy), reads=reads, writes=writes)
            if eng == "pool":
                return P.op("pool", lambda e: e.tensor_copy(out_ap, in_ap), reads=reads, writes=writes)
            return P.op("dve", lambda e: e.tensor_copy(out_ap, in_ap), reads=reads, writes=writes)

        def mm(out_ap, lhsT, rhs, start, stop, reads, writes):
            return P.op("pe", lambda e: e.matmul(out_ap, lhsT, rhs, start=start, stop=stop),
                        reads=reads, writes=writes)

        def dma(q, out_ap, in_ap, reads, writes):
            return P.op(q, lambda e: e.dma_start(out=out_ap, in_=in_ap), reads=reads, writes=writes, dma=True)

        idb = A.alloc([128, 128], BF16)
        idf = A.alloc([128, 128], F32)
        small = A.alloc([128, 400], F32)
        scratch = A.alloc([128, 16], F32)
        esink = A.alloc([128, 16], F32)
        neglam = A.alloc([128, 1], F32)
        wsub = A.alloc([128, 128], F32)
        lam_t = A.alloc([128, 8], F32)
        epsc = A.alloc([128, 1], F32)
        r_idb, r_idf, r_small, r_const = P.R(), P.R(), P.R(), P.R()
        dma("pool", idb, ident_d, [], [r_idb])
        dma("sp", idf, ident_d, [], [r_idf])
        dma("sp", small, smallp, [], [r_small])
        P.op("dve", lambda e: e.memset(epsc, EPS), writes=[r_const])
        P.op("act", lambda e: e.activation(esink, small[:, 0:16], AF.Exp), reads=[r_small], writes=[r_const])
        lq = small[:, 16:272].rearrange("p (a b) -> p a b", a=4)
        prod_t = A.alloc([128, 2, 64], F32)
        P.op("dve", lambda e: e.tensor_tensor(prod_t[:, 0, :], lq[:, 0, :], lq[:, 1, :], ALU.mult), reads=[r_small], writes=[r_const])
        P.op("dve", lambda e: e.tensor_tensor(prod_t[:, 1, :], lq[:, 2, :], lq[:, 3, :], ALU.mult), reads=[r_small], writes=[r_const])
        P.op("dve", lambda e: e.reduce_sum(lam_t[:, 0:2], prod_t, axis=AX.X), reads=[r_const], writes=[r_const])
        P.op("act", lambda e: e.activation(lam_t[:, 2:4], lam_t[:, 0:2], AF.Exp), reads=[r_const], writes=[r_const])
        P.op("dve", lambda e: e.tensor_tensor(lam_t[:, 4:5], lam_t[:, 3:4], lam_t[:, 2:3], ALU.subtract), reads=[r_const], writes=[r_const])
        P.op("dve", lambda e: e.tensor_scalar_add(neglam, lam_t[:, 4:5], -LAMBDA_INIT), reads=[r_const], writes=[r_const])
        P.op("dve", lambda e: e.tensor_scalar_mul(wsub, small[:, 272:400], (1.0 - LAMBDA_INIT)), reads=[r_small], writes=[r_const])
        assert A.mark() <= 4 * 1024
        M0 = 4 * 1024
        A.reset(M0)

        xT = A.alloc([128, 16, NOWN], BF16)
        r_xT = [[P.R() for _ in range(2)] for _ in range(10)]
        after_xT = A.mark()
        QbT = A.alloc([128, 8, TOK], BF16)
        oaT = A.alloc([128, 8, TOK], BF16)
        obT = A.alloc([128, 8, TOK], BF16)
        xT_mark = A.mark()
        assert xT_mark <= 93 * 1024
        A.reset(after_xT)

        Wkb = A.alloc([128, 16, 1024], BF16)
        Wvb = A.alloc([128, 16, 1024], BF16)
        r_Wkb, r_Wvb = P.Rs(2), P.Rs(2)
        xs0 = A.alloc([128, 4, D], BF16)
        r_xs0 = P.R()
        first_g = 16 if (dbg is not None and stage < 4) else 0
        if stage >= 1 and first_g == 0:
            dma("pool", xs0, x_all[0:512, :].rearrange("(b p) d -> p b d", p=128), [], [r_xs0])
        for hlf in range(2):
            dma("pool", Wkb[:, :, hlf * 512:(hlf + 1) * 512],
                w_in[:, C_KB + hlf * 512:C_KB + (hlf + 1) * 512].rearrange("(c p) n -> p c n", p=128), [], [r_Wkb[hlf]])
        for hlf in range(2):
            dma("pool", Wvb[:, :, hlf * 512:(hlf + 1) * 512],
                w_in[:, C_VB + hlf * 512:C_VB + (hlf + 1) * 512].rearrange("(c p) n -> p c n", p=128), [], [r_Wvb[hlf]])
        xs = [xs0, A.alloc([128, 4, D], BF16)]
        xTs = [A.alloc([128, 16, 512], BF16) for _ in range(2)]
        KTs = [A.alloc([128, 8, 512], BF16) for _ in range(2)]
        Vs = [A.alloc([128, 4, 1024], BF16) for _ in range(2)]
        r_xs = [r_xs0, P.R()]
        r_xTs = [[[P.R() for _ in range(2)] for _ in range(4)] for _ in range(2)]
        r_KTs = [P.Rs(8) for _ in range(2)]
        r_Vs = [[[P.R() for _ in range(2)] for _ in range(4)] for _ in range(2)]
        r_ktall, r_vall = P.R(), P.R()
        tb = [0]
        mb = [0]

        def next_tbank():
            b = 6 + tb[0] % 2
            tb[0] += 1
            return b

        def next_mbank(n=6):
            b = mb[0] % n
            mb[0] += 1
            return b

        def transpose_block_bf16(src_tok_major, dst_fn, rsrc, rdst_fn):
            for hlf in range(2):
                b = next_tbank()
                pb = bank(b).bitcast(BF16)
                for c in range(8):
                    cc = hlf * 8 + c
                    P.op("pe", lambda e, pb=pb, c=c, cc=cc: e.transpose(pb[:, c * 128:(c + 1) * 128],
                                                                       src_tok_major[:, cc * 128:(cc + 1) * 128], idb),
                         reads=[rsrc, r_idb], writes=[rbank[b]])
                cp(dst_fn(hlf), pb.rearrange("p (c n) -> p c n", c=8), [rbank[b]], [rdst_fn(hlf)])

        NG = 19
        for g in range(16 if (dbg is not None and stage < 4) else 0, NG if stage >= 1 else 0):
            if g == 15:
                continue
            par = g % 2
            nb = 4 if g < 18 else 2
            ntok = nb * 128
            if g < 16:
                src = x_all[g * 512:(g + 1) * 512, :]
            else:
                src = x_own[(g - 16) * 512:(g - 16) * 512 + ntok, :]
            if g != 0:
                dma("pool", xs[par][:, 0:nb, :], src.rearrange("(b p) d -> p b d", p=128), [], [r_xs[par]])
            for b in range(nb):
                if g < 16:
                    dst_fn = lambda hlf, b=b: xTs[par][:, hlf * 8:(hlf + 1) * 8, b * 128:(b + 1) * 128]
                    rdst_fn = lambda hlf, b=b: r_xTs[par][b][hlf]
                else:
                    lb = (g - 16) * 4 + b
                    dst_fn = lambda hlf, lb=lb: xT[:, hlf * 8:(hlf + 1) * 8, lb * 128:(lb + 1) * 128]
                    rdst_fn = lambda hlf, lb=lb: r_xT[lb][hlf]
                transpose_block_bf16(xs[par][:, b, :], dst_fn, r_xs[par], rdst_fn)
            if g < 16:
                xsrc = xTs[par]
                t0 = 0
                rx = lambda b, hlf: r_xTs[par][b][hlf]
            else:
                xsrc = xT
                t0 = (g - 16) * 512
                rx = lambda b, hlf, g=g: r_xT[(g - 16) * 4 + b][hlf]
            for h in range(8):
                bk = next_mbank()
                for c in range(16):
                    mm(bank(bk)[:, 0:ntok], Wkb[:, c, h * 128:(h + 1) * 128], xsrc[:, c, t0:t0 + ntok], c == 0, c == 15,
                       [r_Wkb[h // 4]] + [rx(b, c // 8) for b in range(nb)], [rbank[bk]])
                cp(KTs[par][:, h, 0:ntok], bank(bk)[:, 0:ntok], [rbank[bk]], [r_KTs[par][h]])
            dma("sp", kt_all[:, :, g * 512:g * 512 + ntok].rearrange("h p n -> p h n"), KTs[par][:, :, 0:ntok],
                r_KTs[par], [r_ktall])
            for b in range(nb):
                for hlf in range(2):
                    bk = next_mbank()
                    for c in range(16):
                        mm(bank(bk), xsrc[:, c, t0 + b * 128:t0 + (b + 1) * 128], Wvb[:, c, hlf * 512:(hlf + 1) * 512], c == 0, c == 15,
                           [r_Wvb[hlf], rx(b, c // 8)], [rbank[bk]])
                    cp(Vs[par][:, b, hlf * 512:(hlf + 1) * 512], bank(bk), [rbank[bk]], [r_Vs[par][b][hlf]])
            dma("sp", v_all[g * 4:g * 4 + nb, :, :].rearrange("b p n -> p b n"), Vs[par][:, 0:nb, :],
                [r for bb in range(nb) for r in r_Vs[par][bb]], [r_vall])
        P.barrier(scratch)
        A.reset(xT_mark)

        QaT = A.alloc([128, 8, TOK], BF16)
        KaT = A.alloc([128, 2, NOWN], BF16)
        Va = A.alloc([128, 10, 2, 65], BF16)
        r_QaT = [P.Rs(2) for _ in range(8)]
        r_QbT = [P.Rs(2) for _ in range(8)]
        r_KaT = [P.Rs(3) for _ in range(2)]
        r_Va = P.Rs(10)
        r_Vaones = P.R()
        r_oaT = P.Rs(8)
        r_obT = [P.Rs(8) for _ in range(8)]
        proj_mark = A.mark()
        allx = [r_xT[b][hh] for b in range(10) for hh in range(2)]

        if stage >= 2:
            Wt = [A.alloc([128, 16, 512], BF16) for _ in range(2)]
            r_Wt = P.Rs(2)
            wi = [0]

            def load_w(col0, ncols=512):
                k = wi[0] % 2
                wi[0] += 1
                dma("pool", Wt[k][:, :, 0:ncols], w_in[:, col0:col0 + ncols].rearrange("(c p) n -> p c n", p=128), [], [r_Wt[k]])
                return Wt[k], r_Wt[k]

            for (col0, dst, rdst) in ((C_QA, QaT, r_QaT), (C_QB, QbT, r_QbT)):
                for pn in range(2):
                    W, rW = load_w(col0 + pn * 512)
                    for ii in range(4):
                        i = pn * 4 + ii
                        for hlf in range(2):
                            bk = next_mbank()
                            for c in range(16):
                                mm(bank(bk), W[:, c, ii * 128:(ii + 1) * 128], xT[:, c, OWN0[hlf]:OWN0[hlf] + 512],
                                   c == 0, c == 15, [rW] + allx, [rbank[bk]])
                            cp(dst[:, i, hlf * 512:(hlf + 1) * 512], bank(bk), [rbank[bk]], [rdst[i][hlf]])
            k = wi[0] % 2
            wi[0] += 1
            Wka = Wt[k][:, :, 0:256].rearrange("p c (g u d) -> p c g u d", g=2, u=2)
            for u in range(2):
                for g in range(2):
                    dma("pool", Wka[:, :, g, u, :], w_in[:, C_KA + g * 64:C_KA + (g + 1) * 64].rearrange("(c p) d -> p c d", p=128), [], [r_Wt[k]])
            Wva = Wt[k][:, :, 256:384]
            dma("pool", Wva, w_in[:, C_VA:C_VA + 128].rearrange("(c p) n -> p c n", p=128), [], [r_Wt[k]])
            for g in range(2):
                for pi, (t0, nt) in enumerate(((0, 512), (512, 512), (1024, 256))):
                    bk = next_mbank()
                    for c in range(16):
                        mm(bank(bk)[:, 0:nt], Wt[k][:, c, g * 128:(g + 1) * 128], xT[:, c, t0:t0 + nt], c == 0, c == 15,
                           [r_Wt[k]] + allx, [rbank[bk]])
                    cp(KaT[:, g, t0:t0 + nt], bank(bk)[:, 0:nt], [rbank[bk]], [r_KaT[g][pi]])
            P.op("pool", lambda e: e.memset(Va[:, :, :, 64:65], 1.0), writes=[r_Vaones])
            for b in range(10):
                bk = next_mbank()
                for c in range(16):
                    mm(bank(bk)[:, 0:128], xT[:, c, b * 128:(b + 1) * 128], Wva[:, c, :], c == 0, c == 15,
                       [r_Wt[k], r_xT[b][c // 8]], [rbank[bk]])
                cp(Va[:, b, :, 0:64], bank(bk)[:, 0:128].rearrange("p (g d) -> p g d", g=2), [rbank[bk]], [r_Va[b]])
            memb = A.alloc([128, 2, D], BF16)
            memT = A.alloc([128, 16, 256], BF16)
            kcs = A.alloc([128, 4, 256], BF16)
            vcs = A.alloc([128, 2, 512], BF16)
            r_memb = P.R()
            r_memT = [P.Rs(2) for _ in range(2)]
            r_kcs, r_vcs = P.Rs(4), P.Rs(2)
            r_kcd, r_vcd = P.R(), P.R()
            dma("pool", memb, mem.rearrange("(b p) d -> p b d", p=128), [], [r_memb])
            for mbk in range(2):
                transpose_block_bf16(memb[:, mbk, :], lambda hlf, mbk=mbk: memT[:, hlf * 8:(hlf + 1) * 8, mbk * 128:(mbk + 1) * 128],
                                     r_memb, lambda hlf, mbk=mbk: r_memT[mbk][hlf])
            all_memT = [r for a in r_memT for r in a]
            k = wi[0] % 2
            wi[0] += 1
            dma("pool", Wt[k], w_mkv[:, 0:512].rearrange("(c p) n -> p c n", p=128), [], [r_Wt[k]])
            for h in range(4):
                bk = next_mbank()
                for c in range(16):
                    mm(bank(bk)[:, 0:256], Wt[k][:, c, h * 128:(h + 1) * 128], memT[:, c, :], c == 0, c == 15,
                       [r_Wt[k]] + all_memT, [rbank[bk]])
                cp(kcs[:, h, :], bank(bk)[:, 0:256], [rbank[bk]], [r_kcs[h]])
            dma("sp", kc_s, kcs.rearrange("p a b -> p (a b)"), r_kcs, [r_kcd])
            k = wi[0] % 2
            wi[0] += 1
            dma("pool", Wt[k], w_mkv[:, 512:1024].rearrange("(c p) n -> p c n", p=128), [], [r_Wt[k]])
            for mbk in range(2):
                bk = next_mbank()
                for c in range(16):
                    mm(bank(bk), memT[:, c, mbk * 128:(mbk + 1) * 128], Wt[k][:, c, :], c == 0, c == 15,
                       [r_Wt[k], r_memT[mbk][c // 8]], [rbank[bk]])
                cp(vcs[:, mbk, :], bank(bk), [rbank[bk]], [r_vcs[mbk]])
            dma("sp", vc_s, vcs.rearrange("p a b -> p (a b)"), r_vcs, [r_vcd])
        P.barrier(scratch)
        A.reset(proj_mark)

        def pipeline(nitems, stages):
            ns = len(stages)
            for tau in range(nitems + ns - 1):
                for si in range(ns - 1, -1, -1):
                    i = tau - si
                    if 0 <= i < nitems:
                        stages[si](i)

        if stage >= 3:
            Bsw = [A.alloc([128, 4, 4, 256], F32) for _ in range(2)]
            r_Bsw = P.Rs(2)
            tmpS = [A.alloc([128, 4, 128], F32) for _ in range(4)]
            r_tmpS = P.Rs(4)
            PTs = [A.alloc([128, 4, 128], BF16) for _ in range(8)]
            r_PTs = P.Rs(8)
            oa = [A.alloc([128, 1024], BF16) for _ in range(2)]
            r_oa = [P.Rs(16) for _ in range(2)]
            den = [A.alloc([128, 8], F32) for _ in range(4)]
            r_den = P.Rs(4)
            r_rden = P.Rs(4)
            allKa = [r for g in range(2) for r in r_KaT[g]]
            def unpack(i):
                j, gr = divmod(i, 4)
                g, r = divmod(gr, 2)
                return j, g, r

            def sQ(i):
                j, g, r = unpack(i)
                bp = j % 2
                if i % 4 == 0:
                    dma("sp", Bsw[bp].rearrange("p a b c -> p (a b c)"), bias_swa[1 if j == 0 else 0], [], [r_Bsw[bp]])
                for kap in range(2):
                    blk = LB(j) - 1 + kap
                    bk = (i % 2) * 2 + kap
                    mm(bank(bk).rearrange("p (h q) -> p h q", h=4), KaT[r * 64:(r + 1) * 64, g, blk * 128:(blk + 1) * 128],
                       QaT[r * 64:(r + 1) * 64, 4 * g:4 * g + 4, j * 128:(j + 1) * 128], True, True,
                       allKa + [r_QaT[ii][j // 4] for ii in range(4 * g, 4 * g + 4)], [rbank[bk]])

            def sA(i):
                j, g, r = unpack(i)
                bp = j % 2
                for kap in range(2):
                    bk = (i % 2) * 2 + kap
                    tk = (i % 2) * 2 + kap
                    pk = (i % 4) * 2 + kap
                    P.op("dve", lambda e, tk=tk, bk=bk, kap=kap, g=g, r=r, bp=bp: e.scalar_tensor_tensor(
                        tmpS[tk], bank(bk).rearrange("p (h q) -> p h q", h=4), 0.125,
                        Bsw[bp][:, g * 2 + r, :, kap * 128:(kap + 1) * 128], ALU.mult, ALU.add),
                        reads=[rbank[bk], r_Bsw[bp]], writes=[r_tmpS[tk]])
                    P.op("act", lambda e, tk=tk, pk=pk: e.activation(PTs[pk], tmpS[tk], AF.Exp),
                         reads=[r_tmpS[tk]], writes=[r_PTs[pk]])

            def sB(i):
                j, g, r = unpack(i)
                ab = 4 + i % 2
                acc = bank(ab).rearrange("p (h c) -> p h c", h=4)
                for hh in range(4):
                    for kap in range(2):
                        blk = LB(j) - 1 + kap
                        pk = (i % 4) * 2 + kap
                        mm(acc[:, hh, 0:65], PTs[pk][:, hh, :], Va[:, blk, g, :], kap == 0, kap == 1,
                           [r_PTs[pk], r_Va[blk], r_Vaones], [rbank[ab]])

            def sC(i):
                j, g, r = unpack(i)
                ab = 4 + i % 2
                acc = bank(ab).rearrange("p (h c) -> p h c", h=4)
                dp = i % 4
                hsl = slice((g * 2 + r) * 4, (g * 2 + r) * 4 + 4)
                P.op("dve", lambda e, dp=dp, acc=acc, hsl=hsl: e.tensor_tensor(den[dp][:, 0:4], acc[:, :, 64], esink[:, hsl], ALU.add),
                     reads=[rbank[ab], r_const], writes=[r_den[dp]])
                P.op("dve", lambda e, dp=dp: e.reciprocal(den[dp][:, 4:8], den[dp][:, 0:4]),
                     reads=[r_den[dp]], writes=[r_rden[dp]])

            def sD(i):
                j, g, r = unpack(i)
                bp = j % 2
                ab = 4 + i % 2
                acc = bank(ab).rearrange("p (h c) -> p h c", h=4)
                dp = i % 4
                heads = [2 * ii + r for ii in range(4 * g, 4 * g + 4)]
                for hh in range(4):
                    h = heads[hh]
                    P.op("dve", lambda e, dp=dp, acc=acc, hh=hh, h=h, bp=bp: e.tensor_scalar_mul(
                        oa[bp][:, h * 64:(h + 1) * 64], acc[:, hh, 0:64], den[dp][:, 4 + hh:5 + hh]),
                        reads=[rbank[ab], r_rden[dp]], writes=[r_oa[bp][(g * 2 + r) * 4 + hh]])
                if i % 4 == 3:
                    b = next_tbank()
                    pb = bank(b).bitcast(BF16)
                    for c in range(8):
                        P.op("pe", lambda e, pb=pb, c=c, bp=bp: e.transpose(pb[:, c * 128:(c + 1) * 128], oa[bp][:, c * 128:(c + 1) * 128], idb),
                             reads=r_oa[bp] + [r_idb], writes=[rbank[b]])
                    cp(oaT[:, :, j * 128:(j + 1) * 128], pb.rearrange("p (c n) -> p c n", c=8), [rbank[b]], [r_oaT[j]])

            pipeline(NB * 4, [sQ, sA, lambda i: None, sB, sC, sD])
        P.barrier(scratch)
        A.reset(xT_mark)

        if stage >= 4:
            KTh = [A.alloc([128, NTOKX], BF16) for _ in range(2)]
            Vh = [A.alloc([128, 74, 129], BF16) for _ in range(2)]
            r_KTh, r_Vh, r_Vhones = P.Rs(2), P.Rs(2), P.Rs(2)
            Bn = A.alloc([128, 5, 512], F32)
            r_Bn = P.R()
            Bf = A.alloc([128, 2, 64, 8], F32)
            r_Bf = P.R()
            tmpD = [A.alloc([128, 2, 512], F32) for _ in range(2)]
            r_tmpD = P.Rs(2)
            PT = [A.alloc([128, 2, 512], BF16) for _ in range(2)]
            r_PT = P.Rs(2)
            accS = A.alloc([128, 3, 512], F32)
            r_accS = P.Rs(3)
            o0 = [A.alloc([128, 128], F32) for _ in range(4)]
            o1 = [A.alloc([128, 128], F32) for _ in range(4)]
            obk = A.alloc([128, 4, 128], BF16)
            st4 = [A.alloc([128, 8], F32) for _ in range(4)]
            r_r0, r_r1, r_r1l, r_o0, r_o1, r_ss, r_ln, r_rstd, r_obk = [P.Rs(4) for _ in range(9)]
            dma("sp", Bf.rearrange("p a b c -> p (a b c)"), bias_far, [], [r_Bf])
            zt = A.alloc([128, 512], BF16)
            r_zt = P.R()
            P.op("pool", lambda e: e.memset(zt, 0.0), writes=[r_zt])
            for k in range(2):
                P.op("pool", lambda e, k=k: e.memset(Vh[k][:, :, 128:129], 1.0), writes=[r_Vhones[k]])
            def accap(a):
                return bank(4 + a // 3)[:, (a % 3) * 160:(a % 3) * 160 + 129]

            def accS_ap(a):
                return accS[:, a // 3, (a % 3) * 160:(a % 3) * 160 + 129]

            def make_stages(h, ch):
                def g0():
                    for qb in range(4):
                        s4, a0, a1 = st4[qb], accS_ap(qb), accS_ap(4 + qb)
                        P.op("dve", lambda e, s4=s4, a0=a0: e.reciprocal(s4[:, 0:1], a0[:, 128:129]),
                             reads=[r_accS[qb // 3]], writes=[r_r0[qb]])
                        P.op("dve", lambda e, s4=s4, a1=a1: e.reciprocal(s4[:, 1:2], a1[:, 128:129]),
                             reads=[r_accS[(4 + qb) // 3]], writes=[r_r1[qb]])

                def g1():
                    for qb in range(4):
                        s4, a0 = st4[qb], accS_ap(qb)
                        P.op("dve", lambda e, s4=s4: e.tensor_tensor(s4[:, 2:3], s4[:, 1:2], neglam, ALU.mult),
                             reads=[r_r1[qb], r_const], writes=[r_r1l[qb]])
                        P.op("dve", lambda e, s4=s4, a0=a0, qb=qb: e.tensor_scalar_mul(o0[qb], a0[:, 0:128], s4[:, 0:1]),
                             reads=[r_accS[qb // 3], r_r0[qb]], writes=[r_o0[qb]])

                def g2():
                    for qb in range(4):
                        s4, a1 = st4[qb], accS_ap(4 + qb)
                        P.op("dve", lambda e, s4=s4, a1=a1, qb=qb: e.scalar_tensor_tensor(o1[qb], a1[:, 0:128], s4[:, 2:3], o0[qb], ALU.mult, ALU.add),
                             reads=[r_accS[(4 + qb) // 3], r_r1l[qb], r_o0[qb]], writes=[r_o1[qb]])
                        P.op("dve", lambda e, s4=s4: e.memset(s4[:, 3:4], 0.0), writes=[r_ss[qb]])

                def g3():
                    for qb in range(4):
                        s4 = st4[qb]
                        P.op("act", lambda e, s4=s4, qb=qb: e.activation(o0[qb], o1[qb], AF.Square, accum_out=s4[:, 3:4]),
                             reads=[r_o1[qb]], writes=[r_ss[qb], r_o0[qb]])

                def g4():
                    for qb in range(4):
                        s4 = st4[qb]
                        P.op("act", lambda e, s4=s4: e.activation(s4[:, 4:5], s4[:, 3:4], AF.Ln, bias=epsc, scale=1.0 / 128.0),
                             reads=[r_ss[qb], r_const], writes=[r_ln[qb]])

                def g5():
                    for qb in range(4):
                        s4 = st4[qb]
                        P.op("act", lambda e, s4=s4: e.activation(s4[:, 5:6], s4[:, 4:5], AF.Exp, scale=-0.5),
                             reads=[r_ln[qb]], writes=[r_rstd[qb]])

                def g6():
                    for qb in range(4):
                        s4 = st4[qb]
                        P.op("dve", lambda e, s4=s4, qb=qb: e.scalar_tensor_tensor(obk[:, qb, :], o1[qb], s4[:, 5:6], wsub, ALU.mult, ALU.mult),
                             reads=[r_o1[qb], r_rstd[qb], r_const], writes=[r_obk[qb]])

                def g7():
                    pb = bank(7).bitcast(BF16)
                    for qb in range(4):
                        P.op("pe", lambda e, pb=pb, qb=qb: e.transpose(pb[:, qb * 128:(qb + 1) * 128], obk[:, qb, :], idb),
                             reads=[r_obk[qb], r_idb], writes=[rbank[7]])
                    cp(obT[:, h, ch * 512:(ch + 1) * 512], pb[:, 0:512], [rbank[7]], [r_obT[h][ch * 4 + qb] for qb in range(4)], eng="dve")

                return [g0, g1, g2, g3, g4, g5, g6, g7]

            sc = [pp[0][:], pp[1][:]]
            r_sc = P.Rs(2)
            r_acc = P.Rs(3)
            pending = []
            for h in range(8):
                hp = h % 2
                dma("sp", KTh[hp][:, 0:7680], kt_all[h][:, 0:7680], [r_ktall], [r_KTh[hp]])
                dma("sp", KTh[hp][:, S:NTOKX], kt_all[h][:, S:NTOKX], [r_ktall], [r_KTh[hp]])
                dma("sp", Vh[hp][:, 0:60, 0:128], v_all[0:60, :, h * 128:(h + 1) * 128].rearrange("b p n -> p b n"), [r_vall], [r_Vh[hp]])
                dma("sp", Vh[hp][:, 64:74, 0:128], v_all[64:74, :, h * 128:(h + 1) * 128].rearrange("b p n -> p b n"), [r_vall], [r_Vh[hp]])
                for ch in range(2):
                    dma("sp", Bn.rearrange("p a b -> p (a b)"), bias_near[ch, h], [], [r_Bn])
                    tiles = [("far", kb) for kb in range(NFAR[ch])] + [("near", kap) for kap in range(5)]
                    nt = len(tiles)
                    qsl = slice(ch * 512, (ch + 1) * 512)

                    def emit_qk(t):
                        kind, idx = tiles[t]
                        sp_ = t % 2
                        tok0 = idx * 128 if kind == "far" else S + (ch * 5 + idx) * 128
                        for m in range(2):
                            mm(sc[sp_][:, m * 512:(m + 1) * 512], KTh[hp][m * 64:(m + 1) * 64, tok0:tok0 + 128],
                               QbT[m * 64:(m + 1) * 64, h, qsl], True, True,
                               [r_KTh[hp], r_QbT[h][ch]], [r_sc[sp_]])

                    emit_qk(0)
                    emit_qk(1)
                    for ab in range(3):
                        mm(bank(4 + ab), zt[:, 0:128], zt, True, False, [r_zt], [r_acc[ab]])
                    for t in range(nt):
                        kind, idx = tiles[t]
                        sp_ = t % 2
                        if kind == "far":
                            bias_ap = Bf[:, ch, idx, h:h + 1]
                            P.op("act", lambda e, sp_=sp_, bias_ap=bias_ap: e.activation(
                                PT[sp_].rearrange("p a b -> p (a b)"), sc[sp_], AF.Exp, bias=bias_ap, scale=0.125),
                                reads=[r_sc[sp_], r_Bf], writes=[r_PT[sp_]])
                        else:
                            for m in range(2):
                                bn_ap = Bn[:, idx, :]
                                P.op("dve", lambda e, sp_=sp_, bn_ap=bn_ap, m=m: e.scalar_tensor_tensor(
                                    tmpD[sp_][:, m, :], sc[sp_][:, m * 512:(m + 1) * 512], 0.125, bn_ap, ALU.mult, ALU.add),
                                    reads=[r_sc[sp_], r_Bn], writes=[r_tmpD[sp_]])
                            P.op("act", lambda e, sp_=sp_: e.activation(PT[sp_], tmpD[sp_], AF.Exp),
                                 reads=[r_tmpD[sp_]], writes=[r_PT[sp_]])
                        if t + 2 < nt:
                            emit_qk(t + 2)
                        vb = idx if kind == "far" else 64 + ch * 5 + idx
                        for m in range(2):
                            for qb in range(4):
                                a = m * 4 + qb
                                mm(accap(a), PT[sp_][:, m, qb * 128:(qb + 1) * 128], Vh[hp][:, vb, :], False, (t == nt - 1) and a in (2, 5, 7),
                                   [r_PT[sp_], r_Vh[hp], r_Vhones[hp]], [r_acc[a // 3]])
                        if pending and t >= 2:
                            pending.pop(0)()
                    while pending:
                        pending.pop(0)()
                    for ab in range(3):
                        cp(accS[:, ab, :], bank(4 + ab), [r_acc[ab]], [r_accS[ab]], eng="dve")
                    pending = make_stages(h, ch)
            while pending:
                pending.pop(0)()
        P.barrier(scratch)
        A.reset(xT_mark)

        y = None

        def layer_norm_and_T(y, r_y, lnw, r_lnw, hT, r_hT, stt, final_out=None, r_out=None):
            r_st = [P.Rs(4) for _ in range(NB)]
            r_mv, r_rs, r_nm = P.Rs(NB), P.Rs(NB), P.Rs(NB)

            def s0(b):
                yb, s = y[:, b, :], stt[:, b, :]
                stats = s[:, 0:24].rearrange("p (k s) -> p k s", k=4)
                for k in range(4):
                    P.op("dve", lambda e, k=k, yb=yb, stats=stats: e.bn_stats(stats[:, k, :], yb[:, k * 512:(k + 1) * 512]),
                         reads=[r_y[b][k]], writes=[r_st[b][k]])
                P.op("dve", lambda e, s=s, stats=stats: e.bn_aggr(s[:, 24:26], stats), reads=r_st[b], writes=[r_mv[b]])

            def s1(b):
                s = stt[:, b, :]
                P.op("act", lambda e, s=s: e.activation(s[:, 27:28], s[:, 25:26], AF.Ln, bias=epsc), reads=[r_mv[b], r_const], writes=[r_rs[b]])
                P.op("act", lambda e, s=s: e.activation(s[:, 26:27], s[:, 27:28], AF.Exp, scale=-0.5), reads=[r_rs[b]], writes=[r_rs[b]])

            def s2(b):
                s = stt[:, b, :]
                P.op("dve", lambda e, s=s: e.scalar_tensor_tensor(s[:, 28:29], s[:, 24:25], -1.0, s[:, 26:27], ALU.mult, ALU.mult),
                     reads=[r_mv[b], r_rs[b]], writes=[r_nm[b]])

            def s3(b):
                yb, s = y[:, b, :], stt[:, b, :]
                P.op("act", lambda e, s=s, yb=yb: e.activation(yb, yb, AF.Identity, bias=s[:, 28:29], scale=s[:, 26:27]),
                     reads=[r_nm[b], r_rs[b]] + r_y[b], writes=r_y[b])

            def s4(b):
                yb = y[:, b, :]
                P.op("dve", lambda e, yb=yb: e.tensor_tensor(yb, yb, lnw[:, 0, :], ALU.mult), reads=[r_lnw] + r_y[b], writes=r_y[b])

            def s5(b):
                yb = y[:, b, :]
                P.op("pool", lambda e, yb=yb: e.tensor_tensor(yb, yb, lnw[:, 1, :], ALU.add), reads=[r_lnw] + r_y[b], writes=r_y[b])

            def s6(b):
                yb = y[:, b, :]
                if final_out is not None:
                    dma("sp", final_out[b * 128:(b + 1) * 128, :], yb, r_y[b], [r_out])
                    return
                for q4 in range(4):
                    bk = next_tbank()
                    for c in range(4):
                        cc = q4 * 4 + c
                        P.op("pe", lambda e, bk=bk, c=c, cc=cc, yb=yb: e.transpose(bank(bk)[:, c * 128:(c + 1) * 128], yb[:, cc * 128:(cc + 1) * 128], idf),
                             reads=r_y[b] + [r_idf], writes=[rbank[bk]])
                    cp(hT[:, q4 * 4:(q4 + 1) * 4, b * 128:(b + 1) * 128], bank(bk).rearrange("p (c n) -> p c n", c=4),
                       [rbank[bk]], [r_hT[b][q4]])

            pipeline(NB, [s0, s1, s2, s3, s4, s5, s6])

        if stage >= 5:
            A.reset(117 * 1024)
            mixT = A.alloc([128, 16, TOK], BF16)
            r_mixT = [P.Rs(2) for _ in range(16)]
            e_mark = A.mark()
            Wg = [A.alloc([128, 16, 512], BF16) for _ in range(2)]
            Wab = [A.alloc([128, 8, 512], BF16) for _ in range(2)]
            r_Wg, r_Wab = P.Rs(2), P.Rs(2)
            sg = [A.alloc([128, 512], F32) for _ in range(2)]
            r_sg = P.Rs(2)
            m1 = [A.alloc([128, 512], F32) for _ in range(2)]
            r_m1 = P.Rs(2)
            all_oaT = r_oaT
            all_obT = [r for hh in range(8) for r in r_obT[hh]]
            gi = 0
            for jj in range(8):
                k = jj % 2
                dma("pool", Wg[k][:, :, 0:256], w_in[:, C_GA + jj * 256:C_GA + (jj + 1) * 256].rearrange("(c p) n -> p c n", p=128), [], [r_Wg[k]])
                dma("pool", Wg[k][:, :, 256:512], w_in[:, C_GB + jj * 256:C_GB + (jj + 1) * 256].rearrange("(c p) n -> p c n", p=128), [], [r_Wg[k]])
                dma("pool", Wab[k][:, :, 0:256], w_a[:, jj * 256:(jj + 1) * 256].rearrange("(c p) n -> p c n", p=128), [], [r_Wab[k]])
                dma("pool", Wab[k][:, :, 256:512], w_b[:, jj * 256:(jj + 1) * 256].rearrange("(c p) n -> p c n", p=128), [], [r_Wab[k]])
                for sub in range(2):
                    j = jj * 2 + sub
                    for hlf in range(2):
                        tsl = slice(hlf * 512, (hlf + 1) * 512)
                        for br in range(2):
                            gk = gi % 2
                            gi += 1
                            bg = next_mbank()
                            for c in range(16):
                                mm(bank(bg), Wg[k][:, c, br * 256 + sub * 128:br * 256 + (sub + 1) * 128],
                                   xT[:, c, OWN0[hlf]:OWN0[hlf] + 512], c == 0, c == 15, [r_Wg[k]] + allx, [rbank[bg]])
                            P.op("act", lambda e, gk=gk, bg=bg: e.activation(sg[gk], bank(bg), AF.Sigmoid),
                                 reads=[rbank[bg]], writes=[r_sg[gk]])
                            bb = next_mbank()
                            srcT = oaT if br == 0 else obT
                            rsrc = all_oaT if br == 0 else all_obT
                            for c in range(8):
                                mm(bank(bb), Wab[k][:, c, br * 256 + sub * 128:br * 256 + (sub + 1) * 128], srcT[:, c, tsl],
                                   c == 0, c == 7, [r_Wab[k]] + rsrc, [rbank[bb]])
                            if br == 0:
                                mk = (gi // 2) % 2
                                P.op("dve", lambda e, mk=mk, bb=bb, gk=gk: e.tensor_tensor(m1[mk], bank(bb), sg[gk], ALU.mult),
                                     reads=[rbank[bb], r_sg[gk]], writes=[r_m1[mk]])
                                mk_a = mk
                            else:
                                P.op("dve", lambda e, bb=bb, gk=gk: e.tensor_tensor(sg[gk], bank(bb), sg[gk], ALU.mult),
                                     reads=[rbank[bb], r_sg[gk]], writes=[r_sg[gk]])
                                P.op("dve", lambda e, gk=gk, mk_a=mk_a, j=j, tsl=tsl: e.tensor_tensor(mixT[:, j, tsl], m1[mk_a], sg[gk], ALU.add),
                                     reads=[r_m1[mk_a], r_sg[gk]], writes=[r_mixT[j][hlf]])
            P.barrier(scratch)
            A.reset(M0)
            y = A.alloc([128, NB, D], F32)
            r_y = [P.Rs(4) for _ in range(NB)]
            lnw = A.alloc([128, 2, D], F32)
            r_lnw = P.R()
            stt = A.alloc([128, NB, 32], F32)
            hT = A.alloc([128, 16, TOK], BF16)
            r_hT = [P.Rs(4) for _ in range(NB)]
            h_mark = A.mark()
            assert h_mark <= 117 * 1024
            A.reset(e_mark)
            Wo = [A.alloc([128, 16, 512], BF16) for _ in range(2)]
            r_Wo = P.Rs(2)
            dma("pool", Wo[0], w_o[:, 0:512].rearrange("(c p) n -> p c n", p=128), [], [r_Wo[0]])
            for b in range(NB):
                dma("sp", y[:, b, :], x_own[LB(b) * 128:(LB(b) + 1) * 128, :], [r_Wo[0]], r_y[b])
            dma("sp", lnw, lnp[:, 0:2, :], [], [r_lnw])
            all_mix = [r for j in range(16) for r in r_mixT[j]]
            for pn in range(4):
                k = pn % 2
                if pn > 0:
                    dma("pool", Wo[k], w_o[:, pn * 512:(pn + 1) * 512].rearrange("(c p) n -> p c n", p=128), [], [r_Wo[k]])
                for b in range(NB):
                    bk = next_mbank()
                    for c in range(16):
                        mm(bank(bk), mixT[:, c, b * 128:(b + 1) * 128], Wo[k][:, c, :], c == 0, c == 15,
                           [r_Wo[k], r_mixT[c][b // 4]], [rbank[bk]])
                    ysl = y[:, b, pn * 512:(pn + 1) * 512]
                    P.op("dve", lambda e, ysl=ysl, bk=bk: e.scalar_tensor_tensor(ysl, ysl, ALPHA, bank(bk), ALU.mult, ALU.add),
                         reads=[rbank[bk], r_y[b][pn]], writes=[r_y[b][pn]])
            A.reset(181 * 1024)
            Wq = A.alloc([128, 16, 512], BF16)
            r_Wq = P.R()
            dma("pool", Wq, w_cq.rearrange("(c p) n -> p c n", p=128), [], [r_Wq])
            layer_norm_and_T(y, r_y, lnw, r_lnw, hT, r_hT, stt)
            all_hT = [r for b in range(NB) for r in r_hT[b]]

        if stage >= 6:
            P.barrier(scratch)
            A.reset(h_mark)
            KcT = A.alloc([128, 4, 256], BF16)
            Vc = A.alloc([128, 2, 4, 129], BF16)
            r_KcT = [P.R()] * 4
            r_Vc = P.Rs(2)
            r_Vcones = P.R()
            qcT = A.alloc([128, 4, TOK], BF16)
            r_qcT = [P.Rs(2) for _ in range(4)]
            oc = A.alloc([128, NB, 512], BF16)
            r_oc = [P.Rs(4) for _ in range(NB)]
            ocT = A.alloc([128, 4, TOK], BF16)
            r_ocT = P.Rs(NB)
            Wco = A.alloc([128, 4, D], BF16)
            r_Wco = P.R()
            PTc = [A.alloc([128, 512], BF16) for _ in range(4)]
            r_PTc = P.Rs(4)
            rc = [A.alloc([128, 4], F32) for _ in range(2)]
            r_rc = P.Rs(2)
            assert A.mark() <= 181 * 1024
            dma("sp", KcT.rearrange("p a b -> p (a b)"), kc_s, [r_kcd], [r_KcT[0]])
            P.op("pool", lambda e: e.memset(Vc[:, :, :, 128:129], 1.0), writes=[r_Vcones])
            for mbk in range(2):
                dma("sp", Vc[:, mbk, :, 0:128], vc_s[:, mbk * 512:(mbk + 1) * 512].rearrange("p (h d) -> p h d", h=4), [r_vcd], [r_Vc[mbk]])
            dma("sp", lnw, lnp[:, 2:4, :], [], [r_lnw])
            dma("pool", Wco, w_co.rearrange("(c p) n -> p c n", p=128), [], [r_Wco])
            for h in range(4):
                for hlf in range(2):
                    bk = next_mbank()
                    for c in range(16):
                        mm(bank(bk), Wq[:, c, h * 128:(h + 1) * 128], hT[:, c, hlf * 512:(hlf + 1) * 512], c == 0, c == 15,
                           [r_Wq] + all_hT, [rbank[bk]])
                    cp(qcT[:, h, hlf * 512:(hlf + 1) * 512], bank(bk), [rbank[bk]], [r_qcT[h][hlf]])
            ci = 0
            CSC = 128.0 ** -0.5
            for h in range(4):
                for hlf in range(2):
                    pts = []
                    for mbk in range(2):
                        bk = next_mbank(4)
                        mm(bank(bk), KcT[:, h, mbk * 128:(mbk + 1) * 128], qcT[:, h, hlf * 512:(hlf + 1) * 512], True, True,
                           [r_KcT[h], r_qcT[h][hlf]], [rbank[bk]])
                        pk = ci % 4
                        ci += 1
                        P.op("act", lambda e, pk=pk, bk=bk: e.activation(PTc[pk], bank(bk), AF.Exp, scale=CSC),
                             reads=[rbank[bk]], writes=[r_PTc[pk]])
                        pts.append(pk)
                    ab0 = 4 + 2 * ((h * 2 + hlf) % 2)
                    def cacc(qb, ab0=ab0):
                        return bank(ab0 + qb // 2)[:, (qb % 2) * 160:(qb % 2) * 160 + 129]
                    for qb in range(4):
                        for mbk in range(2):
                            mm(cacc(qb), PTc[pts[mbk]][:, qb * 128:(qb + 1) * 128],
                               Vc[:, mbk, h, :], mbk == 0, mbk == 1, [r_PTc[pts[mbk]], r_Vc[mbk], r_Vcones], [rbank[ab0 + qb // 2]])
                    rk = (h * 2 + hlf) % 2
                    for qb in range(4):
                        a_ap = cacc(qb)
                        blk = hlf * 4 + qb
                        P.op("dve", lambda e, rk=rk, qb=qb, a_ap=a_ap: e.reciprocal(rc[rk][:, qb:qb + 1], a_ap[:, 128:129]),
                             reads=[rbank[ab0 + qb // 2]], writes=[r_rc[rk]])
                        P.op("dve", lambda e, rk=rk, qb=qb, a_ap=a_ap, blk=blk, h=h: e.tensor_scalar_mul(
                            oc[:, blk, h * 128:(h + 1) * 128], a_ap[:, 0:128], rc[rk][:, qb:qb + 1]),
                            reads=[rbank[ab0 + qb // 2], r_rc[rk]], writes=[r_oc[blk][h]])
            for b in range(NB):
                bk = next_tbank()
                pb = bank(bk).bitcast(BF16)
                for c in range(4):
                    P.op("pe", lambda e, pb=pb, c=c, b=b: e.transpose(pb[:, c * 128:(c + 1) * 128], oc[:, b, c * 128:(c + 1) * 128], idb),
                         reads=r_oc[b] + [r_idb], writes=[rbank[bk]])
                cp(ocT[:, :, b * 128:(b + 1) * 128], pb[:, 0:512].rearrange("p (c n) -> p c n", c=4), [rbank[bk]], [r_ocT[b]])
            for b in range(NB):
                for pn in range(4):
                    bk = next_mbank()
                    for c in range(4):
                        mm(bank(bk), ocT[:, c, b * 128:(b + 1) * 128], Wco[:, c, pn * 512:(pn + 1) * 512], c == 0, c == 3,
                           [r_Wco, r_ocT[b]], [rbank[bk]])
                    ysl = y[:, b, pn * 512:(pn + 1) * 512]
                    P.op("dve", lambda e, ysl=ysl, bk=bk: e.scalar_tensor_tensor(ysl, ysl, ALPHA, bank(bk), ALU.mult, ALU.add),
                         reads=[rbank[bk], r_y[b][pn]], writes=[r_y[b][pn]])
            layer_norm_and_T(y, r_y, lnw, r_lnw, hT, r_hT, stt)
            P.barrier(scratch)
            A.reset(h_mark)

        if stage >= 7:
            dma("sp", lnw, lnp[:, 4:6, :], [], [r_lnw])
            Wgu = [A.alloc([128, 16, 512], BF16) for _ in range(2)]
            Wd = [A.alloc([128, 2, D], BF16) for _ in range(2)]
            r_Wgu, r_Wd = P.Rs(2), P.Rs(2)
            Aff = [A.alloc([128, 2, TOK], BF16) for _ in range(2)]
            r_Aff = [[P.Rs(2) for _ in range(2)] for _ in range(2)]
            sgf = [A.alloc([128, 512], F32) for _ in range(2)]
            r_sgf = P.Rs(2)
            fi = 0
            NSC = DFF // 256
            for s in range(NSC):
                k = s % 2
                dma("pool", Wgu[k][:, :, 0:256], w_gu[:, s * 256:(s + 1) * 256].rearrange("(c p) n -> p c n", p=128), [], [r_Wgu[k]])
                dma("pool", Wgu[k][:, :, 256:512], w_gu[:, DFF + s * 256:DFF + (s + 1) * 256].rearrange("(c p) n -> p c n", p=128), [], [r_Wgu[k]])
                dma("pool", Wd[k], w_dn[s * 256:(s + 1) * 256, :].rearrange("(c p) n -> p c n", p=128), [], [r_Wd[k]])
                for sub in range(2):
                    for hlf in range(2):
                        bg = next_mbank()
                        for c in range(16):
                            mm(bank(bg), Wgu[k][:, c, sub * 128:(sub + 1) * 128], hT[:, c, hlf * 512:(hlf + 1) * 512], c == 0, c == 15,
                               [r_Wgu[k]] + all_hT, [rbank[bg]])
                        bu = next_mbank()
                        for c in range(16):
                            mm(bank(bu), Wgu[k][:, c, 256 + sub * 128:256 + (sub + 1) * 128], hT[:, c, hlf * 512:(hlf + 1) * 512], c == 0, c == 15,
                               [r_Wgu[k]] + all_hT, [rbank[bu]])
                        fk = fi % 2
                        fi += 1
                        P.op("act", lambda e, fk=fk, bg=bg: e.activation(sgf[fk], bank(bg), AF.Silu), reads=[rbank[bg]], writes=[r_sgf[fk]])
                        P.op("dve", lambda e, fk=fk, bu=bu, k=k, sub=sub, hlf=hlf: e.tensor_tensor(
                            Aff[k][:, sub, hlf * 512:(hlf + 1) * 512], bank(bu), sgf[fk], ALU.mult),
                            reads=[rbank[bu], r_sgf[fk]], writes=[r_Aff[k][sub][hlf]])
                for b in range(NB):
                    for pn in range(4):
                        bk = next_mbank()
                        for sub in range(2):
                            mm(bank(bk), Aff[k][:, sub, b * 128:(b + 1) * 128], Wd[k][:, sub, pn * 512:(pn + 1) * 512], sub == 0, sub == 1,
                               [r_Wd[k], r_Aff[k][sub][b // 4]], [rbank[bk]])
                        ysl = y[:, b, pn * 512:(pn + 1) * 512]
                        if s == 0:
                            P.op("dve", lambda e, ysl=ysl, bk=bk: e.scalar_tensor_tensor(ysl, ysl, ALPHA, bank(bk), ALU.mult, ALU.add),
                                 reads=[rbank[bk], r_y[b][pn]], writes=[r_y[b][pn]])
                        else:
                            P.op("dve", lambda e, ysl=ysl, bk=bk: e.tensor_tensor(ysl, ysl, bank(bk), ALU.add),
                                 reads=[rbank[bk], r_y[b][pn]], writes=[r_y[b][pn]])
            r_out = P.R()
            layer_norm_and_T(y, r_y, lnw, r_lnw, None, None, stt, final_out=out, r_out=r_out)
            P.final_wait(r_out)

        if dbg is not None:
            P.barrier(scratch)
            r_dbg = P.R()
            if stage == 3:
                stg = A.alloc([128, 8, TOK], F32)
                P.op("dve", lambda e: e.tensor_copy(stg, oaT), reads=r_oaT, writes=[r_dbg])
                dma("sp", dbg_out.rearrange("(c p) t -> p c t", p=128), stg, [r_dbg], [r_dbg])
            elif stage == 4:
                stg = A.alloc([128, 8, TOK], F32)
                P.op("dve", lambda e: e.tensor_copy(stg, obT), reads=[r for hh in range(8) for r in r_obT[hh]], writes=[r_dbg])
                dma("sp", dbg_out.rearrange("(c p) t -> p c t", p=128), stg, [r_dbg], [r_dbg])
            elif stage in (5, 6):
                for b in range(NB):
                    dma("sp", dbg_out[b * 128:(b + 1) * 128, :], y[:, b, :], r_y[b], [r_dbg])
            P.final_wait(r_dbg)
        P.emit()
    return nc


def _rel_bucket(dist):
    n = np.maximum(dist, 0)
    exact = 16
    logv = np.log(np.maximum(n, 1).astype(np.float32) / np.float32(exact)) / np.float32(math.log(128 / 16))
    large = exact + (logv * np.float32(32 - exact)).astype(np.int32)
    large = np.minimum(large, 31)
    return np.where(n < exact, n, large)


def _bias_tables(table, core):
    k = np.arange(128)[:, None]
    q = np.arange(128)[None, :]
    sw = np.full((2, 128, 16, 2, 128), NEG, np.float32)
    for kap in range(2):
        dist = (128 + q - k) if kap == 0 else (q - k)
        ok = (dist >= 0) & (dist < 128)
        bk = _rel_bucket(dist)
        for h in range(16):
            vals = np.where(ok, table[bk, h], np.float32(NEG)).astype(np.float32)
            sw[0, :, h, kap, :] = vals
            if kap == 1:
                sw[1, :, h, kap, :] = vals
    if core != 0:
        sw[1] = sw[0]
    perm = [8 * g + 2 * i4 + r for g in range(2) for r in range(2) for i4 in range(4)]
    sw = sw[:, :, perm]
    q5 = np.arange(512)[None, :]
    near = np.full((2, 8, 128, 5, 512), NEG, np.float32)
    for kap in range(5):
        dist = q5 - 128 * (kap - 1) - k
        ok = dist >= 0
        bk = _rel_bucket(dist)
        for h in range(8):
            vals = np.where(ok, table[bk, 16 + h], np.float32(NEG)).astype(np.float32)
            near[:, h, :, kap, :] = vals[None]
    if core == 0:
        near[0, :, :, 0, :] = NEG
    far = np.full((128, 2, 64, 8), NEG, np.float32)
    for ch in range(2):
        g0 = 4 * (core if ch == 0 else 15 - core)
        for kb in range(64):
            if kb <= g0 - 2:
                far[:, ch, kb, :] = table[31, 16:24][None, :]
    return sw.reshape(2, 128, -1), near.reshape(2, 8, 128, -1), far.reshape(128, -1)


_NC_CACHE = {}


def kernel(x, mem, rel_bias_table, w_in, sinks, lambda_q1, lambda_k1, lambda_q2, lambda_k2,
           subln_w, w_branch_a, w_branch_b, w_o, ln1_g, ln1_b, w_cq, w_mem_kv, w_co,
           ln2_g, ln2_b, w_gate_up, w_down, ln3_g, ln3_b, _stage=99, _dbg=None):
    f = lambda a: np.ascontiguousarray(np.asarray(a, dtype=np.float32))
    x2 = f(x).reshape(S, D)
    table = f(rel_bias_table)
    lnp = np.stack([f(ln1_g)[0], f(ln1_b)[0], f(ln2_g)[0], f(ln2_b)[0], f(ln3_g)[0], f(ln3_b)[0]], 0)
    lnp = np.ascontiguousarray(np.broadcast_to(lnp[None], (128, 6, D)))
    perm = [8 * g + 2 * i4 + r for g in range(2) for r in range(2) for i4 in range(4)]
    small = np.concatenate([f(sinks)[0][perm], f(lambda_q1)[0], f(lambda_k1)[0], f(lambda_q2)[0], f(lambda_k2)[0], f(subln_w)[0]])
    small = np.ascontiguousarray(np.broadcast_to(small[None], (128, small.shape[0])))
    shared = {
        "x_all": x2, "mem": f(mem)[0], "w_in": f(w_in)[0], "w_branch_a": f(w_branch_a)[0], "w_branch_b": f(w_branch_b)[0],
        "w_o": f(w_o)[0], "w_cq": f(w_cq)[0], "w_mem_kv": f(w_mem_kv)[0], "w_co": f(w_co)[0],
        "w_gate_up": f(w_gate_up)[0], "w_down": f(w_down)[0], "lnp": lnp, "smallp": small,
        "ident": np.eye(128, dtype=np.float32),
    }
    in_maps = []
    for c in range(NCORES):
        xo = np.zeros((NOWN, D), np.float32)
        for ch, gc in enumerate((c, 15 - c)):
            r0 = ch * 640
            if gc > 0:
                xo[r0:r0 + 128] = x2[gc * 512 - 128:gc * 512]
            xo[r0 + 128:r0 + 640] = x2[gc * 512:(gc + 1) * 512]
        sw, near, far = _bias_tables(table, c)
        m = dict(shared)
        m.update({"x_own": xo, "bias_swa": sw, "bias_near": near, "bias_far": far})
        in_maps.append(m)
    key = (_stage, _dbg)
    if key not in _NC_CACHE:
        _NC_CACHE[key] = build(_stage, _dbg)
    nc = _NC_CACHE[key]
    res = run_bass_kernel_spmd(nc, in_maps, core_ids=list(range(NCORES)))
    if _dbg is not None:
        return np.concatenate([r["dbg"] for r in res.results], axis=0)
    o = np.empty((S, D), np.float32)
    for c in range(NCORES):
        oc = res.results[c]["out"]
        o[c * 512:(c + 1) * 512] = oc[0:512]
        o[(15 - c) * 512:(16 - c) * 512] = oc[512:1024]
    return o.reshape(1, S, D)
```

```python
import math
import contextlib
import numpy as np
import concourse.bass as bass
import concourse.mybir as mybir
from concourse.bass_utils import run_bass_kernel_spmd

F32 = mybir.dt.float32
BF16 = mybir.dt.bfloat16
AF = mybir.ActivationFunctionType
ALU = mybir.AluOpType
AX = mybir.AxisListType

NCORES = 8
S = 8192
D = 2048
TOK = S // NCORES
NB = TOK // 128
DFF = 5632
NEG = -30000.0
ALPHA = 2.0 ** 0.25
LAMBDA_INIT = 0.8 - 0.6 * math.exp(0.0)
EPS = 1e-5
C_QA, C_KA, C_VA, C_QB, C_KB, C_VB, C_GA, C_GB = 0, 1024, 1152, 1280, 2304, 3328, 4352, 6400
NOWN = 1280
NTOKX = S + NOWN
OWN0 = (128, 768)
NFAR = (27, 59)


def LB(j):
    return j + 1 if j < 4 else j + 2


class Region:
    __slots__ = ("name", "last_w", "readers")

    def __init__(self, name):
        self.name = name
        self.last_w = None
        self.readers = []


class Op:
    __slots__ = ("idx", "eng", "fn", "deps", "dma", "token", "signal", "prewait")

    def __init__(self, idx, eng, fn, dma):
        self.idx = idx
        self.eng = eng
        self.fn = fn
        self.dma = dma
        self.deps = set()
        self.token = None
        self.signal = False
        self.prewait = None


class Prog:
    ENGS = ("pe", "act", "dve", "pool", "sp")
    NDMA_SEM = 8

    def __init__(self, nc, same_engine_sync=True):
        self.nc = nc
        self.ops = []
        self.same_engine_sync = same_engine_sync
        self.finals = []
        self.barrier_idx = None
        self.last_on = {}
        self.dma_since = []

    def R(self, name=""):
        return Region(name)

    def Rs(self, n, name=""):
        return [Region(name) for _ in range(n)]

    def op(self, eng, fn, reads=(), writes=(), dma=False):
        o = Op(len(self.ops), eng, fn, dma)
        for r in reads:
            if r.last_w is not None:
                o.deps.add(r.last_w)
        for w in writes:
            if w.last_w is not None:
                o.deps.add(w.last_w)
            for rd in w.readers:
                o.deps.add(rd)
        for r in reads:
            r.readers.append(o.idx)
        for w in writes:
            w.last_w = o.idx
            w.readers = []
        if self.barrier_idx is not None:
            o.deps.add(self.barrier_idx)
        o.deps.discard(o.idx)
        self.ops.append(o)
        if dma:
            self.dma_since.append(o.idx)
        else:
            self.last_on[eng] = o.idx
        return o

    def barrier(self, scratch):
        o = Op(len(self.ops), "dve", lambda e: e.memset(scratch, 0.0), False)
        for e, i in self.last_on.items():
            o.deps.add(i)
        for i in self.dma_since:
            o.deps.add(i)
        self.dma_since = []
        if self.barrier_idx is not None:
            o.deps.add(self.barrier_idx)
        self.ops.append(o)
        self.last_on["dve"] = o.idx
        self.barrier_idx = o.idx
        return o

    def final_wait(self, region):
        self.finals.append(region.last_w)

    def _needs_sync(self, dep, ename):
        if dep.eng == ename and not dep.dma:
            if ename == "pe" or not self.same_engine_sync:
                return False
        return True

    def emit(self):
        nc = self.nc
        ops = self.ops
        for o in ops:
            for d in o.deps:
                dep = ops[d]
                if self._needs_sync(dep, o.eng):
                    dep.signal = True
        for f in self.finals:
            ops[f].signal = True
        for o in ops:
            if o.dma:
                o.signal = True
        with contextlib.ExitStack() as st:
            esem = {e: st.enter_context(nc.semaphore(f"s_{e}")) for e in ("pe", "act", "dve", "pool")}
            dsem = {q: [st.enter_context(nc.semaphore(f"d_{q}{k}")) for k in range(self.NDMA_SEM)]
                    for q in ("sp", "act", "pool")}
            cnt = {e: 0 for e in esem}
            dcnt = {q: 0 for q in dsem}
            for o in ops:
                if o.dma:
                    i = dcnt[o.eng]
                    dcnt[o.eng] += 1
                    k = i % self.NDMA_SEM
                    m = i // self.NDMA_SEM
                    o.token = (dsem[o.eng][k], 16 * (m + 1))
                    if m > 0:
                        o.prewait = (dsem[o.eng][k], 16 * m)
                elif o.signal:
                    cnt[o.eng] += 1
                    o.token = (esem[o.eng], cnt[o.eng])
            self.sem_counts = dict(cnt)
            block = st.enter_context(nc.Block())
            engobj = {"pe": "tensor", "act": "scalar", "dve": "vector", "pool": "gpsimd", "sp": "sync"}

            def run(ename, eng):
                seen = {}

                def wait(tok):
                    sem, v = tok
                    if seen.get(sem.num, 0) >= v:
                        return
                    eng.wait_ge(sem, v)
                    seen[sem.num] = v

                for o in ops:
                    if o.eng != ename:
                        continue
                    for d in sorted(o.deps, reverse=True):
                        dep = ops[d]
                        if not self._needs_sync(dep, ename):
                            continue
                        wait(dep.token)
                    if o.prewait is not None:
                        wait(o.prewait)
                    ins = o.fn(eng)
                    if o.token is not None:
                        sem, v = o.token
                        ins.then_inc(sem, 16 if o.dma else 1)
                if ename == "sp":
                    for f in self.finals:
                        wait(ops[f].token)

            for ename in self.ENGS:
                getattr(block, engobj[ename])(lambda eng, _e=ename: run(_e, eng))


class Arena:
    def __init__(self, t, nbytes):
        self.t = t
        self.cap = nbytes
        self.off = 0

    def alloc(self, shape, dtype):
        esz = 2 if dtype == BF16 else 4
        n = int(np.prod(shape[1:])) * esz
        off = (self.off + 63) // 64 * 64
        assert off + n <= self.cap, f"SBUF arena overflow: {off + n} > {self.cap}"
        self.off = off + n
        ap = self.t[:, off // 2:(off + n) // 2]
        if dtype != BF16:
            ap = ap.bitcast(dtype)
        if len(shape) == 3:
            ap = ap.rearrange("p (a b) -> p a b", a=shape[1])
        elif len(shape) == 4:
            ap = ap.rearrange("p (a b c) -> p a b c", a=shape[1], b=shape[2])
        elif len(shape) == 5:
            ap = ap.rearrange("p (a b c d) -> p a b c d", a=shape[1], b=shape[2], c=shape[3])
        return ap

    def mark(self):
        return self.off

    def reset(self, m):
        self.off = m


def build(stage=99, dbg=None):
    nc = bass.Bass("TRN2", target_bir_lowering=False)

    def din(name, shape, dt=F32):
        return nc.dram_tensor(name, shape, dt, kind="ExternalInput").ap()

    x_all = din("x_all", [S, D])
    x_own = din("x_own", [NOWN, D])
    mem = din("mem", [256, D])
    w_in = din("w_in", [D, 8448])
    w_a = din("w_branch_a", [1024, D])
    w_b = din("w_branch_b", [1024, D])
    w_o = din("w_o", [D, D])
    w_cq = din("w_cq", [D, 512])
    w_mkv = din("w_mem_kv", [D, 1024])
    w_co = din("w_co", [512, D])
    w_gu = din("w_gate_up", [D, 2 * DFF])
    w_dn = din("w_down", [DFF, D])
    lnp = din("lnp", [128, 6, D])
    smallp = din("smallp", [128, 16 + 256 + 128])
    ident_d = din("ident", [128, 128])
    bias_far = din("bias_far", [128, 2 * 64 * 8])
    bias_near = din("bias_near", [2, 8, 128, 5 * 512])
    bias_swa = din("bias_swa", [2, 128, 16 * 2 * 128])
    out = nc.dram_tensor("out", [TOK, D], F32, kind="ExternalOutput").ap()
    kt_all = nc.dram_tensor("kt_all", [8, 128, NTOKX], BF16).ap()
    v_all = nc.dram_tensor("v_all", [74, 128, 1024], BF16).ap()
    kc_s = nc.dram_tensor("kc_s", [128, 1024], BF16).ap()
    vc_s = nc.dram_tensor("vc_s", [128, 1024], BF16).ap()
    dbg_out = None
    if dbg is not None:
        dbg_out = nc.dram_tensor("dbg", [TOK, dbg], F32, kind="ExternalOutput").ap()

    P = Prog(nc)
    with contextlib.ExitStack() as st:
        ARENA_BYTES = 207 * 1024
        arena_t = st.enter_context(nc.sbuf_tensor("arena", [128, ARENA_BYTES // 2], BF16))
        A = Arena(arena_t, ARENA_BYTES)
        pp = [st.enter_context(nc.psum_tensor(f"pp{i}", [128, 1024], F32)) for i in range(4)]

        def bank(b):
            return pp[b // 2][:, (b % 2) * 512:(b % 2 + 1) * 512]

        rbank = P.Rs(8, "bank")
        cpctr = [0]

        def cp(out_ap, in_ap, reads, writes, eng=None):
            if eng is None:
                eng = ("dve", "act")[cpctr[0] % 2]
                cpctr[0] += 1
            if eng == "act":
                return P.op("act", lambda e: e.activation(out_ap, in_ap, AF.Copy), reads=reads, writes=writes)
            if eng == "pool":
                return P.op("pool", lambda e: e.tensor_copy(out_ap, in_ap), reads=reads, writes=writes)
            return P.op("dve", lambda e: e.tensor_copy(out_ap, in_ap), reads=reads, writes=writes)

        def mm(out_ap, lhsT, rhs, start, stop, reads, writes):
            return P.op("pe", lambda e: e.matmul(out_ap, lhsT, rhs, start=start, stop=stop),
                        reads=reads, writes=writes)

        def dma(q, out_ap, in_ap, reads, writes):
            return P.op(q, lambda e: e.dma_start(out=out_ap, in_=in_ap), reads=reads, writes=writes, dma=True)

        idb = A.alloc([128, 128], BF16)
        idf = A.alloc([128, 128], F32)
        small = A.alloc([128, 400], F32)
        scratch = A.alloc([128, 16], F32)
        esink = A.alloc([128, 16], F32)
        neglam = A.alloc([128, 1], F32)
        wsub = A.alloc([128, 128], F32)
        lam_t = A.alloc([128, 8], F32)
        epsc = A.alloc([128, 1], F32)
        r_idb, r_idf, r_small, r_const = P.R(), P.R(), P.R(), P.R()
        dma("pool", idb, ident_d, [], [r_idb])
        dma("sp", idf, ident_d, [], [r_idf])
        dma("sp", small, smallp, [], [r_small])
        P.op("dve", lambda e: e.memset(epsc, EPS), writes=[r_const])
        P.op("act", lambda e: e.activation(esink, small[:, 0:16], AF.Exp), reads=[r_small], writes=[r_const])
        lq = small[:, 16:272].rearrange("p (a b) -> p a b", a=4)
        prod_t = A.alloc([128, 2, 64], F32)
        P.op("dve", lambda e: e.tensor_tensor(prod_t[:, 0, :], lq[:, 0, :], lq[:, 1, :], ALU.mult), reads=[r_small], writes=[r_const])
        P.op("dve", lambda e: e.tensor_tensor(prod_t[:, 1, :], lq[:, 2, :], lq[:, 3, :], ALU.mult), reads=[r_small], writes=[r_const])
        P.op("dve", lambda e: e.reduce_sum(lam_t[:, 0:2], prod_t, axis=AX.X), reads=[r_const], writes=[r_const])
        P.op("act", lambda e: e.activation(lam_t[:, 2:4], lam_t[:, 0:2], AF.Exp), reads=[r_const], writes=[r_const])
        P.op("dve", lambda e: e.tensor_tensor(lam_t[:, 4:5], lam_t[:, 3:4], lam_t[:, 2:3], ALU.subtract), reads=[r_const], writes=[r_const])
        P.op("dve", lambda e: e.tensor_scalar_add(neglam, lam_t[:, 4:5], -LAMBDA_INIT), reads=[r_const], writes=[r_const])
        P.op("dve", lambda e: e.tensor_scalar_mul(wsub, small[:, 272:400], (1.0 - LAMBDA_INIT)), reads=[r_small], writes=[r_const])
        assert A.mark() <= 4 * 1024
        M0 = 4 * 1024
        A.reset(M0)

        xT = A.alloc([128, 16, NOWN], BF16)
        r_xT = [[P.R() for _ in range(2)] for _ in range(10)]
        after_xT = A.mark()
        QbT = A.alloc([128, 8, TOK], BF16)
        oaT = A.alloc([128, 8, TOK], BF16)
        obT = A.alloc([128, 8, TOK], BF16)
        xT_mark = A.mark()
        assert xT_mark <= 93 * 1024
        A.reset(after_xT)

        Wkb = A.alloc([128, 16, 1024], BF16)
        Wvb = A.alloc([128, 16, 1024], BF16)
        r_Wkb, r_Wvb = P.Rs(2), P.Rs(2)
        xs0 = A.alloc([128, 4, D], BF16)
        r_xs0 = P.R()
        first_g = 16 if (dbg is not None and stage < 4) else 0
        if stage >= 1 and first_g == 0:
            dma("pool", xs0, x_all[0:512, :].rearrange("(b p) d -> p b d", p=128), [], [r_xs0])
        for hlf in range(2):
            dma("pool", Wkb[:, :, hlf * 512:(hlf + 1) * 512],
                w_in[:, C_KB + hlf * 512:C_KB + (hlf + 1) * 512].rearrange("(c p) n -> p c n", p=128), [], [r_Wkb[hlf]])
        for hlf in range(2):
            dma("pool", Wvb[:, :, hlf * 512:(hlf + 1) * 512],
                w_in[:, C_VB + hlf * 512:C_VB + (hlf + 1) * 512].rearrange("(c p) n -> p c n", p=128), [], [r_Wvb[hlf]])
        xs = [xs0, A.alloc([128, 4, D], BF16)]
        xTs = [A.alloc([128, 16, 512], BF16) for _ in range(2)]
        KTs = [A.alloc([128, 8, 512], BF16) for _ in range(2)]
        Vs = [A.alloc([128, 4, 1024], BF16) for _ in range(2)]
        r_xs = [r_xs0, P.R()]
        r_xTs = [[[P.R() for _ in range(2)] for _ in range(4)] for _ in range(2)]
        r_KTs = [P.Rs(8) for _ in range(2)]
        r_Vs = [[[P.R() for _ in range(2)] for _ in range(4)] for _ in range(2)]
        r_ktall, r_vall = P.R(), P.R()
        tb = [0]
        mb = [0]

        def next_tbank():
            b = 6 + tb[0] % 2
            tb[0] += 1
            return b

        def next_mbank(n=6):
            b = mb[0] % n
            mb[0] += 1
            return b

        def transpose_block_bf16(src_tok_major, dst_fn, rsrc, rdst_fn):
            for hlf in range(2):
                b = next_tbank()
                pb = bank(b).bitcast(BF16)
                for c in range(8):
                    cc = hlf * 8 + c
                    P.op("pe", lambda e, pb=pb, c=c, cc=cc: e.transpose(pb[:, c * 128:(c + 1) * 128],
                                                                       src_tok_major[:, cc * 128:(cc + 1) * 128], idb),
                         reads=[rsrc, r_idb], writes=[rbank[b]])
                cp(dst_fn(hlf), pb.rearrange("p (c n) -> p c n", c=8), [rbank[b]], [rdst_fn(hlf)])

        NG = 19
        for g in range(16 if (dbg is not None and stage < 4) else 0, NG if stage >= 1 else 0):
            if g == 15:
                continue
            par = g % 2
            nb = 4 if g < 18 else 2
            ntok = nb * 128
            if g < 16:
                src = x_all[g * 512:(g + 1) * 512, :]
            else:
                src = x_own[(g - 16) * 512:(g - 16) * 512 + ntok, :]
            if g != 0:
                dma("pool", xs[par][:, 0:nb, :], src.rearrange("(b p) d -> p b d", p=128), [], [r_xs[par]])
            for b in range(nb):
                if g < 16:
                    dst_fn = lambda hlf, b=b: xTs[par][:, hlf * 8:(hlf + 1) * 8, b * 128:(b + 1) * 128]
                    rdst_fn = lambda hlf, b=b: r_xTs[par][b][hlf]
                else:
                    lb = (g - 16) * 4 + b
                    dst_fn = lambda hlf, lb=lb: xT[:, hlf * 8:(hlf + 1) * 8, lb * 128:(lb + 1) * 128]
                    rdst_fn = lambda hlf, lb=lb: r_xT[lb][hlf]
                transpose_block_bf16(xs[par][:, b, :], dst_fn, r_xs[par], rdst_fn)
            if g < 16:
                xsrc = xTs[par]
                t0 = 0
                rx = lambda b, hlf: r_xTs[par][b][hlf]
            else:
                xsrc = xT
                t0 = (g - 16) * 512
                rx = lambda b, hlf, g=g: r_xT[(g - 16) * 4 + b][hlf]
            for h in range(8):
                bk = next_mbank()
                for c in range(16):
                    mm(bank(bk)[:, 0:ntok], Wkb[:, c, h * 128:(h + 1) * 128], xsrc[:, c, t0:t0 + ntok], c == 0, c == 15,
                       [r_Wkb[h // 4]] + [rx(b, c // 8) for b in range(nb)], [rbank[bk]])
                cp(KTs[par][:, h, 0:ntok], bank(bk)[:, 0:ntok], [rbank[bk]], [r_KTs[par][h]])
            dma("sp", kt_all[:, :, g * 512:g * 512 + ntok].rearrange("h p n -> p h n"), KTs[par][:, :, 0:ntok],
                r_KTs[par], [r_ktall])
            for b in range(nb):
                for hlf in range(2):
                    bk = next_mbank()
                    for c in range(16):
                        mm(bank(bk), xsrc[:, c, t0 + b * 128:t0 + (b + 1) * 128], Wvb[:, c, hlf * 512:(hlf + 1) * 512], c == 0, c == 15,
                           [r_Wvb[hlf], rx(b, c // 8)], [rbank[bk]])
                    cp(Vs[par][:, b, hlf * 512:(hlf + 1) * 512], bank(bk), [rbank[bk]], [r_Vs[par][b][hlf]])
            dma("sp", v_all[g * 4:g * 4 + nb, :, :].rearrange("b p n -> p b n"), Vs[par][:, 0:nb, :],
                [r for bb in range(nb) for r in r_Vs[par][bb]], [r_vall])
        P.barrier(scratch)
        PF0 = 173568
        A.reset(PF0)
        KTh0 = A.alloc([128, NTOKX], BF16)
        Vh0 = A.alloc([128, 74, 129], BF16)
        r_KTh0, r_Vh0, r_Vhones0 = P.R(), P.R(), P.R()
        if stage >= 4:
            P.op("pool", lambda e: e.memset(Vh0[:, :, 128:129], 1.0), writes=[r_Vhones0])
            dma("sp", KTh0[:, 0:7680], kt_all[0][:, 0:7680], [r_ktall], [r_KTh0])
            dma("sp", KTh0[:, S:NTOKX], kt_all[0][:, S:NTOKX], [r_ktall], [r_KTh0])
            dma("sp", Vh0[:, 0:60, 0:128], v_all[0:60, :, 0:128].rearrange("b p n -> p b n"), [r_vall], [r_Vh0])
            dma("sp", Vh0[:, 64:74, 0:128], v_all[64:74, :, 0:128].rearrange("b p n -> p b n"), [r_vall], [r_Vh0])
        A.reset(xT_mark)

        QaT = A.alloc([128, 8, TOK], BF16)
        KaT = A.alloc([128, 2, NOWN], BF16)
        Va = A.alloc([128, 10, 2, 65], BF16)
        r_QaT = [P.Rs(2) for _ in range(8)]
        r_QbT = [P.Rs(2) for _ in range(8)]
        r_KaT = [P.Rs(3) for _ in range(2)]
        r_Va = P.Rs(10)
        r_Vaones = P.R()
        r_oaT = P.Rs(8)
        r_obT = [P.Rs(8) for _ in range(8)]
        proj_mark = A.mark()
        allx = [r_xT[b][hh] for b in range(10) for hh in range(2)]

        if stage >= 2:
            Wt = [A.alloc([128, 16, 512], BF16) for _ in range(2)]
            r_Wt = P.Rs(2)
            wi = [0]

            def load_w(col0, ncols=512):
                k = wi[0] % 2
                wi[0] += 1
                dma("pool", Wt[k][:, :, 0:ncols], w_in[:, col0:col0 + ncols].rearrange("(c p) n -> p c n", p=128), [], [r_Wt[k]])
                return Wt[k], r_Wt[k]

            for (col0, dst, rdst) in ((C_QA, QaT, r_QaT), (C_QB, QbT, r_QbT)):
                for pn in range(2):
                    W, rW = load_w(col0 + pn * 512)
                    for ii in range(4):
                        i = pn * 4 + ii
                        for hlf in range(2):
                            bk = next_mbank()
                            for c in range(16):
                                mm(bank(bk), W[:, c, ii * 128:(ii + 1) * 128], xT[:, c, OWN0[hlf]:OWN0[hlf] + 512],
                                   c == 0, c == 15, [rW] + allx, [rbank[bk]])
                            cp(dst[:, i, hlf * 512:(hlf + 1) * 512], bank(bk), [rbank[bk]], [rdst[i][hlf]])
            k = wi[0] % 2
            wi[0] += 1
            Wka = Wt[k][:, :, 0:256].rearrange("p c (g u d) -> p c g u d", g=2, u=2)
            for u in range(2):
                for g in range(2):
                    dma("pool", Wka[:, :, g, u, :], w_in[:, C_KA + g * 64:C_KA + (g + 1) * 64].rearrange("(c p) d -> p c d", p=128), [], [r_Wt[k]])
            Wva = Wt[k][:, :, 256:384]
            dma("pool", Wva, w_in[:, C_VA:C_VA + 128].rearrange("(c p) n -> p c n", p=128), [], [r_Wt[k]])
            for g in range(2):
                for pi, (t0, nt) in enumerate(((0, 512), (512, 512), (1024, 256))):
                    bk = next_mbank()
                    for c in range(16):
                        mm(bank(bk)[:, 0:nt], Wt[k][:, c, g * 128:(g + 1) * 128], xT[:, c, t0:t0 + nt], c == 0, c == 15,
                           [r_Wt[k]] + allx, [rbank[bk]])
                    cp(KaT[:, g, t0:t0 + nt], bank(bk)[:, 0:nt], [rbank[bk]], [r_KaT[g][pi]])
            P.op("pool", lambda e: e.memset(Va[:, :, :, 64:65], 1.0), writes=[r_Vaones])
            for b in range(10):
                bk = next_mbank()
                for c in range(16):
                    mm(bank(bk)[:, 0:128], xT[:, c, b * 128:(b + 1) * 128], Wva[:, c, :], c == 0, c == 15,
                       [r_Wt[k], r_xT[b][c // 8]], [rbank[bk]])
                cp(Va[:, b, :, 0:64], bank(bk)[:, 0:128].rearrange("p (g d) -> p g d", g=2), [rbank[bk]], [r_Va[b]])
            memb = A.alloc([128, 2, D], BF16)
            memT = A.alloc([128, 16, 256], BF16)
            kcs = A.alloc([128, 4, 256], BF16)
            vcs = A.alloc([128, 2, 512], BF16)
            r_memb = P.R()
            r_memT = [P.Rs(2) for _ in range(2)]
            r_kcs, r_vcs = P.Rs(4), P.Rs(2)
            r_kcd, r_vcd = P.R(), P.R()
            dma("pool", memb, mem.rearrange("(b p) d -> p b d", p=128), [], [r_memb])
            for mbk in range(2):
                transpose_block_bf16(memb[:, mbk, :], lambda hlf, mbk=mbk: memT[:, hlf * 8:(hlf + 1) * 8, mbk * 128:(mbk + 1) * 128],
                                     r_memb, lambda hlf, mbk=mbk: r_memT[mbk][hlf])
            all_memT = [r for a in r_memT for r in a]
            k = wi[0] % 2
            wi[0] += 1
            dma("pool", Wt[k], w_mkv[:, 0:512].rearrange("(c p) n -> p c n", p=128), [], [r_Wt[k]])
            for h in range(4):
                bk = next_mbank()
                for c in range(16):
                    mm(bank(bk)[:, 0:256], Wt[k][:, c, h * 128:(h + 1) * 128], memT[:, c, :], c == 0, c == 15,
                       [r_Wt[k]] + all_memT, [rbank[bk]])
                cp(kcs[:, h, :], bank(bk)[:, 0:256], [rbank[bk]], [r_kcs[h]])
            dma("sp", kc_s, kcs.rearrange("p a b -> p (a b)"), r_kcs, [r_kcd])
            k = wi[0] % 2
            wi[0] += 1
            dma("pool", Wt[k], w_mkv[:, 512:1024].rearrange("(c p) n -> p c n", p=128), [], [r_Wt[k]])
            for mbk in range(2):
                bk = next_mbank()
                for c in range(16):
                    mm(bank(bk), memT[:, c, mbk * 128:(mbk + 1) * 128], Wt[k][:, c, :], c == 0, c == 15,
                       [r_Wt[k], r_memT[mbk][c // 8]], [rbank[bk]])
                cp(vcs[:, mbk, :], bank(bk), [rbank[bk]], [r_vcs[mbk]])
            dma("sp", vc_s, vcs.rearrange("p a b -> p (a b)"), r_vcs, [r_vcd])
            assert A.mark() <= PF0, A.mark()
        P.barrier(scratch)
        A.reset(proj_mark)

        def pipeline(nitems, stages):
            ns = len(stages)
            for tau in range(nitems + ns - 1):
                for si in range(ns - 1, -1, -1):
                    i = tau - si
                    if 0 <= i < nitems:
                        stages[si](i)

        if stage >= 3:
            Bsw = [A.alloc([128, 4, 4, 256], F32) for _ in range(2)]
            r_Bsw = P.Rs(2)
            tmpS = [A.alloc([128, 4, 128], F32) for _ in range(4)]
            r_tmpS = P.Rs(4)
            PTs = [A.alloc([128, 4, 128], BF16) for _ in range(8)]
            r_PTs = P.Rs(8)
            oa = [A.alloc([128, 1024], BF16) for _ in range(2)]
            r_oa = [P.Rs(16) for _ in range(2)]
            den = [A.alloc([128, 8], F32) for _ in range(4)]
            r_den = P.Rs(4)
            r_rden = P.Rs(4)
            allKa = [r for g in range(2) for r in r_KaT[g]]
            assert A.mark() <= PF0, A.mark()
            def unpack(i):
                j, gr = divmod(i, 4)
                g, r = divmod(gr, 2)
                return j, g, r

            def sQ(i):
                j, g, r = unpack(i)
                bp = j % 2
                if i % 4 == 0:
                    dma("sp", Bsw[bp].rearrange("p a b c -> p (a b c)"), bias_swa[1 if j == 0 else 0], [], [r_Bsw[bp]])
                for kap in range(2):
                    blk = LB(j) - 1 + kap
                    bk = (i % 2) * 2 + kap
                    mm(bank(bk).rearrange("p (h q) -> p h q", h=4), KaT[r * 64:(r + 1) * 64, g, blk * 128:(blk + 1) * 128],
                       QaT[r * 64:(r + 1) * 64, 4 * g:4 * g + 4, j * 128:(j + 1) * 128], True, True,
                       allKa + [r_QaT[ii][j // 4] for ii in range(4 * g, 4 * g + 4)], [rbank[bk]])

            def sA(i):
                j, g, r = unpack(i)
                bp = j % 2
                for kap in range(2):
                    bk = (i % 2) * 2 + kap
                    tk = (i % 2) * 2 + kap
                    pk = (i % 4) * 2 + kap
                    P.op("dve", lambda e, tk=tk, bk=bk, kap=kap, g=g, r=r, bp=bp: e.scalar_tensor_tensor(
                        tmpS[tk], bank(bk).rearrange("p (h q) -> p h q", h=4), 0.125,
                        Bsw[bp][:, g * 2 + r, :, kap * 128:(kap + 1) * 128], ALU.mult, ALU.add),
                        reads=[rbank[bk], r_Bsw[bp]], writes=[r_tmpS[tk]])
                    P.op("act", lambda e, tk=tk, pk=pk: e.activation(PTs[pk], tmpS[tk], AF.Exp),
                         reads=[r_tmpS[tk]], writes=[r_PTs[pk]])

            def sB(i):
                j, g, r = unpack(i)
                ab = 4 + i % 2
                acc = bank(ab).rearrange("p (h c) -> p h c", h=4)
                for hh in range(4):
                    for kap in range(2):
                        blk = LB(j) - 1 + kap
                        pk = (i % 4) * 2 + kap
                        mm(acc[:, hh, 0:65], PTs[pk][:, hh, :], Va[:, blk, g, :], kap == 0, kap == 1,
                           [r_PTs[pk], r_Va[blk], r_Vaones], [rbank[ab]])

            def sC(i):
                j, g, r = unpack(i)
                ab = 4 + i % 2
                acc = bank(ab).rearrange("p (h c) -> p h c", h=4)
                dp = i % 4
                hsl = slice((g * 2 + r) * 4, (g * 2 + r) * 4 + 4)
                P.op("dve", lambda e, dp=dp, acc=acc, hsl=hsl: e.tensor_tensor(den[dp][:, 0:4], acc[:, :, 64], esink[:, hsl], ALU.add),
                     reads=[rbank[ab], r_const], writes=[r_den[dp]])
                P.op("dve", lambda e, dp=dp: e.reciprocal(den[dp][:, 4:8], den[dp][:, 0:4]),
                     reads=[r_den[dp]], writes=[r_rden[dp]])

            def sD(i):
                j, g, r = unpack(i)
                bp = j % 2
                ab = 4 + i % 2
                acc = bank(ab).rearrange("p (h c) -> p h c", h=4)
                dp = i % 4
                heads = [2 * ii + r for ii in range(4 * g, 4 * g + 4)]
                for hh in range(4):
                    h = heads[hh]
                    P.op("dve", lambda e, dp=dp, acc=acc, hh=hh, h=h, bp=bp: e.tensor_scalar_mul(
                        oa[bp][:, h * 64:(h + 1) * 64], acc[:, hh, 0:64], den[dp][:, 4 + hh:5 + hh]),
                        reads=[rbank[ab], r_rden[dp]], writes=[r_oa[bp][(g * 2 + r) * 4 + hh]])
                if i % 4 == 3:
                    b = next_tbank()
                    pb = bank(b).bitcast(BF16)
                    for c in range(8):
                        P.op("pe", lambda e, pb=pb, c=c, bp=bp: e.transpose(pb[:, c * 128:(c + 1) * 128], oa[bp][:, c * 128:(c + 1) * 128], idb),
                             reads=r_oa[bp] + [r_idb], writes=[rbank[b]])
                    cp(oaT[:, :, j * 128:(j + 1) * 128], pb.rearrange("p (c n) -> p c n", c=8), [rbank[b]], [r_oaT[j]])

            pipeline(NB * 4, [sQ, sA, lambda i: None, sB, sC, sD])
        P.barrier(scratch)
        A.reset(xT_mark)

        if stage >= 4:
            KTh = [KTh0, A.alloc([128, NTOKX], BF16)]
            Vh = [Vh0, A.alloc([128, 74, 129], BF16)]
            r_KTh, r_Vh, r_Vhones = [r_KTh0, P.R()], [r_Vh0, P.R()], [r_Vhones0, P.R()]
            Bn = A.alloc([128, 5, 512], F32)
            r_Bn = P.R()
            Bf = A.alloc([128, 2, 64, 8], F32)
            r_Bf = P.R()
            tmpD = [A.alloc([128, 2, 512], F32) for _ in range(2)]
            r_tmpD = P.Rs(2)
            PT = [A.alloc([128, 2, 512], BF16) for _ in range(2)]
            r_PT = P.Rs(2)
            accS = A.alloc([128, 3, 512], F32)
            r_accS = P.Rs(3)
            o0 = [A.alloc([128, 128], F32) for _ in range(4)]
            o1 = [A.alloc([128, 128], F32) for _ in range(4)]
            obk = A.alloc([128, 4, 128], BF16)
            st4 = [A.alloc([128, 8], F32) for _ in range(4)]
            r_r0, r_r1, r_r1l, r_o0, r_o1, r_ss, r_ln, r_rstd, r_obk = [P.Rs(4) for _ in range(9)]
            dma("sp", Bf.rearrange("p a b c -> p (a b c)"), bias_far, [], [r_Bf])
            zt = A.alloc([128, 512], BF16)
            r_zt = P.R()
            P.op("pool", lambda e: e.memset(zt, 0.0), writes=[r_zt])
            P.op("pool", lambda e: e.memset(Vh[1][:, :, 128:129], 1.0), writes=[r_Vhones[1]])
            assert A.mark() <= PF0, A.mark()
            def accap(a):
                return bank(4 + a // 3)[:, (a % 3) * 160:(a % 3) * 160 + 129]

            def accS_ap(a):
                return accS[:, a // 3, (a % 3) * 160:(a % 3) * 160 + 129]

            def make_stages(h, ch):
                def g0():
                    for qb in range(4):
                        s4, a0, a1 = st4[qb], accS_ap(qb), accS_ap(4 + qb)
                        P.op("dve", lambda e, s4=s4, a0=a0: e.reciprocal(s4[:, 0:1], a0[:, 128:129]),
                             reads=[r_accS[qb // 3]], writes=[r_r0[qb]])
                        P.op("dve", lambda e, s4=s4, a1=a1: e.reciprocal(s4[:, 1:2], a1[:, 128:129]),
                             reads=[r_accS[(4 + qb) // 3]], writes=[r_r1[qb]])

                def g1():
                    for qb in range(4):
                        s4, a0 = st4[qb], accS_ap(qb)
                        P.op("dve", lambda e, s4=s4: e.tensor_tensor(s4[:, 2:3], s4[:, 1:2], neglam, ALU.mult),
                             reads=[r_r1[qb], r_const], writes=[r_r1l[qb]])
                        P.op("dve", lambda e, s4=s4, a0=a0, qb=qb: e.tensor_scalar_mul(o0[qb], a0[:, 0:128], s4[:, 0:1]),
                             reads=[r_accS[qb // 3], r_r0[qb]], writes=[r_o0[qb]])

                def g2():
                    for qb in range(4):
                        s4, a1 = st4[qb], accS_ap(4 + qb)
                        P.op("dve", lambda e, s4=s4, a1=a1, qb=qb: e.scalar_tensor_tensor(o1[qb], a1[:, 0:128], s4[:, 2:3], o0[qb], ALU.mult, ALU.add),
                             reads=[r_accS[(4 + qb) // 3], r_r1l[qb], r_o0[qb]], writes=[r_o1[qb]])
                        P.op("dve", lambda e, s4=s4: e.memset(s4[:, 3:4], 0.0), writes=[r_ss[qb]])

                def g3():
                    for qb in range(4):
                        s4 = st4[qb]
                        P.op("act", lambda e, s4=s4, qb=qb: e.activation(o0[qb], o1[qb], AF.Square, accum_out=s4[:, 3:4]),
                             reads=[r_o1[qb]], writes=[r_ss[qb], r_o0[qb]])

                def g4():
                    for qb in range(4):
                        s4 = st4[qb]
                        P.op("act", lambda e, s4=s4: e.activation(s4[:, 4:5], s4[:, 3:4], AF.Ln, bias=epsc, scale=1.0 / 128.0),
                             reads=[r_ss[qb], r_const], writes=[r_ln[qb]])

                def g5():
                    for qb in range(4):
                        s4 = st4[qb]
                        P.op("act", lambda e, s4=s4: e.activation(s4[:, 5:6], s4[:, 4:5], AF.Exp, scale=-0.5),
                             reads=[r_ln[qb]], writes=[r_rstd[qb]])

                def g6():
                    for qb in range(4):
                        s4 = st4[qb]
                        P.op("dve", lambda e, s4=s4, qb=qb: e.scalar_tensor_tensor(obk[:, qb, :], o1[qb], s4[:, 5:6], wsub, ALU.mult, ALU.mult),
                             reads=[r_o1[qb], r_rstd[qb], r_const], writes=[r_obk[qb]])

                def g7():
                    pb = bank(7).bitcast(BF16)
                    for qb in range(4):
                        P.op("pe", lambda e, pb=pb, qb=qb: e.transpose(pb[:, qb * 128:(qb + 1) * 128], obk[:, qb, :], idb),
                             reads=[r_obk[qb], r_idb], writes=[rbank[7]])
                    cp(obT[:, h, ch * 512:(ch + 1) * 512], pb[:, 0:512], [rbank[7]], [r_obT[h][ch * 4 + qb] for qb in range(4)], eng="dve")

                return [g0, g1, g2, g3, g4, g5, g6, g7]

            sc = [pp[0][:], pp[1][:]]
            r_sc = P.Rs(2)
            r_acc = P.Rs(3)
            pending = []
            for h in range(8):
                hp = h % 2
                if h > 0:
                    dma("sp", KTh[hp][:, 0:7680], kt_all[h][:, 0:7680], [r_ktall], [r_KTh[hp]])
                    dma("sp", KTh[hp][:, S:NTOKX], kt_all[h][:, S:NTOKX], [r_ktall], [r_KTh[hp]])
                    dma("sp", Vh[hp][:, 0:60, 0:128], v_all[0:60, :, h * 128:(h + 1) * 128].rearrange("b p n -> p b n"), [r_vall], [r_Vh[hp]])
                    dma("sp", Vh[hp][:, 64:74, 0:128], v_all[64:74, :, h * 128:(h + 1) * 128].rearrange("b p n -> p b n"), [r_vall], [r_Vh[hp]])
                for ch in range(2):
                    dma("sp", Bn.rearrange("p a b -> p (a b)"), bias_near[ch, h], [], [r_Bn])
                    tiles = [("far", kb) for kb in range(NFAR[ch])] + [("near", kap) for kap in range(5)]
                    nt = len(tiles)
                    qsl = slice(ch * 512, (ch + 1) * 512)

                    def emit_qk(t):
                        kind, idx = tiles[t]
                        sp_ = t % 2
                        tok0 = idx * 128 if kind == "far" else S + (ch * 5 + idx) * 128
                        for m in range(2):
                            mm(sc[sp_][:, m * 512:(m + 1) * 512], KTh[hp][m * 64:(m + 1) * 64, tok0:tok0 + 128],
                               QbT[m * 64:(m + 1) * 64, h, qsl], True, True,
                               [r_KTh[hp], r_QbT[h][ch]], [r_sc[sp_]])

                    emit_qk(0)
                    emit_qk(1)
                    for ab in range(3):
                        mm(bank(4 + ab), zt[:, 0:128], zt, True, False, [r_zt], [r_acc[ab]])
                    for t in range(nt):
                        kind, idx = tiles[t]
                        sp_ = t % 2
                        if kind == "far":
                            bias_ap = Bf[:, ch, idx, h:h + 1]
                            P.op("act", lambda e, sp_=sp_, bias_ap=bias_ap: e.activation(
                                PT[sp_].rearrange("p a b -> p (a b)"), sc[sp_], AF.Exp, bias=bias_ap, scale=0.125),
                                reads=[r_sc[sp_], r_Bf], writes=[r_PT[sp_]])
                        else:
                            for m in range(2):
                                bn_ap = Bn[:, idx, :]
                                P.op("dve", lambda e, sp_=sp_, bn_ap=bn_ap, m=m: e.scalar_tensor_tensor(
                                    tmpD[sp_][:, m, :], sc[sp_][:, m * 512:(m + 1) * 512], 0.125, bn_ap, ALU.mult, ALU.add),
                                    reads=[r_sc[sp_], r_Bn], writes=[r_tmpD[sp_]])
                            P.op("act", lambda e, sp_=sp_: e.activation(PT[sp_], tmpD[sp_], AF.Exp),
                                 reads=[r_tmpD[sp_]], writes=[r_PT[sp_]])
                        if t + 2 < nt:
                            emit_qk(t + 2)
                        vb = idx if kind == "far" else 64 + ch * 5 + idx
                        for m in range(2):
                            for qb in range(4):
                                a = m * 4 + qb
                                mm(accap(a), PT[sp_][:, m, qb * 128:(qb + 1) * 128], Vh[hp][:, vb, :], False, (t == nt - 1) and a in (2, 5, 7),
                                   [r_PT[sp_], r_Vh[hp], r_Vhones[hp]], [r_acc[a // 3]])
                        if pending and t >= 2:
                            pending.pop(0)()
                    while pending:
                        pending.pop(0)()
                    for ab in range(3):
                        cp(accS[:, ab, :], bank(4 + ab), [r_acc[ab]], [r_accS[ab]], eng="dve")
                    pending = make_stages(h, ch)
            while pending:
                pending.pop(0)()
        P.barrier(scratch)
        A.reset(xT_mark)

        y = None

        def layer_norm_and_T(y, r_y, lnw, r_lnw, hT, r_hT, stt, final_out=None, r_out=None):
            r_st = [P.Rs(4) for _ in range(NB)]
            r_mv, r_rs, r_nm = P.Rs(NB), P.Rs(NB), P.Rs(NB)

            def s0(b):
                yb, s = y[:, b, :], stt[:, b, :]
                stats = s[:, 0:24].rearrange("p (k s) -> p k s", k=4)
                for k in range(4):
                    P.op("dve", lambda e, k=k, yb=yb, stats=stats: e.bn_stats(stats[:, k, :], yb[:, k * 512:(k + 1) * 512]),
                         reads=[r_y[b][k]], writes=[r_st[b][k]])
                P.op("dve", lambda e, s=s, stats=stats: e.bn_aggr(s[:, 24:26], stats), reads=r_st[b], writes=[r_mv[b]])

            def s1(b):
                s = stt[:, b, :]
                P.op("act", lambda e, s=s: e.activation(s[:, 27:28], s[:, 25:26], AF.Ln, bias=epsc), reads=[r_mv[b], r_const], writes=[r_rs[b]])
                P.op("act", lambda e, s=s: e.activation(s[:, 26:27], s[:, 27:28], AF.Exp, scale=-0.5), reads=[r_rs[b]], writes=[r_rs[b]])

            def s2(b):
                s = stt[:, b, :]
                P.op("dve", lambda e, s=s: e.scalar_tensor_tensor(s[:, 28:29], s[:, 24:25], -1.0, s[:, 26:27], ALU.mult, ALU.mult),
                     reads=[r_mv[b], r_rs[b]], writes=[r_nm[b]])

            def s3(b):
                yb, s = y[:, b, :], stt[:, b, :]
                P.op("act", lambda e, s=s, yb=yb: e.activation(yb, yb, AF.Identity, bias=s[:, 28:29], scale=s[:, 26:27]),
                     reads=[r_nm[b], r_rs[b]] + r_y[b], writes=r_y[b])

            def s4(b):
                yb = y[:, b, :]
                P.op("dve", lambda e, yb=yb: e.tensor_tensor(yb, yb, lnw[:, 0, :], ALU.mult), reads=[r_lnw] + r_y[b], writes=r_y[b])

            def s5(b):
                yb = y[:, b, :]
                P.op("pool", lambda e, yb=yb: e.tensor_tensor(yb, yb, lnw[:, 1, :], ALU.add), reads=[r_lnw] + r_y[b], writes=r_y[b])

            def s6(b):
                yb = y[:, b, :]
                if final_out is not None:
                    dma("sp", final_out[b * 128:(b + 1) * 128, :], yb, r_y[b], [r_out])
                    return
                for q4 in range(4):
                    bk = next_tbank()
                    for c in range(4):
                        cc = q4 * 4 + c
                        P.op("pe", lambda e, bk=bk, c=c, cc=cc, yb=yb: e.transpose(bank(bk)[:, c * 128:(c + 1) * 128], yb[:, cc * 128:(cc + 1) * 128], idf),
                             reads=r_y[b] + [r_idf], writes=[rbank[bk]])
                    cp(hT[:, q4 * 4:(q4 + 1) * 4, b * 128:(b + 1) * 128], bank(bk).rearrange("p (c n) -> p c n", c=4),
                       [rbank[bk]], [r_hT[b][q4]])

            pipeline(NB, [s0, s1, s2, s3, s4, s5, s6])

        if stage >= 5:
            A.reset(117 * 1024)
            mixT = A.alloc([128, 16, TOK], BF16)
            r_mixT = [P.Rs(2) for _ in range(16)]
            e_mark = A.mark()
            Wg = [A.alloc([128, 16, 512], BF16) for _ in range(2)]
            Wab = [A.alloc([128, 8, 512], BF16) for _ in range(2)]
            r_Wg, r_Wab = P.Rs(2), P.Rs(2)
            sg = [A.alloc([128, 512], F32) for _ in range(2)]
            r_sg = P.Rs(2)
            m1 = [A.alloc([128, 512], F32) for _ in range(2)]
            r_m1 = P.Rs(2)
            all_oaT = r_oaT
            all_obT = [r for hh in range(8) for r in r_obT[hh]]
            gi = 0
            for jj in range(8):
                k = jj % 2
                dma("pool", Wg[k][:, :, 0:256], w_in[:, C_GA + jj * 256:C_GA + (jj + 1) * 256].rearrange("(c p) n -> p c n", p=128), [], [r_Wg[k]])
                dma("pool", Wg[k][:, :, 256:512], w_in[:, C_GB + jj * 256:C_GB + (jj + 1) * 256].rearrange("(c p) n -> p c n", p=128), [], [r_Wg[k]])
                dma("pool", Wab[k][:, :, 0:256], w_a[:, jj * 256:(jj + 1) * 256].rearrange("(c p) n -> p c n", p=128), [], [r_Wab[k]])
                dma("pool", Wab[k][:, :, 256:512], w_b[:, jj * 256:(jj + 1) * 256].rearrange("(c p) n -> p c n", p=128), [], [r_Wab[k]])
                for sub in range(2):
                    j = jj * 2 + sub
                    for hlf in range(2):
                        tsl = slice(hlf * 512, (hlf + 1) * 512)
                        for br in range(2):
                            gk = gi % 2
                            gi += 1
                            bg = next_mbank()
                            for c in range(16):
                                mm(bank(bg), Wg[k][:, c, br * 256 + sub * 128:br * 256 + (sub + 1) * 128],
                                   xT[:, c, OWN0[hlf]:OWN0[hlf] + 512], c == 0, c == 15, [r_Wg[k]] + allx, [rbank[bg]])
                            P.op("act", lambda e, gk=gk, bg=bg: e.activation(sg[gk], bank(bg), AF.Sigmoid),
                                 reads=[rbank[bg]], writes=[r_sg[gk]])
                            bb = next_mbank()
                            srcT = oaT if br == 0 else obT
                            rsrc = all_oaT if br == 0 else all_obT
                            for c in range(8):
                                mm(bank(bb), Wab[k][:, c, br * 256 + sub * 128:br * 256 + (sub + 1) * 128], srcT[:, c, tsl],
                                   c == 0, c == 7, [r_Wab[k]] + rsrc, [rbank[bb]])
                            if br == 0:
                                mk = (gi // 2) % 2
                                P.op("dve", lambda e, mk=mk, bb=bb, gk=gk: e.tensor_tensor(m1[mk], bank(bb), sg[gk], ALU.mult),
                                     reads=[rbank[bb], r_sg[gk]], writes=[r_m1[mk]])
                                mk_a = mk
                            else:
                                P.op("dve", lambda e, bb=bb, gk=gk: e.tensor_tensor(sg[gk], bank(bb), sg[gk], ALU.mult),
                                     reads=[rbank[bb], r_sg[gk]], writes=[r_sg[gk]])
                                P.op("dve", lambda e, gk=gk, mk_a=mk_a, j=j, tsl=tsl: e.tensor_tensor(mixT[:, j, tsl], m1[mk_a], sg[gk], ALU.add),
                                     reads=[r_m1[mk_a], r_sg[gk]], writes=[r_mixT[j][hlf]])
            P.barrier(scratch)
            A.reset(M0)
            y = A.alloc([128, NB, D], F32)
            r_y = [P.Rs(4) for _ in range(NB)]
            lnw = A.alloc([128, 2, D], F32)
            r_lnw = P.R()
            stt = A.alloc([128, NB, 32], F32)
            hT = A.alloc([128, 16, TOK], BF16)
            r_hT = [P.Rs(4) for _ in range(NB)]
            h_mark = A.mark()
            assert h_mark <= 117 * 1024
            A.reset(e_mark)
            Wo = [A.alloc([128, 16, 512], BF16) for _ in range(2)]
            r_Wo = P.Rs(2)
            dma("pool", Wo[0], w_o[:, 0:512].rearrange("(c p) n -> p c n", p=128), [], [r_Wo[0]])
            for b in range(NB):
                dma("sp", y[:, b, :], x_own[LB(b) * 128:(LB(b) + 1) * 128, :], [r_Wo[0]], r_y[b])
            dma("sp", lnw, lnp[:, 0:2, :], [], [r_lnw])
            all_mix = [r for j in range(16) for r in r_mixT[j]]
            for pn in range(4):
                k = pn % 2
                if pn > 0:
                    dma("pool", Wo[k], w_o[:, pn * 512:(pn + 1) * 512].rearrange("(c p) n -> p c n", p=128), [], [r_Wo[k]])
                for b in range(NB):
                    bk = next_mbank()
                    for c in range(16):
                        mm(bank(bk), mixT[:, c, b * 128:(b + 1) * 128], Wo[k][:, c, :], c == 0, c == 15,
                           [r_Wo[k], r_mixT[c][b // 4]], [rbank[bk]])
                    ysl = y[:, b, pn * 512:(pn + 1) * 512]
                    P.op("dve", lambda e, ysl=ysl, bk=bk: e.scalar_tensor_tensor(ysl, ysl, ALPHA, bank(bk), ALU.mult, ALU.add),
                         reads=[rbank[bk], r_y[b][pn]], writes=[r_y[b][pn]])
            A.reset(182 * 1024)
            Wq = A.alloc([128, 16, 512], BF16)
            r_Wq = P.R()
            dma("pool", Wq, w_cq.rearrange("(c p) n -> p c n", p=128), [], [r_Wq])
            layer_norm_and_T(y, r_y, lnw, r_lnw, hT, r_hT, stt)
            all_hT = [r for b in range(NB) for r in r_hT[b]]

        if stage >= 6:
            P.barrier(scratch)
            A.reset(h_mark)
            KcT = A.alloc([128, 4, 256], BF16)
            Vc = A.alloc([128, 2, 4, 129], BF16)
            r_KcT = [P.R()] * 4
            r_Vc = P.Rs(2)
            r_Vcones = P.R()
            qcT = A.alloc([128, 4, TOK], BF16)
            r_qcT = [P.Rs(2) for _ in range(4)]
            oc = A.alloc([128, NB, 512], BF16)
            r_oc = [P.Rs(4) for _ in range(NB)]
            ocT = A.alloc([128, 4, TOK], BF16)
            r_ocT = P.Rs(NB)
            Wco = A.alloc([128, 4, D], BF16)
            r_Wco = P.R()
            PTc = [A.alloc([128, 512], BF16) for _ in range(4)]
            r_PTc = P.Rs(4)
            rc = [A.alloc([128, 4], F32) for _ in range(2)]
            r_rc = P.Rs(2)
            assert A.mark() <= 169216, A.mark()
            A.reset(169216)
            Wg0 = A.alloc([128, 16, 256], BF16)
            Wu0 = A.alloc([128, 16, 256], BF16)
            assert A.mark() <= 182 * 1024
            A.reset(198 * 1024)
            Wd0 = A.alloc([128, 2, D], BF16)
            r_Wgu0, r_Wd0 = P.R(), P.R()
            dma("sp", KcT.rearrange("p a b -> p (a b)"), kc_s, [r_kcd], [r_KcT[0]])
            P.op("pool", lambda e: e.memset(Vc[:, :, :, 128:129], 1.0), writes=[r_Vcones])
            for mbk in range(2):
                dma("sp", Vc[:, mbk, :, 0:128], vc_s[:, mbk * 512:(mbk + 1) * 512].rearrange("p (h d) -> p h d", h=4), [r_vcd], [r_Vc[mbk]])
            dma("sp", lnw, lnp[:, 2:4, :], [], [r_lnw])
            dma("pool", Wco, w_co.rearrange("(c p) n -> p c n", p=128), [], [r_Wco])
            if stage >= 7:
                dma("pool", Wg0, w_gu[:, 0:256].rearrange("(c p) n -> p c n", p=128), [], [r_Wgu0])
                dma("pool", Wu0, w_gu[:, DFF:DFF + 256].rearrange("(c p) n -> p c n", p=128), [], [r_Wgu0])
                dma("pool", Wd0, w_dn[0:256, :].rearrange("(c p) n -> p c n", p=128), [], [r_Wd0])
            for h in range(4):
                for hlf in range(2):
                    bk = next_mbank()
                    for c in range(16):
                        mm(bank(bk), Wq[:, c, h * 128:(h + 1) * 128], hT[:, c, hlf * 512:(hlf + 1) * 512], c == 0, c == 15,
                           [r_Wq] + all_hT, [rbank[bk]])
                    cp(qcT[:, h, hlf * 512:(hlf + 1) * 512], bank(bk), [rbank[bk]], [r_qcT[h][hlf]])
            ci = 0
            CSC = 128.0 ** -0.5
            for h in range(4):
                for hlf in range(2):
                    pts = []
                    for mbk in range(2):
                        bk = next_mbank(4)
                        mm(bank(bk), KcT[:, h, mbk * 128:(mbk + 1) * 128], qcT[:, h, hlf * 512:(hlf + 1) * 512], True, True,
                           [r_KcT[h], r_qcT[h][hlf]], [rbank[bk]])
                        pk = ci % 4
                        ci += 1
                        P.op("act", lambda e, pk=pk, bk=bk: e.activation(PTc[pk], bank(bk), AF.Exp, scale=CSC),
                             reads=[rbank[bk]], writes=[r_PTc[pk]])
                        pts.append(pk)
                    ab0 = 4 + 2 * ((h * 2 + hlf) % 2)
                    def cacc(qb, ab0=ab0):
                        return bank(ab0 + qb // 2)[:, (qb % 2) * 160:(qb % 2) * 160 + 129]
                    for qb in range(4):
                        for mbk in range(2):
                            mm(cacc(qb), PTc[pts[mbk]][:, qb * 128:(qb + 1) * 128],
                               Vc[:, mbk, h, :], mbk == 0, mbk == 1, [r_PTc[pts[mbk]], r_Vc[mbk], r_Vcones], [rbank[ab0 + qb // 2]])
                    rk = (h * 2 + hlf) % 2
                    for qb in range(4):
                        a_ap = cacc(qb)
                        blk = hlf * 4 + qb
                        P.op("dve", lambda e, rk=rk, qb=qb, a_ap=a_ap: e.reciprocal(rc[rk][:, qb:qb + 1], a_ap[:, 128:129]),
                             reads=[rbank[ab0 + qb // 2]], writes=[r_rc[rk]])
                        P.op("dve", lambda e, rk=rk, qb=qb, a_ap=a_ap, blk=blk, h=h: e.tensor_scalar_mul(
                            oc[:, blk, h * 128:(h + 1) * 128], a_ap[:, 0:128], rc[rk][:, qb:qb + 1]),
                            reads=[rbank[ab0 + qb // 2], r_rc[rk]], writes=[r_oc[blk][h]])
            for b in range(NB):
                bk = next_tbank()
                pb = bank(bk).bitcast(BF16)
                for c in range(4):
                    P.op("pe", lambda e, pb=pb, c=c, b=b: e.transpose(pb[:, c * 128:(c + 1) * 128], oc[:, b, c * 128:(c + 1) * 128], idb),
                         reads=r_oc[b] + [r_idb], writes=[rbank[bk]])
                cp(ocT[:, :, b * 128:(b + 1) * 128], pb[:, 0:512].rearrange("p (c n) -> p c n", c=4), [rbank[bk]], [r_ocT[b]])
            for b in range(NB):
                for pn in range(4):
                    bk = next_mbank()
                    for c in range(4):
                        mm(bank(bk), ocT[:, c, b * 128:(b + 1) * 128], Wco[:, c, pn * 512:(pn + 1) * 512], c == 0, c == 3,
                           [r_Wco, r_ocT[b]], [rbank[bk]])
                    ysl = y[:, b, pn * 512:(pn + 1) * 512]
                    P.op("dve", lambda e, ysl=ysl, bk=bk: e.scalar_tensor_tensor(ysl, ysl, ALPHA, bank(bk), ALU.mult, ALU.add),
                         reads=[rbank[bk], r_y[b][pn]], writes=[r_y[b][pn]])
            layer_norm_and_T(y, r_y, lnw, r_lnw, hT, r_hT, stt)
            P.barrier(scratch)
            A.reset(h_mark)

        if stage >= 7:
            dma("sp", lnw, lnp[:, 4:6, :], [], [r_lnw])
            Wg_ = [Wg0, A.alloc([128, 16, 256], BF16)]
            Wu_ = [Wu0, A.alloc([128, 16, 256], BF16)]
            Wd = [Wd0, A.alloc([128, 2, D], BF16)]
            r_Wgu, r_Wd = [r_Wgu0, P.R()], [r_Wd0, P.R()]
            Aff = [A.alloc([128, 2, TOK], BF16) for _ in range(2)]
            r_Aff = [[P.Rs(2) for _ in range(2)] for _ in range(2)]
            sgf = [A.alloc([128, 512], F32) for _ in range(2)]
            r_sgf = P.Rs(2)
            fi = 0
            NSC = DFF // 256
            for s in range(NSC):
                k = s % 2
                if s > 0:
                    dma("pool", Wg_[k], w_gu[:, s * 256:(s + 1) * 256].rearrange("(c p) n -> p c n", p=128), [], [r_Wgu[k]])
                    dma("pool", Wu_[k], w_gu[:, DFF + s * 256:DFF + (s + 1) * 256].rearrange("(c p) n -> p c n", p=128), [], [r_Wgu[k]])
                    dma("pool", Wd[k], w_dn[s * 256:(s + 1) * 256, :].rearrange("(c p) n -> p c n", p=128), [], [r_Wd[k]])
                for sub in range(2):
                    for hlf in range(2):
                        bg = next_mbank()
                        for c in range(16):
                            mm(bank(bg), Wg_[k][:, c, sub * 128:(sub + 1) * 128], hT[:, c, hlf * 512:(hlf + 1) * 512], c == 0, c == 15,
                               [r_Wgu[k]] + all_hT, [rbank[bg]])
                        bu = next_mbank()
                        for c in range(16):
                            mm(bank(bu), Wu_[k][:, c, sub * 128:(sub + 1) * 128], hT[:, c, hlf * 512:(hlf + 1) * 512], c == 0, c == 15,
                               [r_Wgu[k]] + all_hT, [rbank[bu]])
                        fk = fi % 2
                        fi += 1
                        P.op("act", lambda e, fk=fk, bg=bg: e.activation(sgf[fk], bank(bg), AF.Silu), reads=[rbank[bg]], writes=[r_sgf[fk]])
                        P.op("dve", lambda e, fk=fk, bu=bu, k=k, sub=sub, hlf=hlf: e.tensor_tensor(
                            Aff[k][:, sub, hlf * 512:(hlf + 1) * 512], bank(bu), sgf[fk], ALU.mult),
                            reads=[rbank[bu], r_sgf[fk]], writes=[r_Aff[k][sub][hlf]])
                for b in range(NB):
                    for pn in range(4):
                        bk = next_mbank()
                        for sub in range(2):
                            mm(bank(bk), Aff[k][:, sub, b * 128:(b + 1) * 128], Wd[k][:, sub, pn * 512:(pn + 1) * 512], sub == 0, sub == 1,
                               [r_Wd[k], r_Aff[k][sub][b // 4]], [rbank[bk]])
                        ysl = y[:, b, pn * 512:(pn + 1) * 512]
                        if s == 0:
                            P.op("dve", lambda e, ysl=ysl, bk=bk: e.scalar_tensor_tensor(ysl, ysl, ALPHA, bank(bk), ALU.mult, ALU.add),
                                 reads=[rbank[bk], r_y[b][pn]], writes=[r_y[b][pn]])
                        else:
                            P.op("dve", lambda e, ysl=ysl, bk=bk: e.tensor_tensor(ysl, ysl, bank(bk), ALU.add),
                                 reads=[rbank[bk], r_y[b][pn]], writes=[r_y[b][pn]])
            r_out = P.R()
            layer_norm_and_T(y, r_y, lnw, r_lnw, None, None, stt, final_out=out, r_out=r_out)
            P.final_wait(r_out)

        if dbg is not None:
            P.barrier(scratch)
            r_dbg = P.R()
            if stage == 3:
                stg = A.alloc([128, 8, TOK], F32)
                P.op("dve", lambda e: e.tensor_copy(stg, oaT), reads=r_oaT, writes=[r_dbg])
                dma("sp", dbg_out.rearrange("(c p) t -> p c t", p=128), stg, [r_dbg], [r_dbg])
            elif stage == 4:
                stg = A.alloc([128, 8, TOK], F32)
                P.op("dve", lambda e: e.tensor_copy(stg, obT), reads=[r for hh in range(8) for r in r_obT[hh]], writes=[r_dbg])
                dma("sp", dbg_out.rearrange("(c p) t -> p c t", p=128), stg, [r_dbg], [r_dbg])
            elif stage in (5, 6):
                for b in range(NB):
                    dma("sp", dbg_out[b * 128:(b + 1) * 128, :], y[:, b, :], r_y[b], [r_dbg])
            P.final_wait(r_dbg)
        P.emit()
    return nc


def _rel_bucket(dist):
    n = np.maximum(dist, 0)
    exact = 16
    logv = np.log(np.maximum(n, 1).astype(np.float32) / np.float32(exact)) / np.float32(math.log(128 / 16))
    large = exact + (logv * np.float32(32 - exact)).astype(np.int32)
    large = np.minimum(large, 31)
    return np.where(n < exact, n, large)


def _bias_tables(table, core):
    k = np.arange(128)[:, None]
    q = np.arange(128)[None, :]
    sw = np.full((2, 128, 16, 2, 128), NEG, np.float32)
    for kap in range(2):
        dist = (128 + q - k) if kap == 0 else (q - k)
        ok = (dist >= 0) & (dist < 128)
        bk = _rel_bucket(dist)
        for h in range(16):
            vals = np.where(ok, table[bk, h], np.float32(NEG)).astype(np.float32)
            sw[0, :, h, kap, :] = vals
            if kap == 1:
                sw[1, :, h, kap, :] = vals
    if core != 0:
        sw[1] = sw[0]
    perm = [8 * g + 2 * i4 + r for g in range(2) for r in range(2) for i4 in range(4)]
    sw = sw[:, :, perm]
    q5 = np.arange(512)[None, :]
    near = np.full((2, 8, 128, 5, 512), NEG, np.float32)
    for kap in range(5):
        dist = q5 - 128 * (kap - 1) - k
        ok = dist >= 0
        bk = _rel_bucket(dist)
        for h in range(8):
            vals = np.where(ok, table[bk, 16 + h], np.float32(NEG)).astype(np.float32)
            near[:, h, :, kap, :] = vals[None]
    if core == 0:
        near[0, :, :, 0, :] = NEG
    far = np.full((128, 2, 64, 8), NEG, np.float32)
    for ch in range(2):
        g0 = 4 * (core if ch == 0 else 15 - core)
        for kb in range(64):
            if kb <= g0 - 2:
                far[:, ch, kb, :] = table[31, 16:24][None, :]
    return sw.reshape(2, 128, -1), near.reshape(2, 8, 128, -1), far.reshape(128, -1)


_NC_CACHE = {}


def kernel(x, mem, rel_bias_table, w_in, sinks, lambda_q1, lambda_k1, lambda_q2, lambda_k2,
           subln_w, w_branch_a, w_branch_b, w_o, ln1_g, ln1_b, w_cq, w_mem_kv, w_co,
           ln2_g, ln2_b, w_gate_up, w_down, ln3_g, ln3_b, _stage=99, _dbg=None):
    f = lambda a: np.ascontiguousarray(np.asarray(a, dtype=np.float32))
    x2 = f(x).reshape(S, D)
    table = f(rel_bias_table)
    lnp = np.stack([f(ln1_g)[0], f(ln1_b)[0], f(ln2_g)[0], f(ln2_b)[0], f(ln3_g)[0], f(ln3_b)[0]], 0)
    lnp = np.ascontiguousarray(np.broadcast_to(lnp[None], (128, 6, D)))
    perm = [8 * g + 2 * i4 + r for g in range(2) for r in range(2) for i4 in range(4)]
    small = np.concatenate([f(sinks)[0][perm], f(lambda_q1)[0], f(lambda_k1)[0], f(lambda_q2)[0], f(lambda_k2)[0], f(subln_w)[0]])
    small = np.ascontiguousarray(np.broadcast_to(small[None], (128, small.shape[0])))
    shared = {
        "x_all": x2, "mem": f(mem)[0], "w_in": f(w_in)[0], "w_branch_a": f(w_branch_a)[0], "w_branch_b": f(w_branch_b)[0],
        "w_o": f(w_o)[0], "w_cq": f(w_cq)[0], "w_mem_kv": f(w_mem_kv)[0], "w_co": f(w_co)[0],
        "w_gate_up": f(w_gate_up)[0], "w_down": f(w_down)[0], "lnp": lnp, "smallp": small,
        "ident": np.eye(128, dtype=np.float32),
    }
    in_maps = []
    for c in range(NCORES):
        xo = np.zeros((NOWN, D), np.float32)
        for ch, gc in enumerate((c, 15 - c)):
            r0 = ch * 640
            if gc > 0:
                xo[r0:r0 + 128] = x2[gc * 512 - 128:gc * 512]
            xo[r0 + 128:r0 + 640] = x2[gc * 512:(gc + 1) * 512]
        sw, near, far = _bias_tables(table, c)
        m = dict(shared)
        m.update({"x_own": xo, "bias_swa": sw, "bias_near": near, "bias_far": far})
        in_maps.append(m)
    key = (_stage, _dbg)
    if key not in _NC_CACHE:
        _NC_CACHE[key] = build(_stage, _dbg)
    nc = _NC_CACHE[key]
    res = run_bass_kernel_spmd(nc, in_maps, core_ids=list(range(NCORES)))
    if _dbg is not None:
        return np.concatenate([r["dbg"] for r in res.results], axis=0)
    o = np.empty((S, D), np.float32)
    for c in range(NCORES):
        oc = res.results[c]["out"]
        o[c * 512:(c + 1) * 512] = oc[0:512]
        o[(15 - c) * 512:(16 - c) * 512] = oc[512:1024]
    return o.reshape(1, S, D)
```
